# Optimizing a Trainium2 kernel written in Bass

```python
import functools
import math
import jax, jax.numpy as jnp
from jax import lax
import numpy as np

D_MODEL = 2048
BATCH = 32
SEQ = 256
DEPTH = 4
DEC_BATCH = 4
DEC_SEQ = 4096
PAST_LEN = 256

GRID_W = 64
BLOCK = 128
EPS = 1e-6
NEG_INF = -1e30
N_Q_HEADS = 16
N_KV_HEADS = 4
HEAD_DIM = 64
Q_PER_KV = N_Q_HEADS // N_KV_HEADS
WINDOW = 128
ROPE_BASE = 10000.0
ROPE_FREQS = HEAD_DIM // 4
ATTN_SCALE = HEAD_DIM ** -0.5
ATTN_WIDTH = N_Q_HEADS * HEAD_DIM
KV_WIDTH = N_KV_HEADS * HEAD_DIM
CONV_WIDTH = 1024
CONV_KERNEL = 31
SSD_HEADS = 16
SSD_HEAD_DIM = 64
SSD_INNER = SSD_HEADS * SSD_HEAD_DIM
SSD_GROUPS = 2
SSD_HEADS_PER_GROUP = SSD_HEADS // SSD_GROUPS
D_STATE = 128
SSD_CONV = 4
SSD_CHUNK = 128
XBC_WIDTH = SSD_INNER + 2 * SSD_GROUPS * D_STATE
PEER_HEADS = 8
N_KEYS = 128
N_EXPERTS = N_KEYS * N_KEYS
PEER_KEY_DIM = 256
PEER_HALF = PEER_KEY_DIM // 2
PEER_TOPK = 16
PEER_SLOTS = PEER_HEADS * PEER_TOPK
PEER_TOKEN_BLOCK = 128
N_BRANCH = 3
IN_SPLITS = (ATTN_WIDTH, KV_WIDTH, KV_WIDTH, 2 * CONV_WIDTH, SSD_INNER, XBC_WIDTH, 2 * SSD_HEADS)
IN_WIDTH = ATTN_WIDTH + 2 * KV_WIDTH + 2 * CONV_WIDTH + SSD_INNER + XBC_WIDTH + 2 * SSD_HEADS + N_BRANCH * D_MODEL

kernel_name = 'hybrid_diffusion_prefix_trunk_step'


def rms_norm(x, g):
    xf = x.astype(jnp.float32)
    y = xf * lax.rsqrt(jnp.mean(xf * xf, axis=-1, keepdims=True) + EPS)
    return (y * g.astype(jnp.float32)).astype(x.dtype)


def layer_norm(x, g, b):
    xf = x.astype(jnp.float32)
    mu = jnp.mean(xf, axis=-1, keepdims=True)
    var = jnp.mean(jnp.square(xf - mu), axis=-1, keepdims=True)
    y = (xf - mu) * lax.rsqrt(var + EPS) * g.astype(jnp.float32) + b.astype(jnp.float32)
    return y.astype(x.dtype)


def modulation(cond, w, b):
    m = jax.nn.silu(cond) @ w + b
    return m.reshape(cond.shape[:-1] + (6, D_MODEL))


def split_in(u):
    offsets = []
    acc = 0
    for w in IN_SPLITS:
        acc += w
        offsets.append(acc)
    return jnp.split(u, offsets, axis=-1)


def grid_rope_angles(length):
    rows = length // GRID_W
    row = jnp.repeat(jnp.arange(rows, dtype=jnp.float32), GRID_W)
    col = jnp.tile(jnp.arange(GRID_W, dtype=jnp.float32), rows)
    freqs = ROPE_BASE ** (-jnp.arange(ROPE_FREQS, dtype=jnp.float32) / ROPE_FREQS)
    return row[:, None] * freqs, col[:, None] * freqs


def rotate_pairs(x, ang):
    cos = jnp.cos(ang)[None, :, None, :].astype(x.dtype)
    sin = jnp.sin(ang)[None, :, None, :].astype(x.dtype)
    x1, x2 = x[..., :ROPE_FREQS], x[..., ROPE_FREQS:]
    return jnp.concatenate([x1 * cos - x2 * sin, x2 * cos + x1 * sin], axis=-1)


def axial_rope(x, ang_row, ang_col):
    half = HEAD_DIM // 2
    return jnp.concatenate([rotate_pairs(x[..., :half], ang_row),
                            rotate_pairs(x[..., half:], ang_col)], axis=-1)


def sink_softmax(s, sink):
    sk = jnp.broadcast_to(sink.astype(jnp.float32).reshape(N_KV_HEADS, Q_PER_KV, 1, 1), s.shape[:-1] + (1,))
    return jax.nn.softmax(jnp.concatenate([s, sk], axis=-1), axis=-1)[..., :-1]


def ctx_attention(q, k, v, sink):
    bsz, length = q.shape[:2]
    nb = length // BLOCK
    qb = jnp.moveaxis(q.reshape(bsz, nb, BLOCK, N_KV_HEADS, Q_PER_KV, HEAD_DIM), 1, 0)

    def one_block(qblk):
        s = jnp.einsum('bqkgd,bskd->bkgqs', qblk, k).astype(jnp.float32) * ATTN_SCALE
        p = sink_softmax(s, sink).astype(v.dtype)
        return jnp.einsum('bkgqs,bskd->bqkgd', p, v)

    o = lax.map(one_block, qb)
    return jnp.moveaxis(o, 0, 1).reshape(bsz, length, ATTN_WIDTH)


def latent_attention(q, k, v, sink, ck, cv, ang_row, ang_col):
    bsz, length = q.shape[:2]
    nb = length // BLOCK
    q = axial_rope(q, ang_row, ang_col)
    k = axial_rope(k, ang_row, ang_col)
    qb = jnp.moveaxis(q.reshape(bsz, nb, BLOCK, N_KV_HEADS, Q_PER_KV, HEAD_DIM), 1, 0)

    def neighbours(t):
        tp = jnp.pad(t, ((0, 0), (BLOCK, BLOCK), (0, 0), (0, 0)))
        tp = tp.reshape(bsz, nb + 2, BLOCK, N_KV_HEADS, HEAD_DIM)
        return jnp.moveaxis(jnp.concatenate([tp[:, :-2], tp[:, 1:-1], tp[:, 2:]], axis=2), 1, 0)

    kw, vw = neighbours(k), neighbours(v)
    n_win = 3 * BLOCK

    def one_block(args):
        qblk, kblk, vblk, n = args
        qi = n * BLOCK + jnp.arange(BLOCK)
        kj = n * BLOCK - BLOCK + jnp.arange(n_win)
        mask = (jnp.abs(qi[:, None] - kj[None, :]) <= WINDOW) & (kj >= 0)[None, :] & (kj < length)[None, :]
        s_win = jnp.einsum('bqkgd,bskd->bkgqs', qblk, kblk).astype(jnp.float32) * ATTN_SCALE
        s_win = jnp.where(mask, s_win, NEG_INF)
        s_ctx = jnp.einsum('bqkgd,bckd->bkgqc', qblk, ck).astype(jnp.float32) * ATTN_SCALE
        p = sink_softmax(jnp.concatenate([s_win, s_ctx], axis=-1), sink).astype(vblk.dtype)
        return (jnp.einsum('bkgqs,bskd->bqkgd', p[..., :n_win], vblk)
                + jnp.einsum('bkgqc,bckd->bqkgd', p[..., n_win:], cv))

    o = lax.map(one_block, (qb, kw, vw, jnp.arange(nb)))
    return jnp.moveaxis(o, 0, 1).reshape(bsz, length, ATTN_WIDTH)


def conformer_conv(u, dw_w, dw_b, ln_g, ln_b):
    a, g = jnp.split(u, 2, axis=-1)
    h = a * jax.nn.sigmoid(g)
    pad = CONV_KERNEL // 2
    h = lax.conv_general_dilated(h, dw_w[:, None, :], (1,), [(pad, pad)],
                                 dimension_numbers=('NWC', 'WIO', 'NWC'),
                                 feature_group_count=CONV_WIDTH) + dw_b
    return jax.nn.silu(layer_norm(h, ln_g, ln_b))


def causal_dwconv(u, w, b):
    return lax.conv_general_dilated(u, w[:, None, :], (1,), [(SSD_CONV - 1, 0)],
                                    dimension_numbers=('NWC', 'WIO', 'NWC'),
                                    feature_group_count=XBC_WIDTH) + b


def ssd_chunked(x, dt, A, Bm, Cm, h0):
    bsz, length = x.shape[:2]
    nc = length // SSD_CHUNK
    shp = (bsz, nc, SSD_CHUNK, SSD_GROUPS, SSD_HEADS_PER_GROUP)
    xg = (x * dt[..., None]).reshape(shp + (SSD_HEAD_DIM,))
    a_cum = jnp.cumsum((dt * A).reshape(shp), axis=2)
    Bc = Bm.reshape(bsz, nc, SSD_CHUNK, SSD_GROUPS, D_STATE)
    Cc = Cm.reshape(bsz, nc, SSD_CHUNK, SSD_GROUPS, D_STATE)
    lower = jnp.tril(jnp.ones((SSD_CHUNK, SSD_CHUNK), dtype=bool))[:, :, None, None]
    seg = a_cum[:, :, :, None] - a_cum[:, :, None, :]
    decay = jnp.exp(jnp.where(lower, seg, -jnp.inf))
    cb = jnp.einsum('bctgn,bcsgn->bctsg', Cc, Bc)
    y_diag = jnp.einsum('bctsgh,bcsghp->bctghp', cb[..., None] * decay, xg)
    to_end = jnp.exp(a_cum[:, :, -1:] - a_cum)
    chunk_states = jnp.einsum('bclgn,bclghp->bcghpn', Bc, xg * to_end[..., None])
    chunk_decay = jnp.exp(a_cum[:, :, -1])

    def step(h, inp):
        st, dc = inp
        return dc[..., None, None] * h + st, h

    h_init = h0.reshape(bsz, SSD_GROUPS, SSD_HEADS_PER_GROUP, SSD_HEAD_DIM, D_STATE)
    h_final, h_in = lax.scan(step, h_init, (jnp.moveaxis(chunk_states, 1, 0), jnp.moveaxis(chunk_decay, 1, 0)))
    h_in = jnp.moveaxis(h_in, 0, 1)
    y_off = jnp.einsum('bclgn,bcghpn->bclghp', Cc, h_in) * jnp.exp(a_cum)[..., None]
    y = (y_diag + y_off).reshape(bsz, length, SSD_HEADS, SSD_HEAD_DIM)
    return y, h_final.reshape(bsz, SSD_HEADS, SSD_HEAD_DIM, D_STATE)


def maybe_flip(t, direction):
    return jnp.flip(t, axis=1) if direction == 1 else t


def ssd_bidirectional(xbc, dt_raw, z, conv_w, conv_b, A_log, dt_bias, D_skip, norm_g, h0):
    bsz, length = xbc.shape[:2]
    ys = []
    finals = []
    for d in range(2):
        u = jax.nn.silu(causal_dwconv(maybe_flip(xbc, d), conv_w[d], conv_b[d])).astype(jnp.float32)
        xs, Bm, Cm = jnp.split(u, [SSD_INNER, SSD_INNER + SSD_GROUPS * D_STATE], axis=-1)
        xs = xs.reshape(bsz, length, SSD_HEADS, SSD_HEAD_DIM)
        Bm = Bm.reshape(bsz, length, SSD_GROUPS, D_STATE)
        Cm = Cm.reshape(bsz, length, SSD_GROUPS, D_STATE)
        dt = jax.nn.softplus(maybe_flip(dt_raw[..., d * SSD_HEADS:(d + 1) * SSD_HEADS], d).astype(jnp.float32)
                             + dt_bias[d].astype(jnp.float32))
        A = -jnp.exp(A_log[d].astype(jnp.float32))
        y, hf = ssd_chunked(xs, dt, A, Bm, Cm, h0[:, d].astype(jnp.float32))
        y = y + D_skip[d].astype(jnp.float32)[:, None] * xs
        ys.append(maybe_flip(y, d))
        finals.append(hf)
    y = (ys[0] + ys[1]).reshape(bsz, length, SSD_INNER)
    y = rms_norm(y * jax.nn.silu(z.astype(jnp.float32)), norm_g).astype(z.dtype)
    return y, jnp.stack(finals, axis=1)


def peer(h, w_q, sub_keys, U, V):
    bsz, length = h.shape[:2]
    T = bsz * length
    hf = h.reshape(T, D_MODEL)
    q = (hf @ w_q).reshape(T, PEER_HEADS, 2, PEER_HALF)
    s = jnp.einsum('thcd,hcnd->thcn', q, sub_keys).astype(jnp.float32)
    s1, i1 = lax.top_k(s[:, :, 0], PEER_TOPK)
    s2, i2 = lax.top_k(s[:, :, 1], PEER_TOPK)
    cand = (s1[..., :, None] + s2[..., None, :]).reshape(T, PEER_HEADS, PEER_TOPK * PEER_TOPK)
    cidx = (i1[..., :, None] * N_KEYS + i2[..., None, :]).reshape(T, PEER_HEADS, PEER_TOPK * PEER_TOPK)
    top, pos = lax.top_k(cand, PEER_TOPK)
    idx = jnp.take_along_axis(cidx, pos, axis=-1)
    w = jax.nn.softmax(top, axis=-1)
    nblk = T // PEER_TOKEN_BLOCK

    def expert_block(args):
        xb, ib, wb = args
        a = jax.nn.gelu(jnp.einsum('td,ted->te', xb, jnp.take(U, ib, axis=0)).astype(jnp.float32), approximate=False)
        return jnp.einsum('te,ted->td', (wb * a).astype(xb.dtype), jnp.take(V, ib, axis=0))

    out = lax.map(expert_block, (hf.reshape(nblk, PEER_TOKEN_BLOCK, D_MODEL),
                                 idx.reshape(nblk, PEER_TOKEN_BLOCK, PEER_SLOTS),
                                 w.reshape(nblk, PEER_TOKEN_BLOCK, PEER_SLOTS)))
    return out.reshape(bsz, length, D_MODEL)


def trunk_layer(x, mod, lw, attend, ssd_h0):
    bsz, length = x.shape[:2]
    shift1, scale1, gate1, shift2, scale2, gate2 = [mod[..., i, :] for i in range(6)]
    h = rms_norm(x, lw['norm1_g']) * (1 + scale1) + shift1
    q, k, v, conv_in, z, xbc, dt_raw, gates = split_in(h @ lw['w_in'])
    q = q.reshape(bsz, length, N_Q_HEADS, HEAD_DIM)
    k = k.reshape(bsz, length, N_KV_HEADS, HEAD_DIM)
    v = v.reshape(bsz, length, N_KV_HEADS, HEAD_DIM)
    a_out = attend(q, k, v, lw['attn_sink']) @ lw['w_attn_o']
    c_out = conformer_conv(conv_in, lw['conv_dw_w'], lw['conv_dw_b'], lw['conv_ln_g'], lw['conv_ln_b']) @ lw['w_conv_o']
    s_y, s_final = ssd_bidirectional(xbc, dt_raw, z, lw['ssd_conv_w'], lw['ssd_conv_b'], lw['ssd_A_log'],
                                     lw['ssd_dt_bias'], lw['ssd_D'], lw['ssd_norm_g'], ssd_h0)
    s_out = s_y @ lw['w_ssd_o']
    g = jax.nn.sigmoid(gates.reshape(bsz, length, N_BRANCH, D_MODEL).astype(jnp.float32)
                       + lw['gate_b'].astype(jnp.float32)).astype(x.dtype)
    merged = g[..., 0, :] * a_out + g[..., 1, :] * c_out + g[..., 2, :] * s_out
    x = x + gate1 * (merged @ lw['w_out'])
    h2 = rms_norm(x, lw['norm2_g']) * (1 + scale2) + shift2
    x = x + gate2 * peer(h2, lw['peer_w_q'], lw['peer_sub_keys'], lw['peer_u'], lw['peer_v'])
    return x, k, v, s_final


def setup_inputs(seed: int = 0) -> dict:
    key = jax.random.key(seed)
    ks = iter(jax.random.split(key, 48))
    f32 = jnp.float32

    def nrm(shape, scale):
        return jax.random.normal(next(ks), shape, f32) * scale

    dt0 = jnp.exp(jax.random.uniform(next(ks), (DEPTH, 2, SSD_HEADS), f32, math.log(1e-3), math.log(1e-1)))
    return {
        'x_prompt': nrm((BATCH, SEQ, D_MODEL), 1.0),
        'x_sample': nrm((DEC_BATCH, DEC_SEQ, D_MODEL), 1.0),
        'cache_k': nrm((DEC_BATCH, DEPTH, PAST_LEN, N_KV_HEADS, HEAD_DIM), 1.0),
        'cache_v': nrm((DEC_BATCH, DEPTH, PAST_LEN, N_KV_HEADS, HEAD_DIM), 1.0),
        'state_ssd': nrm((DEC_BATCH, DEPTH, 2, SSD_HEADS, SSD_HEAD_DIM, D_STATE), 0.1),
        'c': nrm((DEC_BATCH, D_MODEL), 1.0),
        'c_ctx': nrm((D_MODEL,), 1.0),
        'w_ada': nrm((DEPTH, D_MODEL, 6 * D_MODEL), 0.5 * D_MODEL ** -0.5),
        'b_ada': nrm((DEPTH, 6 * D_MODEL), 0.02),
        'norm1_g': 1.0 + nrm((DEPTH, D_MODEL), 0.01),
        'norm2_g': 1.0 + nrm((DEPTH, D_MODEL), 0.01),
        'w_in': nrm((DEPTH, D_MODEL, IN_WIDTH), D_MODEL ** -0.5),
        'gate_b': nrm((DEPTH, N_BRANCH, D_MODEL), 0.01),
        'attn_sink': nrm((DEPTH, N_Q_HEADS), 0.5),
        'w_attn_o': nrm((DEPTH, ATTN_WIDTH, D_MODEL), ATTN_WIDTH ** -0.5),
        'conv_dw_w': nrm((DEPTH, CONV_KERNEL, CONV_WIDTH), CONV_KERNEL ** -0.5),
        'conv_dw_b': nrm((DEPTH, CONV_WIDTH), 0.01),
        'conv_ln_g': 1.0 + nrm((DEPTH, CONV_WIDTH), 0.01),
        'conv_ln_b': nrm((DEPTH, CONV_WIDTH), 0.01),
        'w_conv_o': nrm((DEPTH, CONV_WIDTH, D_MODEL), CONV_WIDTH ** -0.5),
        'ssd_conv_w': nrm((DEPTH, 2, SSD_CONV, XBC_WIDTH), SSD_CONV ** -0.5),
        'ssd_conv_b': nrm((DEPTH, 2, XBC_WIDTH), 0.01),
        'ssd_A_log': jnp.log(jax.random.uniform(next(ks), (DEPTH, 2, SSD_HEADS), f32, 1.0, 16.0)),
        'ssd_dt_bias': dt0 + jnp.log(-jnp.expm1(-dt0)),
        'ssd_D': 1.0 + nrm((DEPTH, 2, SSD_HEADS), 0.1),
        'ssd_norm_g': 1.0 + nrm((DEPTH, SSD_INNER), 0.01),
        'w_ssd_o': nrm((DEPTH, SSD_INNER, D_MODEL), SSD_INNER ** -0.5),
        'w_out': nrm((DEPTH, D_MODEL, D_MODEL), D_MODEL ** -0.5),
        'peer_w_q': nrm((DEPTH, D_MODEL, PEER_HEADS * PEER_KEY_DIM), D_MODEL ** -0.5),
        'peer_sub_keys': nrm((DEPTH, PEER_HEADS, 2, N_KEYS, PEER_HALF), PEER_HALF ** -0.5),
        'peer_u': nrm((DEPTH, N_EXPERTS, D_MODEL), D_MODEL ** -0.5),
        'peer_v': nrm((DEPTH, N_EXPERTS, D_MODEL), PEER_SLOTS ** -0.5),
        'final_g': 1.0 + nrm((D_MODEL,), 0.01),
    }


def reference(x_prompt, x_sample, cache_k, cache_v, state_ssd, c, c_ctx, w_ada, b_ada, norm1_g, norm2_g,
              w_in, gate_b, attn_sink, w_attn_o, conv_dw_w, conv_dw_b, conv_ln_g, conv_ln_b, w_conv_o,
              ssd_conv_w, ssd_conv_b, ssd_A_log, ssd_dt_bias, ssd_D, ssd_norm_g, w_ssd_o, w_out,
              peer_w_q, peer_sub_keys, peer_u, peer_v, final_g):
    ang_row, ang_col = grid_rope_angles(x_sample.shape[1])
    h0_ctx = jnp.zeros((x_prompt.shape[0], 2, SSD_HEADS, SSD_HEAD_DIM, D_STATE), jnp.float32)
    xp = x_prompt
    xs = x_sample
    new_k = []
    new_v = []
    new_s = []
    for l in range(DEPTH):
        lw = dict(norm1_g=norm1_g[l], norm2_g=norm2_g[l], w_in=w_in[l], gate_b=gate_b[l],
                  attn_sink=attn_sink[l], w_attn_o=w_attn_o[l], conv_dw_w=conv_dw_w[l],
                  conv_dw_b=conv_dw_b[l], conv_ln_g=conv_ln_g[l], conv_ln_b=conv_ln_b[l],
                  w_conv_o=w_conv_o[l], ssd_conv_w=ssd_conv_w[l], ssd_conv_b=ssd_conv_b[l],
                  ssd_A_log=ssd_A_log[l], ssd_dt_bias=ssd_dt_bias[l], ssd_D=ssd_D[l],
                  ssd_norm_g=ssd_norm_g[l], w_ssd_o=w_ssd_o[l], w_out=w_out[l],
                  peer_w_q=peer_w_q[l], peer_sub_keys=peer_sub_keys[l], peer_u=peer_u[l], peer_v=peer_v[l])
        mod_ctx = modulation(c_ctx, w_ada[l], b_ada[l])[None, None]
        xp, k_l, v_l, s_l = trunk_layer(xp, mod_ctx, lw, ctx_attention, h0_ctx)
        new_k.append(k_l)
        new_v.append(v_l)
        new_s.append(s_l)
        mod_lat = modulation(c, w_ada[l], b_ada[l])[:, None]
        attend = functools.partial(latent_attention, ck=cache_k[:, l], cv=cache_v[:, l],
                                   ang_row=ang_row, ang_col=ang_col)
        xs, _, _, _ = trunk_layer(xs, mod_lat, lw, attend, state_ssd[:, l])
    y_prompt = rms_norm(xp, final_g)
    y_sample = rms_norm(xs, final_g)
    new_cache_k = jnp.stack(new_k, axis=1)
    new_cache_v = jnp.stack(new_v, axis=1)
    new_state_ssd = jnp.stack(new_s, axis=1)
    return (y_prompt, y_sample, new_cache_k, new_cache_v, new_state_ssd)
```

```python
import numpy as np
import concourse.bass as bass
import concourse.mybir as mybir
from concourse.bass_utils import run_bass_kernel_spmd
from contextlib import ExitStack

dt = mybir.dt
F32, BF16, F32R, I32, U32 = dt.float32, dt.bfloat16, dt.float32r, dt.int32, dt.uint32
AF = mybir.ActivationFunctionType
ALU = mybir.AluOpType
AX = mybir.AxisListType

SEM_LIMIT = 56000


class Buf:
    __slots__ = ("name", "writer", "readers", "dsem")

    def __init__(self, name):
        self.name = name
        self.writer = None
        self.readers = {}
        self.dsem = None


class SemObj:
    __slots__ = ("h", "count", "name")

    def __init__(self, h, name):
        self.h = h
        self.count = 0
        self.name = name


class Sched:
    ENGS = ("pe", "act", "dve", "pool", "sp")

    def __init__(self, nc, stack, n_dma_sems=18, n_eng_sems=0):
        self.nc = nc
        self.stack = stack
        self.eng = {"pe": nc.tensor, "act": nc.scalar, "dve": nc.vector, "pool": nc.gpsimd, "sp": nc.sync}
        self.free_sems = []
        self.nsem = 0
        self.esem = {e: self._new_sem() for e in self.ENGS}
        self.seen = {e: {} for e in self.ENGS}
        self.free_dsems = [self._new_sem() for _ in range(n_dma_sems)]
        self.spare = [self._new_sem() for _ in range(n_eng_sems * len(self.ENGS))]
        self.active_dsems = []
        self.bufs = []
        self.ninstr = 0
        self.nbar = 0

    def _new_sem(self):
        self.nsem += 1
        h = self.stack.enter_context(self.nc.semaphore(f"sem{self.nsem}"))
        return SemObj(h, f"sem{self.nsem}")

    def buf(self, name="b"):
        b = Buf(name)
        self.bufs.append(b)
        return b

    def bufs_n(self, n, name="b"):
        return [self.buf(f"{name}{i}") for i in range(n)]

    def _deps(self, reads, writes):
        deps = {}
        def add(ev):
            if ev is None:
                return
            s, v = ev
            if deps.get(s, 0) < v:
                deps[s] = v
        for b in reads:
            add(b.writer)
        for b in writes:
            add(b.writer)
            for s, v in b.readers.items():
                add((s, v))
        return deps

    def _wait(self, e, deps, skip_self=False):
        seen = self.seen[e]
        for s, v in deps.items():
            if skip_self and s is self.esem[e]:
                continue
            if seen.get(s, 0) < v:
                self.eng[e].wait_ge(s.h, v)
                seen[s] = v

    def _record(self, ev, reads, writes):
        s, v = ev
        for b in writes:
            b.writer = ev
            b.readers = {}
        for b in reads:
            if b.readers.get(s, 0) < v:
                b.readers[s] = v

    def op(self, e, fn, reads=(), writes=(), skip_self=None):
        if skip_self is None:
            skip_self = (e == "pe")
        deps = self._deps(reads, writes)
        self._wait(e, deps, skip_self=skip_self)
        ins = fn()
        s = self.esem[e]
        s.count += 1
        assert s.count < 64000, (e, s.count)
        ins.then_inc(s.h, 1)
        self._record((s, s.count), reads, writes)
        self.ninstr += 1
        return ins

    def dma(self, q, out, in_, reads=(), writes=(), sembuf=None, **kw):
        deps = self._deps(reads, writes)
        self._wait(q, deps)
        if sembuf is None:
            sembuf = (list(writes) + list(reads))[0]
        if sembuf.dsem is None:
            sembuf.dsem = self.free_dsems.pop()
            self.active_dsems.append(sembuf)
        s = sembuf.dsem
        ins = self.eng[q].dma_start(out=out, in_=in_, **kw)
        s.count += 16
        assert s.count < 64000, s.count
        ins.then_inc(s.h, 16)
        self._record((s, s.count), reads, writes)
        self.ninstr += 1
        return ins

    def gather(self, out, in_, idx_ap, reads=(), writes=(), sembuf=None, **kw):
        deps = self._deps(reads, writes)
        self._wait("pool", deps)
        if sembuf is None:
            sembuf = list(writes)[0]
        if sembuf.dsem is None:
            sembuf.dsem = self.free_dsems.pop()
            self.active_dsems.append(sembuf)
        s = sembuf.dsem
        ins = self.nc.gpsimd.indirect_dma_start(
            out=out, out_offset=None, in_=in_,
            in_offset=bass.IndirectOffsetOnAxis(ap=idx_ap, axis=0), **kw)
        s.count += 16
        assert s.count < 64000, s.count
        ins.then_inc(s.h, 16)
        self._record((s, s.count), reads, writes)
        self.ninstr += 1
        return ins

    def barrier(self):
        evs = {}
        for e in self.ENGS:
            s = self.esem[e]
            if s.count > 0:
                evs[s] = s.count
        for b in self.active_dsems:
            evs[b.dsem] = b.dsem.count
        for e in self.ENGS:
            for s, v in evs.items():
                if s is self.esem[e]:
                    continue
                if self.seen[e].get(s, 0) < v:
                    self.eng[e].wait_ge(s.h, v)
                    self.seen[e][s] = v
        for b in self.bufs:
            b.writer = None
            b.readers = {}
        for b in self.active_dsems:
            self.free_dsems.append(b.dsem)
            b.dsem = None
        self.active_dsems = []
        for e in self.ENGS:
            if self.esem[e].count > SEM_LIMIT:
                self.esem[e] = self._fresh()
        for i, s_ in enumerate(self.free_dsems):
            if s_.count > SEM_LIMIT:
                self.free_dsems[i] = self._fresh()

    def _fresh(self):
        if self.spare:
            return self.spare.pop()
        return self._new_sem()

    def _rotate(self, e):
        self.esem[e] = self._fresh()


D = 2048
NKC = 16
EPS = 1e-6
NTOKW = 7712
NFEATW = 6144
HPERM = np.concatenate([np.arange(16, 32), np.arange(0, 16), np.arange(48, 64), np.arange(32, 48)])


class K:
    pass


_uid = [0]


def sbt(k, st, name, shape, dtype=F32):
    _uid[0] += 1
    return st.enter_context(k.nc.sbuf_tensor(f"{name}_{_uid[0]}", list(shape), dtype))


def build(T, L, debug=False, phases=("inproj", "ssd", "attn", "conf", "merge", "peer")):
    nc = bass.Bass("TRN2", target_bir_lowering=False)
    k = K()
    k.nc, k.T, k.L, k.debug = nc, T, L, debug
    k.NT, k.NG = T // 128, T // 512
    k.phases = phases
    kind_s = "ExternalOutput" if debug else "Internal"

    def din(name, shape, dtype=F32):
        return nc.dram_tensor(name, list(shape), dtype, kind="ExternalInput").ap()

    def dscr(name, shape, dtype=F32):
        return nc.dram_tensor(name, list(shape), dtype, kind=kind_s).ap()

    def dout(name, shape, dtype=F32):
        return nc.dram_tensor(name, list(shape), dtype, kind="ExternalOutput").ap()

    k.x0 = din("x0", [T, D])
    k.cond = din("cond", [128, NKC])
    k.w_ada = din("w_ada", [L, D, 6 * D])
    k.b_ada = din("b_ada", [L, 1, 6 * D])
    k.g1c = din("g1c", [L, 128, NKC])
    k.g2c = din("g2c", [L, 128, NKC])
    k.w_tok = din("w_tok", [L, D, NTOKW])
    k.w_feat = din("w_feat", [L, D, NFEATW])
    k.cos = din("cos", [128, T])
    k.sin = din("sin", [128, T])
    k.ident = din("ident", [128, 128])

    NT = T // 128
    k.f0 = din("f0", [128, 1]); k.ctxbias = din("ctxbias", [128, 1])
    k.tri = din("tri", [3, 128, 128]); k.masks = din("masks", [4, 128, 128]); k.iota16 = din("iota16", [128, 16])
    k.ssd_cw = din("ssd_cw", [L, 2, 128, 48]); k.ssd_cb = din("ssd_cb", [L, 2, 128, 12])
    k.ssd_A_log = din("ssd_A_log", [L, 2, 16]); k.ssd_dt_bias = din("ssd_dt_bias", [L, 2, 16]); k.ssd_D = din("ssd_D", [L, 2, 16])
    k.ssd_norm_g = din("ssd_norm_g", [L, 1024]); k.h0 = din("h0", [L, 2, 1024, 128])
    k.sink = din("sink", [L, 16]); k.ckT = din("ckT", [L, 64, 4, 256]); k.cv = din("cv", [L, 256, 256])
    k.conf_w = din("conf_w", [L, 128, 248]); k.conf_b = din("conf_b", [L, 128, 8])
    k.conf_ln_g = din("conf_ln_g", [L, 1024]); k.conf_ln_b = din("conf_ln_b", [L, 1024])
    k.gate_b = din("gate_b", [L, 6144])
    k.w_attn_o = din("w_attn_o", [L, 1024, D]); k.w_conv_o = din("w_conv_o", [L, 1024, D]); k.w_ssd_o = din("w_ssd_o", [L, 1024, D])
    k.w_out = din("w_out", [L, D, D]); k.w_q = din("w_q", [L, D, D])
    k.norm2_g = din("norm2_g", [L, D]); k.final_g = din("final_g", [D])
    k.skT = din("skT", [L, 16, 128, 128])
    k.peer_u = [din(f"peer_u{i}", [16384, D]) for i in range(L)]; k.peer_v = [din(f"peer_v{i}", [16384, D]) for i in range(L)]

    k.MOD = dscr("MOD", [L, 1, 6 * D])
    k.QT = dscr("QT", [1024, T], BF16)
    k.KT = dscr("KT", [256, T], BF16)
    k.GLUT = dscr("GLUT", [1024, T], BF16)
    k.XBCT = dscr("XBCT", [1536, T], BF16)
    k.GATES = dscr("GATES", [T, 6144])
    k.Z = dscr("Z", [T, 1024])
    k.DT = dscr("DT", [T, 32])
    k.YB = dscr("YB", [T, 1024]); k.SY = dscr("SY", [T, 1024]); k.AO = dscr("AO", [T, 1024]); k.CO = dscr("CO", [T, 1024])
    k.KVO = dout("KVO", [L, T, 512])
    k.SSDO = dout("SSDO", [L, 2, NT // 2, 1024, 128])
    k.XR = dscr("XR", [T, D])
    k.Y = dout("Y", [T, D])

    with ExitStack() as st:
        k.s = Sched(nc, st)
        k.ps = [st.enter_context(nc.psum_tensor(f"ps{i}", [128, 512], F32)) for i in range(8)]
        k.psb = None
        phase_mod(k)
        for l in range(L):
            for ph in k.phases:
                if ph == "inproj": phase_inproj(k, l)
                elif ph == "ssd":
                    phase_ssd(k, l, 1); phase_ssd(k, l, 0)
                elif ph == "attn": phase_attn(k, l)
                elif ph == "conf": phase_conf(k, l)
                elif ph == "merge": phase_merge(k, l)
                elif ph == "peer": phase_peer(k, l)
        k.s.barrier()
        print("instructions:", k.s.ninstr, "sems:", k.s.nsem)
    return nc


def new_ps_bufs(k):
    k.psb = k.s.bufs_n(8, "ps")


def phase_mod(k):
    nc, s = k.nc, k.s
    with ExitStack() as st:
        new_ps_bufs(k)
        cond = sbt(k, st, "cond", [128, NKC], F32)
        sc = sbt(k, st, "sc", [128, NKC], F32)
        wb = [sbt(k, st, f"wada{i}", [128, NKC, 512], F32) for i in range(2)]
        brow = sbt(k, st, "brow", [1, 6 * D], F32)
        orow = sbt(k, st, "orow", [1, 6 * D], F32)
        b_cond, b_sc, b_brow, b_orow = s.buf("cond"), s.buf("sc"), s.buf("brow"), s.buf("orow")
        b_w = s.bufs_n(2, "wada")
        s.dma("sp", cond[:], k.cond[:, :], writes=[b_cond])
        s.op("act", lambda: nc.scalar.activation(out=sc[:], in_=cond[:], func=AF.Silu), reads=[b_cond], writes=[b_sc])
        it = 0
        for l in range(k.L):
            s.dma("sp", brow[:], k.b_ada[l], writes=[b_brow])
            for n in range(24):
                wi = it % 2
                s.dma("sp", wb[wi][:], k.w_ada[l][:, n * 512:(n + 1) * 512].rearrange("(c p) n -> p c n", p=128),
                      writes=[b_w[wi]])
                pi = it % 8
                for c in range(NKC):
                    s.op("pe", lambda c=c: nc.tensor.matmul(k.ps[pi][0:1, :], lhsT=sc[:, c:c + 1], rhs=wb[wi][:, c, :],
                                                            start=(c == 0), stop=(c == NKC - 1)),
                         reads=[b_sc, b_w[wi]], writes=[k.psb[pi]])
                s.op("dve", lambda: nc.vector.tensor_tensor(out=orow[0:1, n * 512:(n + 1) * 512], in0=k.ps[pi][0:1, :],
                                                            in1=brow[0:1, n * 512:(n + 1) * 512], op=ALU.add),
                     reads=[k.psb[pi], b_brow], writes=[b_orow])
                it += 1
            s.dma("sp", k.MOD[l], orow[:], reads=[b_orow])
        s.barrier()


def load_mod_cols(k, st, l, which):
    nc, s = k.nc, k.s
    base = 0 if which == 0 else 3
    gsrc = k.g1c if which == 0 else k.g2c
    sh = sbt(k, st, f"modsh{which}", [128, NKC], F32)
    scl = sbt(k, st, f"modsc{which}", [128, NKC], F32)
    gg = sbt(k, st, f"modg{which}", [128, NKC], F32)
    A = sbt(k, st, f"modA{which}", [128, NKC], F32)
    b_sh, b_scl, b_g, b_A = s.buf(), s.buf(), s.buf(), s.buf()
    mrow = k.MOD[l]
    with nc.allow_non_contiguous_dma(reason="tiny modulation relayout"):
        s.dma("sp", sh[:], mrow[0, base * D:(base + 1) * D].rearrange("(c p) -> p c", p=128), writes=[b_sh])
        s.dma("sp", scl[:], mrow[0, (base + 1) * D:(base + 2) * D].rearrange("(c p) -> p c", p=128), writes=[b_scl])
    s.dma("sp", gg[:], gsrc[l], writes=[b_g])
    s.op("dve", lambda: nc.vector.scalar_tensor_tensor(out=A[:], in0=scl[:], scalar=1.0, in1=gg[:],
                                                       op0=ALU.add, op1=ALU.mult),
         reads=[b_scl, b_g], writes=[b_A])
    return A, b_A, sh, b_sh


def norm_mod_transpose(k, xt, b_xt, hT, b_hT, col0, A, b_A, sh, b_sh, tmp):
    nc, s = k.nc, k.s
    junk, b_junk, ssq, b_ssq, rstd, b_rstd, xs, b_xs, ident, b_ident = tmp
    s.op("act", lambda: nc.scalar.activation(out=junk[:], in_=xt[:], func=AF.Square, accum_out=ssq[:]),
         reads=[b_xt], writes=[b_junk, b_ssq])
    s.op("dve", lambda: nc.vector.tensor_scalar(rstd[:], ssq[:], 1.0 / D, EPS, ALU.mult, ALU.add),
         reads=[b_ssq], writes=[b_rstd])
    s.op("act", lambda: nc.scalar.activation(out=rstd[:], in_=rstd[:], func=AF.Sqrt), reads=[b_rstd], writes=[b_rstd])
    s.op("dve", lambda: nc.vector.reciprocal(rstd[:], rstd[:]), reads=[b_rstd], writes=[b_rstd])
    s.op("act", lambda: nc.scalar.activation(out=xs[:], in_=xt[:], func=AF.Copy, scale=rstd[:, 0:1]),
         reads=[b_xt, b_rstd], writes=[b_xs])
    for c4 in range(4):
        pi = k.psrot % 8
        k.psrot += 1
        for j in range(4):
            c = c4 * 4 + j
            s.op("pe", lambda c=c, j=j: nc.tensor.transpose(k.ps[pi][:, j * 128:(j + 1) * 128], xs[:, c * 128:(c + 1) * 128],
                                                            ident[:]),
                 reads=[b_xs, b_ident], writes=[k.psb[pi]])
        for j in range(4):
            c = c4 * 4 + j
            s.op("act", lambda c=c, j=j: nc.scalar.activation(out=hT[:, c, col0:col0 + 128],
                                                              in_=k.ps[pi][:, j * 128:(j + 1) * 128],
                                                              func=AF.Identity, scale=A[:, c:c + 1], bias=sh[:, c:c + 1]),
                 reads=[k.psb[pi], b_A, b_sh], writes=[b_hT])


def alloc_norm_tmp(k, st):
    nc, s = k.nc, k.s
    junk = sbt(k, st, "junk", [128, D], F32)
    ssq = sbt(k, st, "ssq", [128, 1], F32)
    rstd = sbt(k, st, "rstd", [128, 1], F32)
    xs = sbt(k, st, "xs", [128, D], F32)
    ident = sbt(k, st, "ident", [128, 128], F32)
    b_ident = s.buf("ident")
    s.dma("sp", ident[:], k.ident[:, :], writes=[b_ident])
    return (junk, s.buf(), ssq, s.buf(), rstd, s.buf(), xs, s.buf(), ident, b_ident)


def phase_inproj(k, l):
    nc, s, T = k.nc, k.s, k.T
    xsrc = k.x0 if l == 0 else k.XR
    with ExitStack() as st:
        new_ps_bufs(k)
        k.psrot = 0
        A, b_A, sh, b_sh = load_mod_cols(k, st, l, 0)
        tmp = alloc_norm_tmp(k, st)
        xt = [sbt(k, st, f"xt{i}", [128, D], F32) for i in range(2)]
        b_xt = s.bufs_n(2, "xt")
        hT = sbt(k, st, "hT", [128, NKC, 512], BF16)
        b_hT = s.buf("hT")
        wc = [sbt(k, st, f"wc{i}", [128, NKC, 512], BF16) for i in range(2)]
        b_wc = s.bufs_n(2, "wc")
        stg = [sbt(k, st, f"stg{i}", [128, 512], F32) for i in range(4)]
        b_stg = s.bufs_n(4, "stg")
        stgb = [sbt(k, st, f"stgb{i}", [128, 512], BF16) for i in range(4)]
        b_stgb = s.bufs_n(4, "stgb")
        tA = sbt(k, st, "tA", [128, 512], F32)
        b_tA = s.buf("tA")
        cs = sbt(k, st, "cs", [128, 2, 512], F32)
        b_cs = s.buf("cs")
        wit = 0
        sti = 0
        for g in range(k.NG):
            t0 = g * 512
            s.dma("sp", cs[:, 0, :], k.cos[:, t0:t0 + 512], writes=[b_cs])
            s.dma("sp", cs[:, 1, :], k.sin[:, t0:t0 + 512], writes=[b_cs])
            for ti in range(4):
                xi = (g * 4 + ti) % 2
                s.dma("sp", xt[xi][:], xsrc[t0 + ti * 128:t0 + (ti + 1) * 128, :], writes=[b_xt[xi]])
                norm_mod_transpose(k, xt[xi], b_xt[xi], hT, b_hT, ti * 128, A, b_A, sh, b_sh, tmp)
            ncols = [512] * 15 + [32]
            c0 = 0
            for n, w in enumerate(ncols):
                wi = wit % 2
                wit += 1
                s.dma("pool", wc[wi][:, :, 0:w], k.w_tok[l][:, c0:c0 + w].rearrange("(c p) n -> p c n", p=128),
                      writes=[b_wc[wi]])
                pbase = (k.psrot % 2) * 4
                k.psrot += 1
                for ti in range(4):
                    pi = pbase + ti
                    for c in range(NKC):
                        s.op("pe", lambda c=c, ti=ti, pi=pi: nc.tensor.matmul(
                            k.ps[pi][:, 0:w], lhsT=hT[:, c, ti * 128:(ti + 1) * 128],
                            rhs=wc[wi][:, c, 0:w], start=(c == 0), stop=(c == NKC - 1)),
                            reads=[b_hT, b_wc[wi]], writes=[k.psb[pi]])
                    si = sti % 4
                    sti += 1
                    eng = "act" if ti % 2 == 0 else "dve"
                    if eng == "act":
                        s.op("act", lambda si=si, pi=pi: nc.scalar.copy(out=stg[si][:, 0:w], in_=k.ps[pi][:, 0:w]),
                             reads=[k.psb[pi]], writes=[b_stg[si]])
                    else:
                        s.op("dve", lambda si=si, pi=pi: nc.vector.tensor_copy(out=stg[si][:, 0:w], in_=k.ps[pi][:, 0:w]),
                             reads=[k.psb[pi]], writes=[b_stg[si]])
                    r0 = t0 + ti * 128
                    if n < 12:
                        dst = k.GATES[r0:r0 + 128, n * 512:(n + 1) * 512]
                    elif n < 14:
                        dst = k.Z[r0:r0 + 128, (n - 12) * 512:(n - 11) * 512]
                    elif n == 14:
                        dst = k.KVO[l][r0:r0 + 128, :]
                    else:
                        dst = k.DT[r0:r0 + 128, :]
                    s.dma("pool", dst, stg[si][:, 0:w], reads=[b_stg[si]])
                c0 += w
            for n in range(12):
                wi = wit % 2
                wit += 1
                s.dma("pool", wc[wi][:], k.w_feat[l][:, n * 512:(n + 1) * 512].rearrange("(c p) n -> p c n", p=128),
                      writes=[b_wc[wi]])
                pbase = (k.psrot % 2) * 4
                k.psrot += 1
                for j in range(4):
                    pi = pbase + j
                    for c in range(NKC):
                        s.op("pe", lambda c=c, j=j, pi=pi: nc.tensor.matmul(
                            k.ps[pi][:, :], lhsT=wc[wi][:, c, j * 128:(j + 1) * 128],
                            rhs=hT[:, c, :], start=(c == 0), stop=(c == NKC - 1)),
                            reads=[b_hT, b_wc[wi]], writes=[k.psb[pi]])
                if n < 5:
                    for pr in range(2):
                        pa, pb_ = pbase + 2 * pr, pbase + 2 * pr + 1
                        si = sti % 4
                        sti += 1
                        s.op("dve", lambda: nc.vector.tensor_tensor(out=tA[:], in0=k.ps[pb_][:, :], in1=cs[:, 1, :], op=ALU.mult),
                             reads=[k.psb[pb_], b_cs], writes=[b_tA])
                        s.op("dve", lambda: nc.vector.tensor_tensor(out=stg[si][:], in0=k.ps[pa][:, :], in1=cs[:, 0, :], op=ALU.mult),
                             reads=[k.psb[pa], b_cs], writes=[b_stg[si]])
                        s.op("pool", lambda: nc.gpsimd.tensor_tensor(out=stgb[si][:], in0=stg[si][:], in1=tA[:], op=ALU.add),
                             reads=[b_tA, b_stg[si]], writes=[b_stgb[si]])
                        if n < 4:
                            ch = n * 2 + pr
                            dst = k.QT[ch * 128:(ch + 1) * 128, t0:t0 + 512]
                        else:
                            dst = k.KT[pr * 128:(pr + 1) * 128, t0:t0 + 512]
                        s.dma("pool", dst, stgb[si][:], reads=[b_stgb[si]])
                elif n < 9:
                    for pr in range(2):
                        pa, pb_ = pbase + 2 * pr, pbase + 2 * pr + 1
                        si = sti % 4
                        sti += 1
                        s.op("act", lambda: nc.scalar.activation(out=tA[:], in_=k.ps[pb_][:, :], func=AF.Sigmoid),
                             reads=[k.psb[pb_]], writes=[b_tA])
                        s.op("dve", lambda: nc.vector.tensor_tensor(out=stgb[si][:], in0=k.ps[pa][:, :], in1=tA[:], op=ALU.mult),
                             reads=[k.psb[pa], b_tA], writes=[b_stgb[si]])
                        ch = (n - 5) * 2 + pr
                        s.dma("pool", k.GLUT[ch * 128:(ch + 1) * 128, t0:t0 + 512], stgb[si][:], reads=[b_stgb[si]])
                else:
                    for j in range(4):
                        pi = pbase + j
                        si = sti % 4
                        sti += 1
                        if j % 2 == 0:
                            s.op("act", lambda: nc.scalar.copy(out=stgb[si][:], in_=k.ps[pi][:, :]),
                                 reads=[k.psb[pi]], writes=[b_stgb[si]])
                        else:
                            s.op("dve", lambda: nc.vector.tensor_copy(out=stgb[si][:], in_=k.ps[pi][:, :]),
                                 reads=[k.psb[pi]], writes=[b_stgb[si]])
                        ch = (n - 9) * 4 + j
                        s.dma("pool", k.XBCT[ch * 128:(ch + 1) * 128, t0:t0 + 512], stgb[si][:], reads=[b_stgb[si]])
            if g % 2 == 1:
                s.barrier()
        s.barrier()


def host_weights(inp, L):
    w_in = inp["w_in"][:L]
    q, kk, v, conv, z, xbc, dtc, gates = np.split(w_in, np.cumsum([1024, 256, 256, 2048, 1024, 1536, 32])[:], axis=-1)
    w_tok = np.concatenate([gates, z, kk, v, dtc], axis=-1)
    def sw(w, nh):
        w4 = w.reshape(w.shape[0], w.shape[1], nh, 64)
        return w4[..., HPERM].reshape(w.shape)
    qs, ks = sw(q, 16), sw(kk, 4)
    ca, cg = conv[..., :1024], conv[..., 1024:]
    cols = []
    for c in range(8):
        cols += [q[..., c * 128:(c + 1) * 128], qs[..., c * 128:(c + 1) * 128]]
    for c in range(2):
        cols += [kk[..., c * 128:(c + 1) * 128], ks[..., c * 128:(c + 1) * 128]]
    for c in range(8):
        cols += [ca[..., c * 128:(c + 1) * 128], cg[..., c * 128:(c + 1) * 128]]
    cols.append(xbc)
    w_feat = np.concatenate(cols, axis=-1)
    assert w_tok.shape[-1] == NTOKW and w_feat.shape[-1] == NFEATW
    def colsl(g):
        return np.ascontiguousarray(g.reshape(g.shape[0], NKC, 128).transpose(0, 2, 1))
    return dict(
        w_ada=np.ascontiguousarray(inp["w_ada"][:L]), b_ada=np.ascontiguousarray(inp["b_ada"][:L, None, :]),
        g1c=colsl(inp["norm1_g"][:L]), g2c=colsl(inp["norm2_g"][:L]),
        w_tok=np.ascontiguousarray(w_tok), w_feat=np.ascontiguousarray(w_feat),
        ident=np.eye(128, dtype=np.float32),
    )


def rope_tables(T, sample):
    cos = np.ones((128, T), np.float32)
    sin = np.zeros((128, T), np.float32)
    if sample:
        pos = np.arange(T)
        row = (pos // 64).astype(np.float32)
        col = (pos % 64).astype(np.float32)
        freqs = (10000.0 ** (-np.arange(16, dtype=np.float32) / 16)).astype(np.float32)
        ar = row[None, :] * freqs[:, None]
        ac = col[None, :] * freqs[:, None]
        c64 = np.concatenate([np.cos(ar), np.cos(ar), np.cos(ac), np.cos(ac)], 0)
        s64 = np.concatenate([-np.sin(ar), np.sin(ar), -np.sin(ac), np.sin(ac)], 0)
        cos = np.concatenate([c64, c64], 0).astype(np.float32)
        sin = np.concatenate([s64, s64], 0).astype(np.float32)
    return np.ascontiguousarray(cos), np.ascontiguousarray(sin)


def bc_load(k, st, name, src_ap, n, b=None):
    t = sbt(k, st, name, [128, n], F32)
    bb = k.s.buf(name)
    k.s.dma("sp", t[:], src_ap.partition_broadcast(128), writes=[bb])
    return t, bb


def phase_ssd(k, l, d):
    nc, s, T, NT = k.nc, k.s, k.T, k.NT
    with ExitStack() as st:
        new_ps_bufs(k)
        P, PB = k.ps, k.psb
        ident = sbt(k, st, "ident", [128, 128], F32); b_c = s.buf("consts")
        s.dma("sp", ident[:], k.ident[:, :], writes=[b_c])
        identb = sbt(k, st, "identb", [128, 128], BF16)
        s.op("dve", lambda: nc.vector.tensor_copy(out=identb[:], in_=ident[:]), reads=[b_c], writes=[b_c])
        U = sbt(k, st, "U", [128, 128], F32)
        negUT = sbt(k, st, "negUT", [128, 128], F32)
        ones = sbt(k, st, "ones", [128, 128], F32)
        s.dma("sp", U[:], k.tri[d], writes=[b_c])
        s.dma("sp", ones[:], k.tri[2], writes=[b_c])
        s.op("pool", lambda: nc.gpsimd.tensor_scalar(negUT[:], U[:], -1.0, None, ALU.mult), reads=[b_c], writes=[b_c])
        f0 = sbt(k, st, "f0", [128, 1], F32)
        s.dma("sp", f0[:], k.f0[:, :], writes=[b_c])
        cw = sbt(k, st, "cw", [128, 48], F32)
        cb = sbt(k, st, "cb", [128, 12], F32)
        s.dma("sp", cw[:], k.ssd_cw[l, d], writes=[b_c])
        s.dma("sp", cb[:], k.ssd_cb[l, d], writes=[b_c])
        diag = sbt(k, st, "diag", [128, 48, 128], BF16)
        s.op("dve", lambda: nc.vector.tensor_tensor(out=diag[:], in0=ident[:, None, :].to_broadcast([128, 48, 128]),
                                                    in1=cw[:, :, None].to_broadcast([128, 48, 128]), op=ALU.mult),
             reads=[b_c], writes=[b_c])
        Abc, _ = bc_load(k, st, "Abc", k.ssd_A_log[l, d], 16, b_c)
        dtb, b_dtb = bc_load(k, st, "dtb", k.ssd_dt_bias[l, d], 16)
        Dbc, b_Dbc = bc_load(k, st, "Dbc", k.ssd_D[l, d], 16)
        b_A = s.buf("A")
        s.op("act", lambda: nc.scalar.activation(out=Abc[:], in_=Abc[:], func=AF.Exp), reads=[_], writes=[b_A])
        s.op("pool", lambda: nc.gpsimd.tensor_scalar(Abc[:], Abc[:], -1.0, None, ALU.mult), reads=[b_A], writes=[b_A])
        if d == 0:
            gn, b_gn = bc_load(k, st, "gn", k.ssd_norm_g[l], 1024)
        HT = sbt(k, st, "HT", [128, 1024], F32); b_HT = s.buf("HT")
        HTb = sbt(k, st, "HTb", [128, 1024], BF16); b_HTb = s.buf("HTb")
        h0t = sbt(k, st, "h0t", [128, 8, 128], F32); b_h0t = s.buf("h0t")
        s.dma("sp", h0t[:], k.h0[l, d].rearrange("(c p) n -> p c n", p=128), writes=[b_h0t])
        for half in range(2):
            for j in range(4):
                s.op("pe", lambda: nc.tensor.transpose(P[half][:, j * 128:(j + 1) * 128], h0t[:, half * 4 + j, :], ident[:]),
                     reads=[b_h0t, b_c], writes=[PB[half]])
            s.op("act", lambda: nc.scalar.copy(out=HT[:, half * 512:(half + 1) * 512], in_=P[half][:, :]),
                 reads=[PB[half]], writes=[b_HT])
        u = sbt(k, st, "u", [128, 12, 134], BF16); b_u = s.buf("u")
        ua = sbt(k, st, "ua", [128, 12, 128], F32); b_ua = s.buf("ua")
        uab = sbt(k, st, "uab", [128, 4, 128], BF16); b_uab = s.buf("uab")
        xs = sbt(k, st, "xs", [128, 1024], F32); b_xs = s.buf("xs")
        Btok = sbt(k, st, "Btok", [128, 256], BF16); b_Btok = s.buf("Btok")
        dtr = sbt(k, st, "dtr", [128, 16], F32); b_dtr = s.buf("dtr")
        dtt = sbt(k, st, "dtt", [128, 16], F32); b_dtt = s.buf("dtt")
        a = sbt(k, st, "a", [128, 16], F32); b_a = s.buf("a")
        D2 = sbt(k, st, "D2", [128, 16, 128], F32); b_D2 = s.buf("D2")
        D3 = sbt(k, st, "D3", [128, 16, 128], F32); b_D3 = s.buf("D3")
        E = sbt(k, st, "E", [128, 16, 128], F32); b_E = s.buf("E")
        cbm = sbt(k, st, "cbm", [128, 2, 128], F32); b_cbm = s.buf("cbm")
        MT = sbt(k, st, "MT", [128, 16, 128], BF16); b_MT = s.buf("MT")
        xg = sbt(k, st, "xg", [128, 1024], BF16); b_xg = s.buf("xg")
        xge = sbt(k, st, "xge", [128, 1024], BF16); b_xge = s.buf("xge")
        sm = sbt(k, st, "sm", [128, 4, 16], F32); b_sm = s.buf("sm")
        t1 = sbt(k, st, "t1", [128, 1024], F32); b_t1 = s.buf("t1")
        t2 = sbt(k, st, "t2", [128, 1024], F32); b_t2 = s.buf("t2")
        yb = sbt(k, st, "yb", [128, 1024], F32); b_yb = s.buf("yb")
        zt = sbt(k, st, "zt", [128, 1024], F32); b_zt = s.buf("zt")
        so = sbt(k, st, "so", [128, 8, 128], F32); b_so = s.buf("so")
        syo = sbt(k, st, "syo", [128, 1024], F32); b_syo = s.buf("syo")
        ssq = sbt(k, st, "ssq2", [128, 1], F32); b_ssq = s.buf("ssq2")

        order = list(range(NT)) if d == 0 else list(range(NT - 1, -1, -1))
        for c in order:
            t0 = c * 128
            seg_first = (c % 2 == 0) if d == 0 else (c % 2 == 1)
            seg_last = not seg_first
            lo, hi = t0 - 3, t0 + 131
            lo_c, hi_c = max(lo, 0), min(hi, T)
            s.dma("sp", u[:, :, lo_c - lo:134 - (hi - hi_c)], k.XBCT[:, lo_c:hi_c].rearrange("(c p) t -> p c t", p=128),
                  writes=[b_u])
            if d == 0:
                if c == 0:
                    s.op("pool", lambda: nc.gpsimd.memset(u[:, :, 0:3], 0.0), writes=[b_u])
                elif seg_first:
                    s.op("pool", lambda: nc.gpsimd.tensor_scalar(u[:, :, 0:3], u[:, :, 0:3], f0[:, 0:1], None, ALU.mult),
                         reads=[b_c], writes=[b_u])
            else:
                if c == NT - 1:
                    s.op("pool", lambda: nc.gpsimd.memset(u[:, :, 131:134], 0.0), writes=[b_u])
                elif seg_first:
                    s.op("pool", lambda: nc.gpsimd.tensor_scalar(u[:, :, 131:134], u[:, :, 131:134], f0[:, 0:1], None, ALU.mult),
                         reads=[b_c], writes=[b_u])
            for ch in range(12):
                bank = ch // 4
                for j in range(4):
                    off = j if d == 0 else 6 - j
                    s.op("pe", lambda: nc.tensor.matmul(P[bank][:, (ch % 4) * 128:(ch % 4 + 1) * 128], lhsT=diag[:, j * 12 + ch, :],
                                                        rhs=u[:, ch, off:off + 128], start=(j == 0), stop=(j == 3)),
                         reads=[b_u, b_c], writes=[PB[bank]])
            for ch in range(12):
                bank = ch // 4
                s.op("act", lambda: nc.scalar.activation(out=ua[:, ch, :], in_=P[bank][:, (ch % 4) * 128:(ch % 4 + 1) * 128],
                                                         func=AF.Silu, bias=cb[:, ch:ch + 1]),
                     reads=[PB[bank], b_c], writes=[b_ua])
            s.op("pool", lambda: nc.gpsimd.tensor_copy(out=uab[:], in_=ua[:, 8:12, :]), reads=[b_ua], writes=[b_uab])
            for ch in range(8):
                bank = 5 + ch // 4
                s.op("pe", lambda: nc.tensor.transpose(P[bank][:, (ch % 4) * 128:(ch % 4 + 1) * 128], ua[:, ch, :], ident[:]),
                     reads=[b_ua, b_c], writes=[PB[bank]])
            for g in range(2):
                s.op("pe", lambda: nc.tensor.transpose(P[7][:, g * 128:(g + 1) * 128], ua[:, 8 + g, :], ident[:]),
                     reads=[b_ua, b_c], writes=[PB[7]])
            s.op("act", lambda: nc.scalar.copy(out=xs[:, 0:512], in_=P[5][:, :]), reads=[PB[5]], writes=[b_xs])
            s.op("act", lambda: nc.scalar.copy(out=xs[:, 512:1024], in_=P[6][:, :]), reads=[PB[6]], writes=[b_xs])
            s.op("dve", lambda: nc.vector.tensor_copy(out=Btok[:], in_=P[7][:, 0:256]), reads=[PB[7]], writes=[b_Btok])
            s.dma("sp", dtr[:], k.DT[t0:t0 + 128, d * 16:(d + 1) * 16], writes=[b_dtr])
            s.op("dve", lambda: nc.vector.tensor_tensor(out=dtt[:], in0=dtr[:], in1=dtb[:], op=ALU.add),
                 reads=[b_dtr, b_dtb], writes=[b_dtt])
            s.op("act", lambda: nc.scalar.activation(out=dtt[:], in_=dtt[:], func=AF.Exp), reads=[b_dtt], writes=[b_dtt])
            s.op("act", lambda: nc.scalar.activation(out=dtt[:], in_=dtt[:], func=AF.Ln, bias=1.0), reads=[b_dtt], writes=[b_dtt])
            s.op("dve", lambda: nc.vector.tensor_tensor(out=a[:], in0=dtt[:], in1=Abc[:], op=ALU.mult),
                 reads=[b_dtt, b_A], writes=[b_a])
            s.op("dve", lambda: nc.vector.tensor_tensor(out=D2[:], in0=a[:, :, None].to_broadcast([128, 16, 128]),
                                                        in1=U[:, None, :].to_broadcast([128, 16, 128]), op=ALU.mult),
                 reads=[b_a, b_c], writes=[b_D2])
            s.op("pool", lambda: nc.gpsimd.tensor_copy(out=D3[:], in_=a[:, :, None].to_broadcast([128, 16, 128])),
                 reads=[b_a], writes=[b_D3])
            for q4 in range(4):
                s.op("pe", lambda: nc.tensor.matmul(P[q4][:, :], lhsT=ones[:], rhs=D2[:, q4 * 4:(q4 + 1) * 4, :],
                                                    start=True, stop=False), reads=[b_D2, b_c], writes=[PB[q4]])
                s.op("pe", lambda: nc.tensor.matmul(P[q4][:, :], lhsT=negUT[:], rhs=D3[:, q4 * 4:(q4 + 1) * 4, :],
                                                    start=False, stop=True), reads=[b_D3, b_c], writes=[PB[q4]])
            s.op("pe", lambda: nc.tensor.matmul(P[4][:, 256:272], lhsT=U[:], rhs=a[:], start=True, stop=True),
                 reads=[b_a, b_c], writes=[PB[4]])
            s.op("pe", lambda: nc.tensor.matmul(P[4][:, 272:288], lhsT=ones[:], rhs=a[:], start=True, stop=True),
                 reads=[b_a, b_c], writes=[PB[4]])
            for g in range(2):
                s.op("pe", lambda: nc.tensor.matmul(P[4][:, g * 128:(g + 1) * 128], lhsT=uab[:, g, :], rhs=uab[:, 2 + g, :],
                                                    start=True, stop=True), reads=[b_uab], writes=[PB[4]])
            for q4 in range(4):
                s.op("dve", lambda: nc.vector.tensor_scalar(E[:, q4 * 4:(q4 + 1) * 4, :], P[q4][:, :], 0.0, None, ALU.min),
                     reads=[PB[q4]], writes=[b_E])
            s.op("act", lambda: nc.scalar.activation(out=E[:], in_=E[:], func=AF.Exp), reads=[b_E], writes=[b_E])
            s.op("dve", lambda: nc.vector.tensor_tensor(out=cbm[:], in0=P[4][:, 0:256], in1=U[:, None, :].to_broadcast([128, 2, 128]),
                                                        op=ALU.mult), reads=[PB[4], b_c], writes=[b_cbm])
            for g in range(2):
                s.op("pool", lambda: nc.gpsimd.tensor_tensor(out=MT[:, g * 8:(g + 1) * 8, :], in0=E[:, g * 8:(g + 1) * 8, :],
                                                             in1=cbm[:, g:g + 1, :].to_broadcast([128, 8, 128]), op=ALU.mult),
                     reads=[b_E, b_cbm], writes=[b_MT])
            s.op("dve", lambda: nc.vector.tensor_copy(out=sm[:, 0, :], in_=P[4][:, 256:272]), reads=[PB[4]], writes=[b_sm])
            s.op("act", lambda: nc.scalar.activation(out=sm[:, 1, :], in_=P[4][:, 256:272], func=AF.Exp), reads=[PB[4]], writes=[b_sm])
            s.op("dve", lambda: nc.vector.tensor_tensor(out=sm[:, 2, :], in0=P[4][:, 272:288], in1=sm[:, 0, :], op=ALU.subtract),
                 reads=[PB[4], b_sm], writes=[b_sm])
            s.op("act", lambda: nc.scalar.activation(out=sm[:, 2, :], in_=sm[:, 2, :], func=AF.Exp), reads=[b_sm], writes=[b_sm])
            s.op("dve", lambda: nc.vector.tensor_tensor(out=sm[:, 2, :], in0=sm[:, 2, :], in1=dtt[:], op=ALU.mult),
                 reads=[b_sm, b_dtt], writes=[b_sm])
            s.op("act", lambda: nc.scalar.activation(out=sm[:, 3, :], in_=P[4][:, 272:288], func=AF.Exp), reads=[PB[4]], writes=[b_sm])
            xs3 = xs[:].rearrange("p (h e) -> p h e", e=64)
            s.op("dve", lambda: nc.vector.tensor_tensor(out=xg[:].rearrange("p (h e) -> p h e", e=64), in0=xs3,
                                                        in1=dtt[:, :, None].to_broadcast([128, 16, 64]), op=ALU.mult),
                 reads=[b_xs, b_dtt], writes=[b_xg])
            s.op("pool", lambda: nc.gpsimd.tensor_tensor(out=xge[:].rearrange("p (h e) -> p h e", e=64), in0=xs3,
                                                         in1=sm[:, 2, :, None].to_broadcast([128, 16, 64]), op=ALU.mult),
                 reads=[b_xs, b_sm], writes=[b_xge])
            if seg_first and c != order[0]:
                s.op("dve", lambda: nc.vector.tensor_scalar(HT[:], HT[:], f0[:, 0:1], None, ALU.mult), reads=[b_c], writes=[b_HT])
            s.op("act", lambda: nc.scalar.copy(out=HTb[:], in_=HT[:]), reads=[b_HT], writes=[b_HTb])
            for h in range(16):
                bank = h // 8
                s.op("pe", lambda: nc.tensor.matmul(P[bank][:, (h % 8) * 64:(h % 8 + 1) * 64], lhsT=MT[:, h, :],
                                                    rhs=xg[:, h * 64:(h + 1) * 64], start=True, stop=True),
                     reads=[b_MT, b_xg], writes=[PB[bank]])
            for g in range(2):
                s.op("pe", lambda: nc.tensor.matmul(P[2 + g][:, :], lhsT=uab[:, 2 + g, :], rhs=HTb[:, g * 512:(g + 1) * 512],
                                                    start=True, stop=True), reads=[b_uab, b_HTb], writes=[PB[2 + g]])
            for g in range(2):
                s.op("pe", lambda: nc.tensor.matmul(P[5 + g][:, :], lhsT=Btok[:, g * 128:(g + 1) * 128], rhs=xge[:, g * 512:(g + 1) * 512],
                                                    start=True, stop=True), reads=[b_Btok, b_xge], writes=[PB[5 + g]])
            for g in range(2):
                sl = slice(g * 512, (g + 1) * 512)
                s.op("dve", lambda: nc.vector.tensor_tensor(out=t1[:, sl].rearrange("p (h e) -> p h e", e=64),
                                                            in0=P[2 + g][:, :].rearrange("p (h e) -> p h e", e=64),
                                                            in1=sm[:, 1, g * 8:(g + 1) * 8, None].to_broadcast([128, 8, 64]), op=ALU.mult),
                     reads=[PB[2 + g], b_sm], writes=[b_t1])
            s.op("pool", lambda: nc.gpsimd.tensor_tensor(out=t2[:].rearrange("p (h e) -> p h e", e=64), in0=xs3,
                                                         in1=Dbc[:, :, None].to_broadcast([128, 16, 64]), op=ALU.mult),
                 reads=[b_xs, b_Dbc], writes=[b_t2])
            s.op("pool", lambda: nc.gpsimd.tensor_tensor(out=t1[:], in0=t1[:], in1=t2[:], op=ALU.add), reads=[b_t1, b_t2], writes=[b_t1])
            for g in range(2):
                sl = slice(g * 512, (g + 1) * 512)
                s.op("dve", lambda: nc.vector.tensor_tensor(out=t1[:, sl], in0=P[g][:, :], in1=t1[:, sl], op=ALU.add),
                     reads=[PB[g], b_t1], writes=[b_t1])
            s.op("dve", lambda: nc.vector.tensor_tensor(out=HT[:].rearrange("p (h e) -> p h e", e=64),
                                                        in0=HT[:].rearrange("p (h e) -> p h e", e=64),
                                                        in1=sm[:, 3, :, None].to_broadcast([128, 16, 64]), op=ALU.mult),
                 reads=[b_sm, b_HTb], writes=[b_HT])
            for g in range(2):
                sl = slice(g * 512, (g + 1) * 512)
                s.op("dve", lambda: nc.vector.tensor_tensor(out=HT[:, sl], in0=P[5 + g][:, :], in1=HT[:, sl], op=ALU.add),
                     reads=[PB[5 + g]], writes=[b_HT])
            if d == 1:
                s.dma("pool", k.YB[t0:t0 + 128, :], t1[:], reads=[b_t1])
            else:
                s.dma("sp", yb[:], k.YB[t0:t0 + 128, :], writes=[b_yb])
                s.dma("sp", zt[:], k.Z[t0:t0 + 128, :], writes=[b_zt])
                s.op("pool", lambda: nc.gpsimd.tensor_tensor(out=t1[:], in0=t1[:], in1=yb[:], op=ALU.add), reads=[b_yb, b_t1], writes=[b_t1])
                s.op("act", lambda: nc.scalar.activation(out=zt[:], in_=zt[:], func=AF.Silu), reads=[b_zt], writes=[b_zt])
                s.op("dve", lambda: nc.vector.tensor_tensor(out=t1[:], in0=t1[:], in1=zt[:], op=ALU.mult), reads=[b_t1, b_zt], writes=[b_t1])
                s.op("act", lambda: nc.scalar.activation(out=t2[:], in_=t1[:], func=AF.Square, accum_out=ssq[:]),
                     reads=[b_t1], writes=[b_t2, b_ssq])
                s.op("dve", lambda: nc.vector.tensor_scalar(ssq[:], ssq[:], 1.0 / 1024, EPS, ALU.mult, ALU.add), reads=[b_ssq], writes=[b_ssq])
                s.op("act", lambda: nc.scalar.activation(out=ssq[:], in_=ssq[:], func=AF.Sqrt), reads=[b_ssq], writes=[b_ssq])
                s.op("dve", lambda: nc.vector.reciprocal(ssq[:], ssq[:]), reads=[b_ssq], writes=[b_ssq])
                s.op("dve", lambda: nc.vector.scalar_tensor_tensor(out=syo[:], in0=t1[:], scalar=ssq[:, 0:1], in1=gn[:],
                                                                   op0=ALU.mult, op1=ALU.mult),
                     reads=[b_t1, b_ssq, b_gn], writes=[b_syo])
                s.dma("pool", k.SY[t0:t0 + 128, :], syo[:], reads=[b_syo])
            if seg_last:
                for half in range(2):
                    for j in range(4):
                        s.op("pe", lambda: nc.tensor.transpose(P[half][:, j * 128:(j + 1) * 128],
                                                               HT[:, (half * 4 + j) * 128:(half * 4 + j + 1) * 128], ident[:]),
                             reads=[b_HT, b_c], writes=[PB[half]])
                    s.op("act", lambda: nc.scalar.copy(out=so[:, half * 4:(half + 1) * 4, :],
                                                       in_=P[half][:, :].rearrange("p (c n) -> p c n", n=128)),
                         reads=[PB[half]], writes=[b_so])
                s.dma("pool", k.SSDO[l, d, c // 2].rearrange("(c p) n -> p c n", p=128), so[:], reads=[b_so])
        s.barrier()


def phase_attn(k, l):
    nc, s, T, NT = k.nc, k.s, k.T, k.NT
    SCALE = 64 ** -0.5
    with ExitStack() as st:
        new_ps_bufs(k)
        P, PB = k.ps, k.psb
        b_c = s.buf("consts")
        masks = sbt(k, st, "masks", [128, 4, 128], F32)
        s.dma("sp", masks[:], k.masks.rearrange("m k q -> k m q"), writes=[b_c])
        ctxb = sbt(k, st, "ctxb", [128, 1], F32)
        s.dma("sp", ctxb[:], k.ctxbias[:, :], writes=[b_c])
        esink, b_es = bc_load(k, st, "esink", k.sink[l], 16)
        s.op("act", lambda: nc.scalar.activation(out=esink[:], in_=esink[:], func=AF.Exp), reads=[b_es], writes=[b_es])
        ckT = sbt(k, st, "ckT", [64, 4, 256], BF16)
        s.dma("pool", ckT[:], k.ckT[l], writes=[b_c])
        cv = sbt(k, st, "cv", [128, 2, 4, 65], BF16)
        s.op("dve", lambda: nc.vector.memset(cv[:], 1.0), writes=[b_c])
        for b2 in range(2):
            s.dma("pool", cv[:, b2, :, 0:64], k.cv[l][b2 * 128:(b2 + 1) * 128, :].rearrange("p (h e) -> p h e", e=64), writes=[b_c])
        NB = 2
        qt = [sbt(k, st, f"qt{i}", [64, 16, 128], BF16) for i in range(NB)]; b_qt = s.bufs_n(NB, "qt")
        kt = [sbt(k, st, f"kt{i}", [64, 4, 384], BF16) for i in range(NB)]; b_kt = s.bufs_n(NB, "kt")
        vt = [sbt(k, st, f"vt{i}", [128, 3, 4, 65], BF16) for i in range(NB)]; b_vt = s.bufs_n(NB, "vt")
        for i in range(NB):
            s.op("dve", lambda: nc.vector.memset(vt[i][:], 1.0), writes=[b_vt[i]])
        pt = [sbt(k, st, f"pt{i}", [128, 5, 512], BF16) for i in range(2)]; b_pt = s.bufs_n(2, "pt")
        ao = [sbt(k, st, f"ao{i}", [128, 1024], F32) for i in range(2)]; b_ao = s.bufs_n(2, "ao")
        rden = sbt(k, st, "rden", [128, 4], F32); b_rden = s.buf("rden")
        it = 0
        for q in range(NT):
            t0 = q * 128
            bi = q % NB
            blks = []
            if q > 0:
                blks.append(0)
            blks.append(1)
            if q < NT - 1:
                blks.append(2)
            s.dma("sp", qt[bi][:], k.QT[:, t0:t0 + 128].rearrange("(h d) t -> d h t", d=64), writes=[b_qt[bi]])
            lo, hi = max(t0 - 128, 0), min(t0 + 256, T)
            s.dma("sp", kt[bi][:, :, lo - (t0 - 128):hi - (t0 - 128)], k.KT[:, lo:hi].rearrange("(h d) t -> d h t", d=64),
                  writes=[b_kt[bi]])
            for b in blks:
                r0 = t0 + (b - 1) * 128
                s.dma("pool", vt[bi][:, b, :, 0:64], k.KVO[l][r0:r0 + 128, 256:512].rearrange("p (h e) -> p h e", e=64),
                      writes=[b_vt[bi]])
            par = q % 2
            for hk in range(4):
                pi = it % 2
                it += 1
                allb = blks + [3, 4]
                for n, b in enumerate(allb):
                    bank = (it * 5 + n) % 6
                    if b < 3:
                        lhsT = kt[bi][:, hk, b * 128:(b + 1) * 128]
                    else:
                        lhsT = ckT[:, hk, (b - 3) * 128:(b - 2) * 128]
                    s.op("pe", lambda: nc.tensor.matmul(P[bank][:, :], lhsT=lhsT, rhs=qt[bi][:, hk * 4:(hk + 1) * 4, :],
                                                        start=True, stop=True),
                         reads=[b_kt[bi], b_qt[bi], b_c], writes=[PB[bank]])
                    if b < 3:
                        s.op("act", lambda: nc.scalar.activation(out=pt[pi][:, b, :], in_=P[bank][:, :], func=AF.Exp, scale=SCALE),
                             reads=[PB[bank]], writes=[b_pt[pi]])
                        if b != 1:
                            m = (0 if b == 0 else 2) + par
                            s.op("pool", lambda: nc.gpsimd.tensor_tensor(
                                out=pt[pi][:, b, :].rearrange("p (g q) -> p g q", g=4),
                                in0=pt[pi][:, b, :].rearrange("p (g q) -> p g q", g=4),
                                in1=masks[:, m:m + 1, :].to_broadcast([128, 4, 128]), op=ALU.mult),
                                reads=[b_c], writes=[b_pt[pi]])
                    else:
                        s.op("act", lambda: nc.scalar.activation(out=pt[pi][:, b, :], in_=P[bank][:, :], func=AF.Exp, scale=SCALE,
                                                                 bias=ctxb[:, 0:1]),
                             reads=[PB[bank], b_c], writes=[b_pt[pi]])
                ob = 6 + (it % 2)
                for g in range(4):
                    for n, b in enumerate(allb):
                        rhs = vt[bi][:, b, hk, :] if b < 3 else cv[:, b - 3, hk, :]
                        s.op("pe", lambda: nc.tensor.matmul(P[ob][:, g * 65:(g + 1) * 65], lhsT=pt[pi][:, b, g * 128:(g + 1) * 128],
                                                            rhs=rhs, start=(n == 0), stop=(n == len(allb) - 1)),
                             reads=[b_pt[pi], b_vt[bi], b_c], writes=[PB[ob]])
                o3 = P[ob][:, 0:260].rearrange("p (g e) -> p g e", e=65)
                s.op("dve", lambda: nc.vector.tensor_tensor(out=rden[:], in0=o3[:, :, 64], in1=esink[:, hk * 4:(hk + 1) * 4], op=ALU.add),
                     reads=[PB[ob], b_es], writes=[b_rden])
                s.op("dve", lambda: nc.vector.reciprocal(rden[:], rden[:]), reads=[b_rden], writes=[b_rden])
                ai = q % 2
                s.op("dve", lambda: nc.vector.tensor_tensor(
                    out=ao[ai][:, hk * 256:(hk + 1) * 256].rearrange("p (g e) -> p g e", e=64), in0=o3[:, :, 0:64],
                    in1=rden[:, :, None].to_broadcast([128, 4, 64]), op=ALU.mult),
                    reads=[PB[ob], b_rden], writes=[b_ao[ai]])
            s.dma("pool", k.AO[t0:t0 + 128, :], ao[q % 2][:], reads=[b_ao[q % 2]])
        s.barrier()


def phase_conf(k, l):
    nc, s, T = k.nc, k.s, k.T
    NU = T // 256
    with ExitStack() as st:
        new_ps_bufs(k)
        P, PB = k.ps, k.psb
        b_c = s.buf("consts")
        ident = sbt(k, st, "ident", [128, 128], F32)
        s.dma("sp", ident[:], k.ident[:, :], writes=[b_c])
        f0 = sbt(k, st, "f0", [128, 1], F32)
        s.dma("sp", f0[:], k.f0[:, :], writes=[b_c])
        cw = sbt(k, st, "cw", [128, 248], F32)
        cb = sbt(k, st, "cb", [128, 8], F32)
        s.dma("sp", cw[:], k.conf_w[l], writes=[b_c])
        s.dma("sp", cb[:], k.conf_b[l], writes=[b_c])
        diag = sbt(k, st, "diag", [128, 248, 128], BF16)
        for h2 in range(2):
            s.op("dve", lambda: nc.vector.tensor_tensor(out=diag[:, h2 * 124:(h2 + 1) * 124, :],
                                                        in0=ident[:, None, :].to_broadcast([128, 124, 128]),
                                                        in1=cw[:, h2 * 124:(h2 + 1) * 124, None].to_broadcast([128, 124, 128]), op=ALU.mult),
                 reads=[b_c], writes=[b_c])
        lg, b_lg = bc_load(k, st, "lng", k.conf_ln_g[l], 1024)
        lb, b_lb = bc_load(k, st, "lnb", k.conf_ln_b[l], 1024)
        hin = [sbt(k, st, f"hin{i}", [128, 8, 286], BF16) for i in range(2)]; b_hin = s.bufs_n(2, "hin")
        cvo = sbt(k, st, "cvo", [128, 8, 256], F32); b_cvo = s.buf("cvo")
        st4 = sbt(k, st, "st4", [128, 4], F32); b_st4 = s.buf("st4")
        junk = sbt(k, st, "junk", [128, 1024], F32); b_junk = s.buf("junk")
        y = sbt(k, st, "y", [128, 1024], F32); b_y = s.buf("y")
        yo = [sbt(k, st, f"yo{i}", [128, 1024], F32) for i in range(2)]; b_yo = s.bufs_n(2, "yo")
        for un in range(NU):
            t0 = un * 256
            hi_ = un % 2
            lo, hi = t0 - 15, t0 + 271
            lo_c, hi_c = max(lo, 0), min(hi, T)
            s.dma("sp", hin[hi_][:, :, lo_c - lo:286 - (hi - hi_c)], k.GLUT[:, lo_c:hi_c].rearrange("(c p) t -> p c t", p=128),
                  writes=[b_hin[hi_]])
            if un == 0:
                s.op("pool", lambda: nc.gpsimd.memset(hin[hi_][:, :, 0:15], 0.0), writes=[b_hin[hi_]])
            else:
                s.op("pool", lambda: nc.gpsimd.tensor_scalar(hin[hi_][:, :, 0:15], hin[hi_][:, :, 0:15], f0[:, 0:1], None, ALU.mult),
                     reads=[b_c], writes=[b_hin[hi_]])
            if un == NU - 1:
                s.op("pool", lambda: nc.gpsimd.memset(hin[hi_][:, :, 271:286], 0.0), writes=[b_hin[hi_]])
            else:
                s.op("pool", lambda: nc.gpsimd.tensor_scalar(hin[hi_][:, :, 271:286], hin[hi_][:, :, 271:286], f0[:, 0:1], None, ALU.mult),
                     reads=[b_c], writes=[b_hin[hi_]])
            for ch in range(8):
                bank = ch % 4
                for j in range(31):
                    s.op("pe", lambda: nc.tensor.matmul(P[bank][:, 0:256], lhsT=diag[:, j * 8 + ch, :], rhs=hin[hi_][:, ch, j:j + 256],
                                                        start=(j == 0), stop=(j == 30)),
                         reads=[b_hin[hi_], b_c], writes=[PB[bank]])
                s.op("act", lambda: nc.scalar.activation(out=cvo[:, ch, :], in_=P[bank][:, 0:256], func=AF.Identity, bias=cb[:, ch:ch + 1]),
                     reads=[PB[bank], b_c], writes=[b_cvo])
            for tt in range(2):
                for ch in range(8):
                    bank = 4 + ch // 4
                    s.op("pe", lambda: nc.tensor.transpose(P[bank][:, (ch % 4) * 128:(ch % 4 + 1) * 128],
                                                           cvo[:, ch, tt * 128:(tt + 1) * 128], ident[:]),
                         reads=[b_cvo, b_c], writes=[PB[bank]])
                for hf in range(2):
                    s.op("act", lambda: nc.scalar.activation(out=y[:, hf * 512:(hf + 1) * 512], in_=P[4 + hf][:, :], func=AF.Copy,
                                                             accum_out=st4[:, hf:hf + 1]),
                         reads=[PB[4 + hf]], writes=[b_y, b_st4])
                s.op("dve", lambda: nc.vector.tensor_tensor(out=st4[:, 0:1], in0=st4[:, 0:1], in1=st4[:, 1:2], op=ALU.add),
                     reads=[b_st4], writes=[b_st4])
                s.op("dve", lambda: nc.vector.tensor_scalar(st4[:, 0:1], st4[:, 0:1], 1.0 / 1024, None, ALU.mult), reads=[b_st4], writes=[b_st4])
                s.op("dve", lambda: nc.vector.tensor_scalar(y[:], y[:], st4[:, 0:1], None, ALU.subtract), reads=[b_st4], writes=[b_y])
                s.op("act", lambda: nc.scalar.activation(out=junk[:], in_=y[:], func=AF.Square, accum_out=st4[:, 2:3]),
                     reads=[b_y], writes=[b_junk, b_st4])
                s.op("dve", lambda: nc.vector.tensor_scalar(st4[:, 2:3], st4[:, 2:3], 1.0 / 1024, EPS, ALU.mult, ALU.add),
                     reads=[b_st4], writes=[b_st4])
                s.op("act", lambda: nc.scalar.activation(out=st4[:, 2:3], in_=st4[:, 2:3], func=AF.Sqrt), reads=[b_st4], writes=[b_st4])
                s.op("dve", lambda: nc.vector.reciprocal(st4[:, 2:3], st4[:, 2:3]), reads=[b_st4], writes=[b_st4])
                s.op("dve", lambda: nc.vector.scalar_tensor_tensor(out=y[:], in0=y[:], scalar=st4[:, 2:3], in1=lg[:],
                                                                   op0=ALU.mult, op1=ALU.mult), reads=[b_st4, b_lg], writes=[b_y])
                s.op("pool", lambda: nc.gpsimd.tensor_tensor(out=y[:], in0=y[:], in1=lb[:], op=ALU.add), reads=[b_lb], writes=[b_y])
                oi = (un * 2 + tt) % 2
                s.op("act", lambda: nc.scalar.activation(out=yo[oi][:], in_=y[:], func=AF.Silu), reads=[b_y], writes=[b_yo[oi]])
                s.dma("pool", k.CO[t0 + tt * 128:t0 + (tt + 1) * 128, :], yo[oi][:], reads=[b_yo[oi]])
        s.barrier()


def phase_merge(k, l):
    nc, s, T = k.nc, k.s, k.T
    xsrc = k.x0 if l == 0 else k.XR
    with ExitStack() as st:
        new_ps_bufs(k)
        P, PB = k.ps, k.psb
        b_c = s.buf("consts")
        ident = sbt(k, st, "ident", [128, 128], F32)
        s.dma("sp", ident[:], k.ident[:, :], writes=[b_c])
        gb, b_gb = bc_load(k, st, "gb", k.gate_b[l], 6144)
        g1, b_g1 = bc_load(k, st, "g1", k.MOD[l][0, 2 * D:3 * D], D)
        XT = sbt(k, st, "XT", [128, 24, 512], BF16); b_XT = s.buf("XT")
        xin = [sbt(k, st, f"xin{i}", [128, 1024], F32) for i in range(2)]; b_xin = s.bufs_n(2, "xin")
        wcb = [sbt(k, st, f"wcb{i}", [128, 16, 512], BF16) for i in range(2)]; b_w = s.bufs_n(2, "wcb")
        mg = [sbt(k, st, f"mg{i}", [128, D], F32) for i in range(4)]; b_mg = s.bufs_n(4, "mg")
        xt = [sbt(k, st, f"xt{i}", [128, D], F32) for i in range(4)]; b_xt = s.bufs_n(4, "xt")
        gt = [sbt(k, st, f"gt{i}", [128, 512], F32) for i in range(2)]; b_gt = s.bufs_n(2, "gt")
        tm = [sbt(k, st, f"tm{i}", [128, 512], F32) for i in range(2)]; b_tm = s.bufs_n(2, "tm")
        srcs = [k.AO, k.CO, k.SY]
        wsrc = [k.w_attn_o, k.w_conv_o, k.w_ssd_o]
        wit = 0; xit = 0; git = 0; prot = 0
        for g in range(k.NG):
            t0 = g * 512
            for ti in range(4):
                s.dma("sp", xt[ti][:], xsrc[t0 + ti * 128:t0 + (ti + 1) * 128, :], writes=[b_xt[ti]])
            for b in range(3):
                for ti in range(4):
                    xi = xit % 2; xit += 1
                    s.dma("sp", xin[xi][:], srcs[b][t0 + ti * 128:t0 + (ti + 1) * 128, :], writes=[b_xin[xi]])
                    for half in range(2):
                        bank = prot % 8; prot += 1
                        for j in range(4):
                            c = half * 4 + j
                            s.op("pe", lambda: nc.tensor.transpose(P[bank][:, j * 128:(j + 1) * 128], xin[xi][:, c * 128:(c + 1) * 128], ident[:]),
                                 reads=[b_xin[xi], b_c], writes=[PB[bank]])
                        s.op("act", lambda: nc.scalar.copy(out=XT[:, b * 8 + half * 4:b * 8 + half * 4 + 4, ti * 128:(ti + 1) * 128],
                                                           in_=P[bank][:, :].rearrange("p (c t) -> p c t", t=128)),
                             reads=[PB[bank]], writes=[b_XT])
            for n in range(4):
                for b in range(3):
                    wi = wit % 2; wit += 1
                    s.dma("pool", wcb[wi][:, 0:8, :], wsrc[b][l][:, n * 512:(n + 1) * 512].rearrange("(c p) n -> p c n", p=128),
                          writes=[b_w[wi]])
                    for ti in range(4):
                        bank = prot % 8; prot += 1
                        for c in range(8):
                            s.op("pe", lambda: nc.tensor.matmul(P[bank][:, :], lhsT=XT[:, b * 8 + c, ti * 128:(ti + 1) * 128],
                                                                rhs=wcb[wi][:, c, :], start=(c == 0), stop=(c == 7)),
                                 reads=[b_XT, b_w[wi]], writes=[PB[bank]])
                        gi = git % 2; git += 1
                        col = b * 2048 + n * 512
                        s.dma("sp", gt[gi][:], k.GATES[t0 + ti * 128:t0 + (ti + 1) * 128, col:col + 512], writes=[b_gt[gi]])
                        s.op("pool", lambda: nc.gpsimd.tensor_tensor(out=gt[gi][:], in0=gt[gi][:], in1=gb[:, col:col + 512], op=ALU.add),
                             reads=[b_gb], writes=[b_gt[gi]])
                        s.op("act", lambda: nc.scalar.activation(out=gt[gi][:], in_=gt[gi][:], func=AF.Sigmoid), reads=[], writes=[b_gt[gi]])
                        msl = mg[ti][:, n * 512:(n + 1) * 512]
                        if b == 0:
                            s.op("dve", lambda: nc.vector.tensor_tensor(out=msl, in0=P[bank][:, :], in1=gt[gi][:], op=ALU.mult),
                                 reads=[PB[bank], b_gt[gi]], writes=[b_mg[ti]])
                        else:
                            s.op("dve", lambda: nc.vector.tensor_tensor(out=tm[gi][:], in0=P[bank][:, :], in1=gt[gi][:], op=ALU.mult),
                                 reads=[PB[bank], b_gt[gi]], writes=[b_tm[gi]])
                            s.op("pool", lambda: nc.gpsimd.tensor_tensor(out=msl, in0=msl, in1=tm[gi][:], op=ALU.add),
                                 reads=[b_tm[gi]], writes=[b_mg[ti]])
            for ti in range(4):
                for c4 in range(4):
                    bank = prot % 8; prot += 1
                    for j in range(4):
                        c = c4 * 4 + j
                        s.op("pe", lambda: nc.tensor.transpose(P[bank][:, j * 128:(j + 1) * 128], mg[ti][:, c * 128:(c + 1) * 128], ident[:]),
                             reads=[b_mg[ti], b_c], writes=[PB[bank]])
                    s.op("act", lambda: nc.scalar.copy(out=XT[:, c4 * 4:c4 * 4 + 4, ti * 128:(ti + 1) * 128],
                                                       in_=P[bank][:, :].rearrange("p (c t) -> p c t", t=128)),
                         reads=[PB[bank]], writes=[b_XT])
            for n in range(4):
                wi = wit % 2; wit += 1
                s.dma("pool", wcb[wi][:], k.w_out[l][:, n * 512:(n + 1) * 512].rearrange("(c p) n -> p c n", p=128), writes=[b_w[wi]])
                for ti in range(4):
                    bank = prot % 8; prot += 1
                    for c in range(16):
                        s.op("pe", lambda: nc.tensor.matmul(P[bank][:, :], lhsT=XT[:, c, ti * 128:(ti + 1) * 128], rhs=wcb[wi][:, c, :],
                                                            start=(c == 0), stop=(c == 15)),
                             reads=[b_XT, b_w[wi]], writes=[PB[bank]])
                    gi = git % 2; git += 1
                    sl = slice(n * 512, (n + 1) * 512)
                    s.op("dve", lambda: nc.vector.tensor_tensor(out=tm[gi][:], in0=P[bank][:, :], in1=g1[:, sl], op=ALU.mult),
                         reads=[PB[bank], b_g1], writes=[b_tm[gi]])
                    s.op("pool", lambda: nc.gpsimd.tensor_tensor(out=xt[ti][:, sl], in0=xt[ti][:, sl], in1=tm[gi][:], op=ALU.add),
                         reads=[b_tm[gi]], writes=[b_xt[ti]])
            for ti in range(4):
                s.dma("sp", k.XR[t0 + ti * 128:t0 + (ti + 1) * 128, :], xt[ti][:], reads=[b_xt[ti]])
        s.barrier()


def phase_peer(k, l):
    nc, s, T = k.nc, k.s, k.T
    last = (l == k.L - 1)
    NEG = -1.0e30
    with ExitStack() as st:
        new_ps_bufs(k)
        k.psrot = 0
        P, PB = k.ps, k.psb
        A2, b_A2, sh2, b_sh2 = load_mod_cols(k, st, l, 1)
        tmp = alloc_norm_tmp(k, st)
        xs, b_xs = tmp[6], tmp[7]
        b_c = s.buf("consts")
        A2r, b_A2r = bc_load(k, st, "A2r", k.MOD[l][0, 4 * D:5 * D], D)
        g2r, b_g2r = bc_load(k, st, "g2r", k.norm2_g[l], D)
        s.op("dve", lambda: nc.vector.scalar_tensor_tensor(out=A2r[:], in0=A2r[:], scalar=1.0, in1=g2r[:], op0=ALU.add, op1=ALU.mult),
             reads=[b_g2r], writes=[b_A2r])
        sh2r, b_sh2r = bc_load(k, st, "sh2r", k.MOD[l][0, 3 * D:4 * D], D)
        gate2, b_gate2 = bc_load(k, st, "gate2", k.MOD[l][0, 5 * D:6 * D], D)
        if last:
            s.dma("sp", g2r[:], k.final_g.partition_broadcast(128), reads=[b_A2r], writes=[b_g2r])
        skT = sbt(k, st, "skT", [128, 16, 128], BF16)
        s.dma("pool", skT[:], k.skT[l].rearrange("j d n -> d j n"), writes=[b_c])
        iota = sbt(k, st, "iota", [128, 16], F32)
        s.dma("sp", iota[:], k.iota16[:, :], writes=[b_c])
        xt = [sbt(k, st, f"xt{i}", [128, D], F32) for i in range(2)]; b_xt = s.bufs_n(2, "xt")
        h2t = [sbt(k, st, f"h2t{i}", [128, D], BF16) for i in range(4)]; b_h2t = s.bufs_n(4, "h2t")
        hT = sbt(k, st, "hT", [128, NKC, 512], BF16); b_hT = s.buf("hT")
        qT = sbt(k, st, "qT", [128, 16, 512], BF16); b_qT = s.buf("qT")
        wcb = [sbt(k, st, f"wcb{i}", [128, 16, 512], BF16) for i in range(2)]; b_w = s.bufs_n(2, "wcb")
        NGB = 4
        gbuf = [sbt(k, st, f"gbuf{i}", [128, D], F32) for i in range(NGB)]; b_gbuf = s.bufs_n(NGB, "gbuf")
        acc = sbt(k, st, "acc", [128, D], F32); b_acc = s.buf("acc")
        sv = sbt(k, st, "sv", [128, 16, 16], F32); b_sv = s.buf("sv")
        si = sbt(k, st, "si", [128, 16, 16], U32); b_si = s.buf("si")
        sif = sbt(k, st, "sif", [128, 16, 16], F32); b_sif = s.buf("sif")
        scr = sbt(k, st, "scr", [128, 256], F32); b_scr = s.buf("scr")
        cand = sbt(k, st, "cand", [128, 256], F32); b_cand = s.buf("cand")
        topv = sbt(k, st, "topv", [128, 8, 16], F32); b_topv = s.buf("topv")
        pos = sbt(k, st, "pos", [128, 8, 16], U32); b_pos = s.buf("pos")
        pa = sbt(k, st, "pa", [128, 128], U32); b_pa = s.buf("pa")
        paf = sbt(k, st, "paf", [128, 2, 128], F32); b_paf = s.buf("paf")
        oh = sbt(k, st, "oh", [128, 128, 16], F32); b_oh = s.buf("oh")
        iab = sbt(k, st, "iab", [128, 2, 128], F32); b_iab = s.buf("iab")
        idx = sbt(k, st, "idx", [128, 128], I32); b_idx = s.buf("idx")
        wgt = sbt(k, st, "wgt", [128, 8, 16], F32); b_wgt = s.buf("wgt")
        rs = sbt(k, st, "rs", [128, 8], F32); b_rs = s.buf("rs")
        av = sbt(k, st, "av", [128, 128], F32); b_av = s.buf("av")
        coef = sbt(k, st, "coef", [128, 128], F32); b_coef = s.buf("coef")
        wit = 0; git = 0
        for g in range(k.NG):
            t0 = g * 512
            for ti in range(4):
                xi = ti % 2
                s.dma("sp", xt[xi][:], k.XR[t0 + ti * 128:t0 + (ti + 1) * 128, :], writes=[b_xt[xi]])
                norm_mod_transpose(k, xt[xi], b_xt[xi], hT, b_hT, ti * 128, A2, b_A2, sh2, b_sh2, tmp)
                s.op("dve", lambda: nc.vector.tensor_tensor(out=xs[:], in0=xs[:], in1=A2r[:], op=ALU.mult), reads=[b_A2r], writes=[b_xs])
                s.op("pool", lambda: nc.gpsimd.tensor_tensor(out=h2t[ti][:], in0=xs[:], in1=sh2r[:], op=ALU.add),
                     reads=[b_xs, b_sh2r], writes=[b_h2t[ti]])
            for n in range(4):
                wi = wit % 2; wit += 1
                s.dma("pool", wcb[wi][:], k.w_q[l][:, n * 512:(n + 1) * 512].rearrange("(c p) n -> p c n", p=128), writes=[b_w[wi]])
                for j in range(4):
                    bank = k.psrot % 8; k.psrot += 1
                    for c in range(16):
                        s.op("pe", lambda: nc.tensor.matmul(P[bank][:, :], lhsT=wcb[wi][:, c, j * 128:(j + 1) * 128], rhs=hT[:, c, :],
                                                            start=(c == 0), stop=(c == 15)), reads=[b_hT, b_w[wi]], writes=[PB[bank]])
                    s.op("act", lambda: nc.scalar.copy(out=qT[:, n * 4 + j, :], in_=P[bank][:, :]), reads=[PB[bank]], writes=[b_qT])
            for ti in range(4):
                r0 = t0 + ti * 128
                for jj in range(16):
                    s.op("pe", lambda: nc.tensor.matmul(P[jj // 4][:, (jj % 4) * 128:(jj % 4 + 1) * 128], lhsT=qT[:, jj, ti * 128:(ti + 1) * 128],
                                                        rhs=skT[:, jj, :], start=True, stop=True), reads=[b_qT, b_c], writes=[PB[jj // 4]])
                for jj in range(16):
                    S_ = P[jj // 4][:, (jj % 4) * 128:(jj % 4 + 1) * 128]
                    pb_ = PB[jj // 4]
                    s.op("dve", lambda: nc.vector.max(out=sv[:, jj, 0:8], in_=S_), reads=[pb_], writes=[b_sv])
                    s.op("dve", lambda: nc.vector.max_index(out=si[:, jj, 0:8], in_max=sv[:, jj, 0:8], in_values=S_), reads=[pb_, b_sv], writes=[b_si])
                    s.op("dve", lambda: nc.vector.match_replace(out=scr[:, 0:128], in_to_replace=sv[:, jj, 0:8], in_values=S_, imm_value=NEG),
                         reads=[pb_, b_sv], writes=[b_scr])
                    s.op("dve", lambda: nc.vector.max(out=sv[:, jj, 8:16], in_=scr[:, 0:128]), reads=[b_scr], writes=[b_sv])
                    s.op("dve", lambda: nc.vector.max_index(out=si[:, jj, 8:16], in_max=sv[:, jj, 8:16], in_values=scr[:, 0:128]),
                         reads=[b_scr, b_sv], writes=[b_si])
                s.op("dve", lambda: nc.vector.tensor_copy(out=sif[:], in_=si[:]), reads=[b_si], writes=[b_sif])
                for h in range(8):
                    s.op("dve", lambda: nc.vector.tensor_tensor(out=cand[:].rearrange("p (a b) -> p a b", b=16),
                                                                in0=sv[:, 2 * h, :, None].to_broadcast([128, 16, 16]),
                                                                in1=sv[:, 2 * h + 1, None, :].to_broadcast([128, 16, 16]), op=ALU.add),
                         reads=[b_sv], writes=[b_cand])
                    s.op("dve", lambda: nc.vector.max(out=topv[:, h, 0:8], in_=cand[:]), reads=[b_cand], writes=[b_topv])
                    s.op("dve", lambda: nc.vector.max_index(out=pos[:, h, 0:8], in_max=topv[:, h, 0:8], in_values=cand[:]),
                         reads=[b_cand, b_topv], writes=[b_pos])
                    s.op("dve", lambda: nc.vector.match_replace(out=scr[:], in_to_replace=topv[:, h, 0:8], in_values=cand[:], imm_value=NEG),
                         reads=[b_cand, b_topv], writes=[b_scr])
                    s.op("dve", lambda: nc.vector.max(out=topv[:, h, 8:16], in_=scr[:]), reads=[b_scr], writes=[b_topv])
                    s.op("dve", lambda: nc.vector.max_index(out=pos[:, h, 8:16], in_max=topv[:, h, 8:16], in_values=scr[:]),
                         reads=[b_scr, b_topv], writes=[b_pos])
                posf = pos[:].rearrange("p h k -> p (h k)")
                s.op("dve", lambda: nc.vector.tensor_scalar(pa[:], posf, 4, None, ALU.logical_shift_right), reads=[b_pos], writes=[b_pa])
                s.op("dve", lambda: nc.vector.tensor_copy(out=paf[:, 0, :], in_=pa[:]), reads=[b_pa], writes=[b_paf])
                s.op("dve", lambda: nc.vector.tensor_scalar(pa[:], posf, 15, None, ALU.bitwise_and), reads=[b_pos], writes=[b_pa])
                s.op("dve", lambda: nc.vector.tensor_copy(out=paf[:, 1, :], in_=pa[:]), reads=[b_pa], writes=[b_paf])
                sif4 = sif[:].rearrange("p (h c) a -> p h c a", c=2)
                for ab in range(2):
                    s.op("dve", lambda: nc.vector.tensor_tensor(out=oh[:], in0=paf[:, ab, :, None].to_broadcast([128, 128, 16]),
                                                                in1=iota[:, None, :].to_broadcast([128, 128, 16]), op=ALU.is_equal),
                         reads=[b_paf, b_c], writes=[b_oh])
                    s.op("dve", lambda: nc.vector.tensor_tensor(out=oh[:].rearrange("p (h k) a -> p h k a", k=16),
                                                                in0=oh[:].rearrange("p (h k) a -> p h k a", k=16),
                                                                in1=sif4[:, :, ab, None, :].to_broadcast([128, 8, 16, 16]), op=ALU.mult),
                         reads=[b_sif], writes=[b_oh])
                    s.op("dve", lambda: nc.vector.tensor_reduce(out=iab[:, ab, :], in_=oh[:], axis=AX.X, op=ALU.add), reads=[b_oh], writes=[b_iab])
                s.op("dve", lambda: nc.vector.scalar_tensor_tensor(out=iab[:, 0, :], in0=iab[:, 0, :], scalar=128.0, in1=iab[:, 1, :],
                                                                   op0=ALU.mult, op1=ALU.add), reads=[], writes=[b_iab])
                s.op("dve", lambda: nc.vector.tensor_copy(out=idx[:], in_=iab[:, 0, :]), reads=[b_iab], writes=[b_idx])
                s.op("dve", lambda: nc.vector.tensor_tensor(out=wgt[:], in0=topv[:], in1=topv[:, :, 0:1].to_broadcast([128, 8, 16]), op=ALU.subtract),
                     reads=[b_topv], writes=[b_wgt])
                s.op("act", lambda: nc.scalar.activation(out=wgt[:], in_=wgt[:], func=AF.Exp), reads=[], writes=[b_wgt])
                s.op("dve", lambda: nc.vector.tensor_reduce(out=rs[:], in_=wgt[:], axis=AX.X, op=ALU.add), reads=[b_wgt], writes=[b_rs])
                s.op("dve", lambda: nc.vector.reciprocal(rs[:], rs[:]), reads=[], writes=[b_rs])
                s.op("dve", lambda: nc.vector.tensor_tensor(out=wgt[:], in0=wgt[:], in1=rs[:, :, None].to_broadcast([128, 8, 16]), op=ALU.mult),
                     reads=[b_rs], writes=[b_wgt])
                for e in range(128):
                    gi = git % NGB; git += 1
                    s.gather(gbuf[gi][:], k.peer_u[l], idx[:, e:e + 1], reads=[b_idx], writes=[b_gbuf[gi]])
                    s.op("dve", lambda: nc.vector.scalar_tensor_tensor(out=tmp[0][:], in0=gbuf[gi][:], scalar=1.0, in1=h2t[ti][:],
                                                                       op0=ALU.mult, op1=ALU.mult, accum_out=av[:, e:e + 1]),
                         reads=[b_gbuf[gi], b_h2t[ti]], writes=[tmp[1], b_av])
                s.op("act", lambda: nc.scalar.activation(out=coef[:], in_=av[:], func=AF.Gelu), reads=[b_av], writes=[b_coef])
                s.op("dve", lambda: nc.vector.tensor_tensor(out=coef[:], in0=coef[:], in1=wgt[:].rearrange("p h k -> p (h k)"), op=ALU.mult),
                     reads=[b_wgt], writes=[b_coef])
                for e in range(128):
                    gi = git % NGB; git += 1
                    s.gather(gbuf[gi][:], k.peer_v[l], idx[:, e:e + 1], reads=[b_idx], writes=[b_gbuf[gi]])
                    if e == 0:
                        s.op("act", lambda: nc.scalar.activation(out=acc[:], in_=gbuf[gi][:], func=AF.Copy, scale=coef[:, 0:1]),
                             reads=[b_gbuf[gi], b_coef], writes=[b_acc])
                    elif e % 2 == 1:
                        s.op("dve", lambda: nc.vector.scalar_tensor_tensor(out=acc[:], in0=gbuf[gi][:], scalar=coef[:, e:e + 1], in1=acc[:],
                                                                           op0=ALU.mult, op1=ALU.add),
                             reads=[b_gbuf[gi], b_coef], writes=[b_acc])
                    else:
                        s.op("act", lambda: nc.scalar.activation(out=gbuf[gi][:], in_=gbuf[gi][:], func=AF.Copy, scale=coef[:, e:e + 1]),
                             reads=[b_coef], writes=[b_gbuf[gi]])
                        s.op("pool", lambda: nc.gpsimd.tensor_tensor(out=acc[:], in0=acc[:], in1=gbuf[gi][:], op=ALU.add),
                             reads=[b_gbuf[gi]], writes=[b_acc])
                xi = ti % 2
                s.dma("sp", xt[xi][:], k.XR[r0:r0 + 128, :], writes=[b_xt[xi]])
                s.op("dve", lambda: nc.vector.tensor_tensor(out=acc[:], in0=acc[:], in1=gate2[:], op=ALU.mult), reads=[b_gate2], writes=[b_acc])
                s.op("pool", lambda: nc.gpsimd.tensor_tensor(out=xt[xi][:], in0=xt[xi][:], in1=acc[:], op=ALU.add), reads=[b_acc], writes=[b_xt[xi]])
                if not last:
                    s.dma("sp", k.XR[r0:r0 + 128, :], xt[xi][:], reads=[b_xt[xi]])
                else:
                    junk, b_junk, ssq, b_ssq, rstd, b_rstd = tmp[0], tmp[1], tmp[2], tmp[3], tmp[4], tmp[5]
                    s.op("act", lambda: nc.scalar.activation(out=junk[:], in_=xt[xi][:], func=AF.Square, accum_out=ssq[:]),
                         reads=[b_xt[xi]], writes=[b_junk, b_ssq])
                    s.op("dve", lambda: nc.vector.tensor_scalar(rstd[:], ssq[:], 1.0 / D, EPS, ALU.mult, ALU.add), reads=[b_ssq], writes=[b_rstd])
                    s.op("act", lambda: nc.scalar.activation(out=rstd[:], in_=rstd[:], func=AF.Sqrt), reads=[], writes=[b_rstd])
                    s.op("dve", lambda: nc.vector.reciprocal(rstd[:], rstd[:]), reads=[], writes=[b_rstd])
                    s.op("dve", lambda: nc.vector.scalar_tensor_tensor(out=xt[xi][:], in0=xt[xi][:], scalar=rstd[:, 0:1], in1=g2r[:],
                                                                       op0=ALU.mult, op1=ALU.mult), reads=[b_rstd, b_g2r], writes=[b_xt[xi]])
                    s.dma("sp", k.Y[r0:r0 + 128, :], xt[xi][:], reads=[b_xt[xi]])
            s.barrier()
        s.barrier()


def host_shared(inp, L):
    hw = host_weights(inp, L)
    f = np.float32
    c = lambda a: np.ascontiguousarray(a, dtype=f)
    scw = inp["ssd_conv_w"][:L]
    hw["ssd_cw"] = c(scw.reshape(L, 2, 4, 12, 128).transpose(0, 1, 4, 2, 3).reshape(L, 2, 128, 48))
    hw["ssd_cb"] = c(inp["ssd_conv_b"][:L].reshape(L, 2, 12, 128).transpose(0, 1, 3, 2))
    hw["ssd_A_log"] = c(inp["ssd_A_log"][:L]); hw["ssd_dt_bias"] = c(inp["ssd_dt_bias"][:L]); hw["ssd_D"] = c(inp["ssd_D"][:L])
    hw["ssd_norm_g"] = c(inp["ssd_norm_g"][:L])
    hw["sink"] = c(inp["attn_sink"][:L])
    hw["conf_w"] = c(inp["conv_dw_w"][:L].reshape(L, 31, 8, 128).transpose(0, 3, 1, 2).reshape(L, 128, 248))
    hw["conf_b"] = c(inp["conv_dw_b"][:L].reshape(L, 8, 128).transpose(0, 2, 1))
    hw["conf_ln_g"] = c(inp["conv_ln_g"][:L]); hw["conf_ln_b"] = c(inp["conv_ln_b"][:L])
    hw["gate_b"] = c(inp["gate_b"][:L].reshape(L, 6144))
    for nm in ("w_attn_o", "w_conv_o", "w_ssd_o", "w_out"):
        hw[nm] = c(inp[nm][:L])
    hw["w_q"] = c(inp["peer_w_q"][:L])
    hw["norm2_g"] = c(inp["norm2_g"][:L]); hw["final_g"] = c(inp["final_g"])
    hw["skT"] = c(inp["peer_sub_keys"][:L].reshape(L, 16, 128, 128).transpose(0, 1, 3, 2))
    for i in range(L):
        hw[f"peer_u{i}"] = c(inp["peer_u"][i]); hw[f"peer_v{i}"] = c(inp["peer_v"][i])
    ii = np.arange(128)
    tri = np.zeros((3, 128, 128), f)
    tri[0] = (ii[:, None] <= ii[None, :]); tri[1] = (ii[:, None] >= ii[None, :]); tri[2] = 1.0
    hw["tri"] = tri
    hw["iota16"] = c(np.tile(np.arange(16, dtype=f)[None, :], (128, 1)))
    return hw


def host_core(sample, T, L, x, cond, ck=None, cv=None, h0=None):
    f = np.float32
    cos, sin = rope_tables(T, sample)
    ii = np.arange(128)
    masks = np.zeros((4, 128, 128), f)
    if sample:
        masks[0] = masks[1] = (ii[None, :] <= ii[:, None])
        masks[2] = masks[3] = (ii[:, None] <= ii[None, :])
    else:
        masks[1] = 1.0
        masks[2] = 1.0
    m = dict(x0=np.ascontiguousarray(x, dtype=f), cond=np.ascontiguousarray(cond.reshape(16, 128).T, dtype=f), cos=cos, sin=sin,
             masks=masks, f0=np.full((128, 1), 1.0 if sample else 0.0, f),
             ctxbias=np.full((128, 1), 0.0 if sample else -30000.0, f))
    if sample:
        m["ckT"] = np.ascontiguousarray(ck[:L].transpose(0, 3, 2, 1), dtype=f)
        m["cv"] = np.ascontiguousarray(cv[:L].reshape(L, 256, 256), dtype=f)
        m["h0"] = np.ascontiguousarray(h0[:L].reshape(L, 2, 1024, 128), dtype=f)
    else:
        m["ckT"] = np.zeros((L, 64, 4, 256), f); m["cv"] = np.zeros((L, 256, 256), f); m["h0"] = np.zeros((L, 2, 1024, 128), f)
    return m


T_CORE, N_LAYERS = 4096, 4


def kernel(**inp):
    inp = {k_: np.asarray(v) for k_, v in inp.items()}
    T, L = T_CORE, N_LAYERS
    nc = build(T, L)
    hw = host_shared(inp, L)
    in_maps = []
    for b in range(4):
        m = dict(hw)
        m.update(host_core(True, T, L, inp["x_sample"][b], inp["c"][b], inp["cache_k"][b], inp["cache_v"][b], inp["state_ssd"][b]))
        in_maps.append(m)
    for i in range(4):
        xp = inp["x_prompt"][8 * i:8 * i + 8].reshape(2048, 2048)
        m = dict(hw)
        m.update(host_core(False, T, L, np.concatenate([xp, xp], 0), inp["c_ctx"]))
        in_maps.append(m)
    res = run_bass_kernel_spmd(nc, in_maps, core_ids=list(range(8)))
    r = res.results
    y_sample = np.stack([r[b]["Y"] for b in range(4)], 0).astype(np.float32)
    y_prompt = np.concatenate([r[4 + i]["Y"][:2048].reshape(8, 256, 2048) for i in range(4)], 0).astype(np.float32)
    nk = np.zeros((32, L, 256, 4, 64), np.float32)
    nv = np.zeros((32, L, 256, 4, 64), np.float32)
    ns = np.zeros((32, L, 2, 16, 64, 128), np.float32)
    for i in range(4):
        kvo = r[4 + i]["KVO"]
        sso = r[4 + i]["SSDO"]
        for sq in range(8):
            nk[8 * i + sq] = kvo[:, sq * 256:(sq + 1) * 256, 0:256].reshape(L, 256, 4, 64)
            nv[8 * i + sq] = kvo[:, sq * 256:(sq + 1) * 256, 256:512].reshape(L, 256, 4, 64)
            ns[8 * i + sq] = sso[:, :, sq].reshape(L, 2, 16, 64, 128)
    return (y_prompt, y_sample, nk, nv, ns)
```

```python
import numpy as np
import concourse.bass as bass
import concourse.mybir as mybir
from concourse.bass_utils import run_bass_kernel_spmd
from contextlib import ExitStack

dt = mybir.dt
F32, BF16, F32R, I32, U32 = dt.float32, dt.bfloat16, dt.float32r, dt.int32, dt.uint32
AF = mybir.ActivationFunctionType
ALU = mybir.AluOpType
AX = mybir.AxisListType

SEM_LIMIT = 56000


class Buf:
    __slots__ = ("name", "writer", "readers", "dsem")

    def __init__(self, name):
        self.name = name
        self.writer = None
        self.readers = {}
        self.dsem = None


class SemObj:
    __slots__ = ("h", "count", "name")

    def __init__(self, h, name):
        self.h = h
        self.count = 0
        self.name = name


class Sched:
    ENGS = ("pe", "act", "dve", "pool", "sp")

    def __init__(self, nc, stack, n_dma_sems=18, n_eng_sems=0):
        self.nc = nc
        self.stack = stack
        self.eng = {"pe": nc.tensor, "act": nc.scalar, "dve": nc.vector, "pool": nc.gpsimd, "sp": nc.sync}
        self.free_sems = []
        self.nsem = 0
        self.esem = {e: self._new_sem() for e in self.ENGS}
        self.seen = {e: {} for e in self.ENGS}
        self.free_dsems = [self._new_sem() for _ in range(n_dma_sems)]
        self.spare = [self._new_sem() for _ in range(n_eng_sems * len(self.ENGS))]
        self.active_dsems = []
        self.bufs = []
        self.ninstr = 0
        self.nbar = 0

    def _new_sem(self):
        self.nsem += 1
        h = self.stack.enter_context(self.nc.semaphore(f"sem{self.nsem}"))
        return SemObj(h, f"sem{self.nsem}")

    def buf(self, name="b"):
        b = Buf(name)
        self.bufs.append(b)
        return b

    def bufs_n(self, n, name="b"):
        return [self.buf(f"{name}{i}") for i in range(n)]

    def _deps(self, reads, writes):
        deps = {}
        def add(ev):
            if ev is None:
                return
            s, v = ev
            if deps.get(s, 0) < v:
                deps[s] = v
        for b in reads:
            add(b.writer)
        for b in writes:
            add(b.writer)
            for s, v in b.readers.items():
                add((s, v))
        return deps

    def _wait(self, e, deps, skip_self=False):
        seen = self.seen[e]
        for s, v in deps.items():
            if skip_self and s is self.esem[e]:
                continue
            if seen.get(s, 0) < v:
                self.eng[e].wait_ge(s.h, v)
                seen[s] = v

    def _record(self, ev, reads, writes):
        s, v = ev
        for b in writes:
            b.writer = ev
            b.readers = {}
        for b in reads:
            if b.readers.get(s, 0) < v:
                b.readers[s] = v

    def op(self, e, fn, reads=(), writes=(), skip_self=None):
        if skip_self is None:
            skip_self = (e == "pe")
        deps = self._deps(reads, writes)
        self._wait(e, deps, skip_self=skip_self)
        ins = fn()
        s = self.esem[e]
        s.count += 1
        assert s.count < 64000, (e, s.count)
        ins.then_inc(s.h, 1)
        self._record((s, s.count), reads, writes)
        self.ninstr += 1
        return ins

    def dma(self, q, out, in_, reads=(), writes=(), sembuf=None, **kw):
        deps = self._deps(reads, writes)
        self._wait(q, deps)
        if sembuf is None:
            sembuf = (list(writes) + list(reads))[0]
        if sembuf.dsem is None:
            sembuf.dsem = self.free_dsems.pop()
            self.active_dsems.append(sembuf)
        s = sembuf.dsem
        ins = self.eng[q].dma_start(out=out, in_=in_, **kw)
        s.count += 16
        assert s.count < 64000, s.count
        ins.then_inc(s.h, 16)
        self._record((s, s.count), reads, writes)
        self.ninstr += 1
        return ins

    def gather(self, out, in_, idx_ap, reads=(), writes=(), sembuf=None, **kw):
        deps = self._deps(reads, writes)
        self._wait("pool", deps)
        if sembuf is None:
            sembuf = list(writes)[0]
        if sembuf.dsem is None:
            sembuf.dsem = self.free_dsems.pop()
            self.active_dsems.append(sembuf)
        s = sembuf.dsem
        ins = self.nc.gpsimd.indirect_dma_start(
            out=out, out_offset=None, in_=in_,
            in_offset=bass.IndirectOffsetOnAxis(ap=idx_ap, axis=0), **kw)
        s.count += 16
        assert s.count < 64000, s.count
        ins.then_inc(s.h, 16)
        self._record((s, s.count), reads, writes)
        self.ninstr += 1
        return ins

    def barrier(self):
        evs = {}
        for e in self.ENGS:
            s = self.esem[e]
            if s.count > 0:
                evs[s] = s.count
        for b in self.active_dsems:
            evs[b.dsem] = b.dsem.count
        for e in self.ENGS:
            for s, v in evs.items():
                if s is self.esem[e]:
                    continue
                if self.seen[e].get(s, 0) < v:
                    self.eng[e].wait_ge(s.h, v)
                    self.seen[e][s] = v
        for b in self.bufs:
            b.writer = None
            b.readers = {}
        for b in self.active_dsems:
            self.free_dsems.append(b.dsem)
            b.dsem = None
        self.active_dsems = []
        for e in self.ENGS:
            if self.esem[e].count > SEM_LIMIT:
                self.esem[e] = self._fresh()
        for i, s_ in enumerate(self.free_dsems):
            if s_.count > SEM_LIMIT:
                self.free_dsems[i] = self._fresh()

    def _fresh(self):
        if self.spare:
            return self.spare.pop()
        return self._new_sem()

    def _rotate(self, e):
        self.esem[e] = self._fresh()


D = 2048
NKC = 16
EPS = 1e-6
NTOKW = 7712
NFEATW = 6144
HPERM = np.concatenate([np.arange(16, 32), np.arange(0, 16), np.arange(48, 64), np.arange(32, 48)])


class K:
    pass


_uid = [0]


def sbt(k, st, name, shape, dtype=F32):
    _uid[0] += 1
    return st.enter_context(k.nc.sbuf_tensor(f"{name}_{_uid[0]}", list(shape), dtype))


def build(T, L, debug=False, phases=("inproj", "ssd", "attn", "conf", "merge", "peer")):
    nc = bass.Bass("TRN2", target_bir_lowering=False)
    k = K()
    k.nc, k.T, k.L, k.debug = nc, T, L, debug
    k.NT, k.NG = T // 128, T // 512
    k.phases = phases
    kind_s = "ExternalOutput" if debug else "Internal"

    def din(name, shape, dtype=F32):
        return nc.dram_tensor(name, list(shape), dtype, kind="ExternalInput").ap()

    def dscr(name, shape, dtype=F32):
        return nc.dram_tensor(name, list(shape), dtype, kind=kind_s).ap()

    def dout(name, shape, dtype=F32):
        return nc.dram_tensor(name, list(shape), dtype, kind="ExternalOutput").ap()

    k.x0 = din("x0", [T, D])
    k.cond = din("cond", [128, NKC])
    k.w_ada = din("w_ada", [L, D, 6 * D])
    k.b_ada = din("b_ada", [L, 1, 6 * D])
    k.g1c = din("g1c", [L, 128, NKC])
    k.g2c = din("g2c", [L, 128, NKC])
    k.w_tok = din("w_tok", [L, D, NTOKW])
    k.w_feat = din("w_feat", [L, D, NFEATW])
    k.cos = din("cos", [128, T])
    k.sin = din("sin", [128, T])
    k.ident = din("ident", [128, 128])

    NT = T // 128
    k.f0 = din("f0", [128, 1]); k.ctxbias = din("ctxbias", [128, 1])
    k.tri = din("tri", [3, 128, 128]); k.masks = din("masks", [4, 128, 128]); k.iota16 = din("iota16", [128, 16])
    k.ssd_cw = din("ssd_cw", [L, 2, 128, 48]); k.ssd_cb = din("ssd_cb", [L, 2, 128, 12])
    k.ssd_A_log = din("ssd_A_log", [L, 2, 16]); k.ssd_dt_bias = din("ssd_dt_bias", [L, 2, 16]); k.ssd_D = din("ssd_D", [L, 2, 16])
    k.ssd_norm_g = din("ssd_norm_g", [L, 1024]); k.h0 = din("h0", [L, 2, 1024, 128])
    k.sink = din("sink", [L, 16]); k.ckT = din("ckT", [L, 64, 4, 256]); k.cv = din("cv", [L, 256, 256])
    k.conf_w = din("conf_w", [L, 128, 248]); k.conf_b = din("conf_b", [L, 128, 8])
    k.conf_ln_g = din("conf_ln_g", [L, 1024]); k.conf_ln_b = din("conf_ln_b", [L, 1024])
    k.gate_b = din("gate_b", [L, 6144])
    k.w_attn_o = din("w_attn_o", [L, 1024, D]); k.w_conv_o = din("w_conv_o", [L, 1024, D]); k.w_ssd_o = din("w_ssd_o", [L, 1024, D])
    k.w_out = din("w_out", [L, D, D]); k.w_q = din("w_q", [L, D, D])
    k.norm2_g = din("norm2_g", [L, D]); k.final_g = din("final_g", [D])
    k.skT = din("skT", [L, 16, 128, 128])
    k.peer_u = [din(f"peer_u{i}", [16384, D]) for i in range(L)]; k.peer_v = [din(f"peer_v{i}", [16384, D]) for i in range(L)]

    k.MOD = dscr("MOD", [L, 1, 6 * D])
    k.QT = dscr("QT", [1024, T], BF16)
    k.KT = dscr("KT", [256, T], BF16)
    k.GLUT = dscr("GLUT", [1024, T], BF16)
    k.XBCT = dscr("XBCT", [1536, T], BF16)
    k.GATES = dscr("GATES", [T, 6144])
    k.Z = dscr("Z", [T, 1024])
    k.DT = dscr("DT", [T, 32])
    k.YB = dscr("YB", [T, 1024]); k.SY = dscr("SY", [T, 1024]); k.AO = dscr("AO", [T, 1024]); k.CO = dscr("CO", [T, 1024])
    k.KVO = dout("KVO", [L, T, 512])
    k.SSDO = dout("SSDO", [L, 2, NT // 2, 1024, 128])
    k.XR = dscr("XR", [T, D])
    k.Y = dout("Y", [T, D])

    with ExitStack() as st:
        k.s = Sched(nc, st)
        k.ps = [st.enter_context(nc.psum_tensor(f"ps{i}", [128, 512], F32)) for i in range(8)]
        k.psb = None
        phase_mod(k)
        for l in range(L):
            for ph in k.phases:
                if ph == "inproj": phase_inproj(k, l)
                elif ph == "ssd":
                    phase_ssd(k, l, 1); phase_ssd(k, l, 0)
                elif ph == "attn": phase_attn(k, l)
                elif ph == "conf": phase_conf(k, l)
                elif ph == "merge": phase_merge(k, l)
                elif ph == "peer": phase_peer(k, l)
        k.s.barrier()
        print("instructions:", k.s.ninstr, "sems:", k.s.nsem)
    return nc


def new_ps_bufs(k):
    k.psb = k.s.bufs_n(8, "ps")


def phase_mod(k):
    nc, s = k.nc, k.s
    with ExitStack() as st:
        new_ps_bufs(k)
        cond = sbt(k, st, "cond", [128, NKC], F32)
        sc = sbt(k, st, "sc", [128, NKC], F32)
        wb = [sbt(k, st, f"wada{i}", [128, NKC, 512], F32) for i in range(2)]
        brow = sbt(k, st, "brow", [1, 6 * D], F32)
        orow = sbt(k, st, "orow", [1, 6 * D], F32)
        b_cond, b_sc, b_brow, b_orow = s.buf("cond"), s.buf("sc"), s.buf("brow"), s.buf("orow")
        b_w = s.bufs_n(2, "wada")
        s.dma("sp", cond[:], k.cond[:, :], writes=[b_cond])
        s.op("act", lambda: nc.scalar.activation(out=sc[:], in_=cond[:], func=AF.Silu), reads=[b_cond], writes=[b_sc])
        it = 0
        for l in range(k.L):
            s.dma("sp", brow[:], k.b_ada[l], writes=[b_brow])
            for n in range(24):
                wi = it % 2
                s.dma("sp", wb[wi][:], k.w_ada[l][:, n * 512:(n + 1) * 512].rearrange("(c p) n -> p c n", p=128),
                      writes=[b_w[wi]])
                pi = it % 8
                for c in range(NKC):
                    s.op("pe", lambda c=c: nc.tensor.matmul(k.ps[pi][0:1, :], lhsT=sc[:, c:c + 1], rhs=wb[wi][:, c, :],
                                                            start=(c == 0), stop=(c == NKC - 1)),
                         reads=[b_sc, b_w[wi]], writes=[k.psb[pi]])
                s.op("dve", lambda: nc.vector.tensor_tensor(out=orow[0:1, n * 512:(n + 1) * 512], in0=k.ps[pi][0:1, :],
                                                            in1=brow[0:1, n * 512:(n + 1) * 512], op=ALU.add),
                     reads=[k.psb[pi], b_brow], writes=[b_orow])
                it += 1
            s.dma("sp", k.MOD[l], orow[:], reads=[b_orow])
        s.barrier()


def load_mod_cols(k, st, l, which):
    nc, s = k.nc, k.s
    base = 0 if which == 0 else 3
    gsrc = k.g1c if which == 0 else k.g2c
    sh = sbt(k, st, f"modsh{which}", [128, NKC], F32)
    scl = sbt(k, st, f"modsc{which}", [128, NKC], F32)
    gg = sbt(k, st, f"modg{which}", [128, NKC], F32)
    A = sbt(k, st, f"modA{which}", [128, NKC], F32)
    b_sh, b_scl, b_g, b_A = s.buf(), s.buf(), s.buf(), s.buf()
    mrow = k.MOD[l]
    with nc.allow_non_contiguous_dma(reason="tiny modulation relayout"):
        s.dma("sp", sh[:], mrow[0, base * D:(base + 1) * D].rearrange("(c p) -> p c", p=128), writes=[b_sh])
        s.dma("sp", scl[:], mrow[0, (base + 1) * D:(base + 2) * D].rearrange("(c p) -> p c", p=128), writes=[b_scl])
    s.dma("sp", gg[:], gsrc[l], writes=[b_g])
    s.op("dve", lambda: nc.vector.scalar_tensor_tensor(out=A[:], in0=scl[:], scalar=1.0, in1=gg[:],
                                                       op0=ALU.add, op1=ALU.mult),
         reads=[b_scl, b_g], writes=[b_A])
    return A, b_A, sh, b_sh


def norm_mod_transpose(k, xt, b_xt, hT, b_hT, col0, A, b_A, sh, b_sh, tmp):
    nc, s = k.nc, k.s
    junk, b_junk, ssq, b_ssq, rstd, b_rstd, xs, b_xs, ident, b_ident = tmp
    s.op("act", lambda: nc.scalar.activation(out=junk[:], in_=xt[:], func=AF.Square, accum_out=ssq[:]),
         reads=[b_xt], writes=[b_junk, b_ssq])
    s.op("dve", lambda: nc.vector.tensor_scalar(rstd[:], ssq[:], 1.0 / D, EPS, ALU.mult, ALU.add),
         reads=[b_ssq], writes=[b_rstd])
    s.op("act", lambda: nc.scalar.activation(out=rstd[:], in_=rstd[:], func=AF.Sqrt), reads=[b_rstd], writes=[b_rstd])
    s.op("dve", lambda: nc.vector.reciprocal(rstd[:], rstd[:]), reads=[b_rstd], writes=[b_rstd])
    s.op("act", lambda: nc.scalar.activation(out=xs[:], in_=xt[:], func=AF.Copy, scale=rstd[:, 0:1]),
         reads=[b_xt, b_rstd], writes=[b_xs])
    for c4 in range(4):
        pi = k.psrot % 8
        k.psrot += 1
        for j in range(4):
            c = c4 * 4 + j
            s.op("pe", lambda c=c, j=j: nc.tensor.transpose(k.ps[pi][:, j * 128:(j + 1) * 128], xs[:, c * 128:(c + 1) * 128],
                                                            ident[:]),
                 reads=[b_xs, b_ident], writes=[k.psb[pi]])
        for j in range(4):
            c = c4 * 4 + j
            s.op("act", lambda c=c, j=j: nc.scalar.activation(out=hT[:, c, col0:col0 + 128],
                                                              in_=k.ps[pi][:, j * 128:(j + 1) * 128],
                                                              func=AF.Identity, scale=A[:, c:c + 1], bias=sh[:, c:c + 1]),
                 reads=[k.psb[pi], b_A, b_sh], writes=[b_hT])


def alloc_norm_tmp(k, st):
    nc, s = k.nc, k.s
    junk = sbt(k, st, "junk", [128, D], F32)
    ssq = sbt(k, st, "ssq", [128, 1], F32)
    rstd = sbt(k, st, "rstd", [128, 1], F32)
    xs = sbt(k, st, "xs", [128, D], F32)
    ident = sbt(k, st, "ident", [128, 128], F32)
    b_ident = s.buf("ident")
    s.dma("sp", ident[:], k.ident[:, :], writes=[b_ident])
    return (junk, s.buf(), ssq, s.buf(), rstd, s.buf(), xs, s.buf(), ident, b_ident)


def phase_inproj(k, l):
    nc, s, T = k.nc, k.s, k.T
    xsrc = k.x0 if l == 0 else k.XR
    with ExitStack() as st:
        new_ps_bufs(k)
        k.psrot = 0
        A, b_A, sh, b_sh = load_mod_cols(k, st, l, 0)
        tmp = alloc_norm_tmp(k, st)
        xt = [sbt(k, st, f"xt{i}", [128, D], F32) for i in range(2)]
        b_xt = s.bufs_n(2, "xt")
        hT = sbt(k, st, "hT", [128, NKC, 512], BF16)
        b_hT = s.buf("hT")
        wc = [sbt(k, st, f"wc{i}", [128, NKC, 512], BF16) for i in range(2)]
        b_wc = s.bufs_n(2, "wc")
        stg = [sbt(k, st, f"stg{i}", [128, 512], F32) for i in range(4)]
        b_stg = s.bufs_n(4, "stg")
        stgb = [sbt(k, st, f"stgb{i}", [128, 512], BF16) for i in range(4)]
        b_stgb = s.bufs_n(4, "stgb")
        tA = sbt(k, st, "tA", [128, 512], F32)
        b_tA = s.buf("tA")
        cs = sbt(k, st, "cs", [128, 2, 512], F32)
        b_cs = s.buf("cs")
        wit = 0
        sti = 0
        for g in range(k.NG):
            t0 = g * 512
            s.dma("sp", cs[:, 0, :], k.cos[:, t0:t0 + 512], writes=[b_cs])
            s.dma("sp", cs[:, 1, :], k.sin[:, t0:t0 + 512], writes=[b_cs])
            for ti in range(4):
                xi = (g * 4 + ti) % 2
                s.dma("sp", xt[xi][:], xsrc[t0 + ti * 128:t0 + (ti + 1) * 128, :], writes=[b_xt[xi]])
                norm_mod_transpose(k, xt[xi], b_xt[xi], hT, b_hT, ti * 128, A, b_A, sh, b_sh, tmp)
            ncols = [512] * 15 + [32]
            c0 = 0
            for n, w in enumerate(ncols):
                wi = wit % 2
                wit += 1
                s.dma("pool", wc[wi][:, :, 0:w], k.w_tok[l][:, c0:c0 + w].rearrange("(c p) n -> p c n", p=128),
                      writes=[b_wc[wi]])
                pbase = (k.psrot % 2) * 4
                k.psrot += 1
                for ti in range(4):
                    pi = pbase + ti
                    for c in range(NKC):
                        s.op("pe", lambda c=c, ti=ti, pi=pi: nc.tensor.matmul(
                            k.ps[pi][:, 0:w], lhsT=hT[:, c, ti * 128:(ti + 1) * 128],
                            rhs=wc[wi][:, c, 0:w], start=(c == 0), stop=(c == NKC - 1)),
                            reads=[b_hT, b_wc[wi]], writes=[k.psb[pi]])
                    si = sti % 4
                    sti += 1
                    eng = "act" if ti % 2 == 0 else "dve"
                    if eng == "act":
                        s.op("act", lambda si=si, pi=pi: nc.scalar.copy(out=stg[si][:, 0:w], in_=k.ps[pi][:, 0:w]),
                             reads=[k.psb[pi]], writes=[b_stg[si]])
                    else:
                        s.op("dve", lambda si=si, pi=pi: nc.vector.tensor_copy(out=stg[si][:, 0:w], in_=k.ps[pi][:, 0:w]),
                             reads=[k.psb[pi]], writes=[b_stg[si]])
                    r0 = t0 + ti * 128
                    if n < 12:
                        dst = k.GATES[r0:r0 + 128, n * 512:(n + 1) * 512]
                    elif n < 14:
                        dst = k.Z[r0:r0 + 128, (n - 12) * 512:(n - 11) * 512]
                    elif n == 14:
                        dst = k.KVO[l][r0:r0 + 128, :]
                    else:
                        dst = k.DT[r0:r0 + 128, :]
                    s.dma("pool", dst, stg[si][:, 0:w], reads=[b_stg[si]])
                c0 += w
            for n in range(12):
                wi = wit % 2
                wit += 1
                s.dma("pool", wc[wi][:], k.w_feat[l][:, n * 512:(n + 1) * 512].rearrange("(c p) n -> p c n", p=128),
                      writes=[b_wc[wi]])
                pbase = (k.psrot % 2) * 4
                k.psrot += 1
                for j in range(4):
                    pi = pbase + j
                    for c in range(NKC):
                        s.op("pe", lambda c=c, j=j, pi=pi: nc.tensor.matmul(
                            k.ps[pi][:, :], lhsT=wc[wi][:, c, j * 128:(j + 1) * 128],
                            rhs=hT[:, c, :], start=(c == 0), stop=(c == NKC - 1)),
                            reads=[b_hT, b_wc[wi]], writes=[k.psb[pi]])
                if n < 5:
                    for pr in range(2):
                        pa, pb_ = pbase + 2 * pr, pbase + 2 * pr + 1
                        si = sti % 4
                        sti += 1
                        s.op("dve", lambda: nc.vector.tensor_tensor(out=tA[:], in0=k.ps[pb_][:, :], in1=cs[:, 1, :], op=ALU.mult),
                             reads=[k.psb[pb_], b_cs], writes=[b_tA])
                        s.op("dve", lambda: nc.vector.tensor_tensor(out=stg[si][:], in0=k.ps[pa][:, :], in1=cs[:, 0, :], op=ALU.mult),
                             reads=[k.psb[pa], b_cs], writes=[b_stg[si]])
                        s.op("pool", lambda: nc.gpsimd.tensor_tensor(out=stgb[si][:], in0=stg[si][:], in1=tA[:], op=ALU.add),
                             reads=[b_tA, b_stg[si]], writes=[b_stgb[si]])
                        if n < 4:
                            ch = n * 2 + pr
                            dst = k.QT[ch * 128:(ch + 1) * 128, t0:t0 + 512]
                        else:
                            dst = k.KT[pr * 128:(pr + 1) * 128, t0:t0 + 512]
                        s.dma("pool", dst, stgb[si][:], reads=[b_stgb[si]])
                elif n < 9:
                    for pr in range(2):
                        pa, pb_ = pbase + 2 * pr, pbase + 2 * pr + 1
                        si = sti % 4
                        sti += 1
                        s.op("act", lambda: nc.scalar.activation(out=tA[:], in_=k.ps[pb_][:, :], func=AF.Sigmoid),
                             reads=[k.psb[pb_]], writes=[b_tA])
                        s.op("dve", lambda: nc.vector.tensor_tensor(out=stgb[si][:], in0=k.ps[pa][:, :], in1=tA[:], op=ALU.mult),
                             reads=[k.psb[pa], b_tA], writes=[b_stgb[si]])
                        ch = (n - 5) * 2 + pr
                        s.dma("pool", k.GLUT[ch * 128:(ch + 1) * 128, t0:t0 + 512], stgb[si][:], reads=[b_stgb[si]])
                else:
                    for j in range(4):
                        pi = pbase + j
                        si = sti % 4
                        sti += 1
                        if j % 2 == 0:
                            s.op("act", lambda: nc.scalar.copy(out=stgb[si][:], in_=k.ps[pi][:, :]),
                                 reads=[k.psb[pi]], writes=[b_stgb[si]])
                        else:
                            s.op("dve", lambda: nc.vector.tensor_copy(out=stgb[si][:], in_=k.ps[pi][:, :]),
                                 reads=[k.psb[pi]], writes=[b_stgb[si]])
                        ch = (n - 9) * 4 + j
                        s.dma("pool", k.XBCT[ch * 128:(ch + 1) * 128, t0:t0 + 512], stgb[si][:], reads=[b_stgb[si]])
            if g % 2 == 1:
                s.barrier()
        s.barrier()


def host_weights(inp, L):
    w_in = inp["w_in"][:L]
    q, kk, v, conv, z, xbc, dtc, gates = np.split(w_in, np.cumsum([1024, 256, 256, 2048, 1024, 1536, 32])[:], axis=-1)
    w_tok = np.concatenate([gates, z, kk, v, dtc], axis=-1)
    def sw(w, nh):
        w4 = w.reshape(w.shape[0], w.shape[1], nh, 64)
        return w4[..., HPERM].reshape(w.shape)
    qs, ks = sw(q, 16), sw(kk, 4)
    ca, cg = conv[..., :1024], conv[..., 1024:]
    cols = []
    for c in range(8):
        cols += [q[..., c * 128:(c + 1) * 128], qs[..., c * 128:(c + 1) * 128]]
    for c in range(2):
        cols += [kk[..., c * 128:(c + 1) * 128], ks[..., c * 128:(c + 1) * 128]]
    for c in range(8):
        cols += [ca[..., c * 128:(c + 1) * 128], cg[..., c * 128:(c + 1) * 128]]
    cols.append(xbc)
    w_feat = np.concatenate(cols, axis=-1)
    assert w_tok.shape[-1] == NTOKW and w_feat.shape[-1] == NFEATW
    def colsl(g):
        return np.ascontiguousarray(g.reshape(g.shape[0], NKC, 128).transpose(0, 2, 1))
    return dict(
        w_ada=np.ascontiguousarray(inp["w_ada"][:L]), b_ada=np.ascontiguousarray(inp["b_ada"][:L, None, :]),
        g1c=colsl(inp["norm1_g"][:L]), g2c=colsl(inp["norm2_g"][:L]),
        w_tok=np.ascontiguousarray(w_tok), w_feat=np.ascontiguousarray(w_feat),
        ident=np.eye(128, dtype=np.float32),
    )


def rope_tables(T, sample):
    cos = np.ones((128, T), np.float32)
    sin = np.zeros((128, T), np.float32)
    if sample:
        pos = np.arange(T)
        row = (pos // 64).astype(np.float32)
        col = (pos % 64).astype(np.float32)
        freqs = (10000.0 ** (-np.arange(16, dtype=np.float32) / 16)).astype(np.float32)
        ar = row[None, :] * freqs[:, None]
        ac = col[None, :] * freqs[:, None]
        c64 = np.concatenate([np.cos(ar), np.cos(ar), np.cos(ac), np.cos(ac)], 0)
        s64 = np.concatenate([-np.sin(ar), np.sin(ar), -np.sin(ac), np.sin(ac)], 0)
        cos = np.concatenate([c64, c64], 0).astype(np.float32)
        sin = np.concatenate([s64, s64], 0).astype(np.float32)
    return np.ascontiguousarray(cos), np.ascontiguousarray(sin)


def bc_load(k, st, name, src_ap, n, b=None):
    t = sbt(k, st, name, [128, n], F32)
    bb = k.s.buf(name)
    k.s.dma("sp", t[:], src_ap.partition_broadcast(128), writes=[bb])
    return t, bb


def phase_ssd(k, l, d):
    nc, s, T, NT = k.nc, k.s, k.T, k.NT
    with ExitStack() as st:
        new_ps_bufs(k)
        P, PB = k.ps, k.psb
        ident = sbt(k, st, "ident", [128, 128], F32); b_c = s.buf("consts")
        s.dma("sp", ident[:], k.ident[:, :], writes=[b_c])
        identb = sbt(k, st, "identb", [128, 128], BF16)
        s.op("dve", lambda: nc.vector.tensor_copy(out=identb[:], in_=ident[:]), reads=[b_c], writes=[b_c])
        U = sbt(k, st, "U", [128, 128], F32)
        negUT = sbt(k, st, "negUT", [128, 128], F32)
        ones = sbt(k, st, "ones", [128, 128], F32)
        s.dma("sp", U[:], k.tri[d], writes=[b_c])
        s.dma("sp", ones[:], k.tri[2], writes=[b_c])
        s.op("pool", lambda: nc.gpsimd.tensor_scalar(negUT[:], U[:], -1.0, None, ALU.mult), reads=[b_c], writes=[b_c])
        f0 = sbt(k, st, "f0", [128, 1], F32)
        s.dma("sp", f0[:], k.f0[:, :], writes=[b_c])
        cw = sbt(k, st, "cw", [128, 48], F32)
        cb = sbt(k, st, "cb", [128, 12], F32)
        s.dma("sp", cw[:], k.ssd_cw[l, d], writes=[b_c])
        s.dma("sp", cb[:], k.ssd_cb[l, d], writes=[b_c])
        diag = sbt(k, st, "diag", [128, 48, 128], BF16)
        s.op("dve", lambda: nc.vector.tensor_tensor(out=diag[:], in0=ident[:, None, :].to_broadcast([128, 48, 128]),
                                                    in1=cw[:, :, None].to_broadcast([128, 48, 128]), op=ALU.mult),
             reads=[b_c], writes=[b_c])
        Abc, _ = bc_load(k, st, "Abc", k.ssd_A_log[l, d], 16, b_c)
        dtb, b_dtb = bc_load(k, st, "dtb", k.ssd_dt_bias[l, d], 16)
        Dbc, b_Dbc = bc_load(k, st, "Dbc", k.ssd_D[l, d], 16)
        b_A = s.buf("A")
        s.op("act", lambda: nc.scalar.activation(out=Abc[:], in_=Abc[:], func=AF.Exp), reads=[_], writes=[b_A])
        s.op("pool", lambda: nc.gpsimd.tensor_scalar(Abc[:], Abc[:], -1.0, None, ALU.mult), reads=[b_A], writes=[b_A])
        if d == 0:
            gn, b_gn = bc_load(k, st, "gn", k.ssd_norm_g[l], 1024)
        HT = sbt(k, st, "HT", [128, 1024], F32); b_HT = s.buf("HT")
        HTb = sbt(k, st, "HTb", [128, 1024], BF16); b_HTb = s.buf("HTb")
        h0t = sbt(k, st, "h0t", [128, 8, 128], F32); b_h0t = s.buf("h0t")
        s.dma("sp", h0t[:], k.h0[l, d].rearrange("(c p) n -> p c n", p=128), writes=[b_h0t])
        for half in range(2):
            for j in range(4):
                s.op("pe", lambda: nc.tensor.transpose(P[half][:, j * 128:(j + 1) * 128], h0t[:, half * 4 + j, :], ident[:]),
                     reads=[b_h0t, b_c], writes=[PB[half]])
            s.op("act", lambda: nc.scalar.copy(out=HT[:, half * 512:(half + 1) * 512], in_=P[half][:, :]),
                 reads=[PB[half]], writes=[b_HT])
        u = sbt(k, st, "u", [128, 12, 134], BF16); b_u = s.buf("u")
        ua = sbt(k, st, "ua", [128, 12, 128], F32); b_ua = s.buf("ua")
        uab = sbt(k, st, "uab", [128, 4, 128], BF16); b_uab = s.buf("uab")
        xs = sbt(k, st, "xs", [128, 1024], F32); b_xs = s.buf("xs")
        Btok = sbt(k, st, "Btok", [128, 256], BF16); b_Btok = s.buf("Btok")
        dtr = sbt(k, st, "dtr", [128, 16], F32); b_dtr = s.buf("dtr")
        dtt = sbt(k, st, "dtt", [128, 16], F32); b_dtt = s.buf("dtt")
        a = sbt(k, st, "a", [128, 16], F32); b_a = s.buf("a")
        D2 = sbt(k, st, "D2", [128, 16, 128], F32); b_D2 = s.buf("D2")
        D3 = sbt(k, st, "D3", [128, 16, 128], F32); b_D3 = s.buf("D3")
        E = sbt(k, st, "E", [128, 16, 128], F32); b_E = s.buf("E")
        cbm = sbt(k, st, "cbm", [128, 2, 128], F32); b_cbm = s.buf("cbm")
        MT = sbt(k, st, "MT", [128, 16, 128], BF16); b_MT = s.buf("MT")
        xg = sbt(k, st, "xg", [128, 1024], BF16); b_xg = s.buf("xg")
        xge = sbt(k, st, "xge", [128, 1024], BF16); b_xge = s.buf("xge")
        sm = sbt(k, st, "sm", [128, 4, 16], F32); b_sm = s.buf("sm")
        t1 = sbt(k, st, "t1", [128, 1024], F32); b_t1 = s.buf("t1")
        t2 = sbt(k, st, "t2", [128, 1024], F32); b_t2 = s.buf("t2")
        yb = sbt(k, st, "yb", [128, 1024], F32); b_yb = s.buf("yb")
        zt = sbt(k, st, "zt", [128, 1024], F32); b_zt = s.buf("zt")
        so = sbt(k, st, "so", [128, 8, 128], F32); b_so = s.buf("so")
        syo = sbt(k, st, "syo", [128, 1024], F32); b_syo = s.buf("syo")
        ssq = sbt(k, st, "ssq2", [128, 1], F32); b_ssq = s.buf("ssq2")

        order = list(range(NT)) if d == 0 else list(range(NT - 1, -1, -1))
        for c in order:
            t0 = c * 128
            seg_first = (c % 2 == 0) if d == 0 else (c % 2 == 1)
            seg_last = not seg_first
            lo, hi = t0 - 3, t0 + 131
            lo_c, hi_c = max(lo, 0), min(hi, T)
            s.dma("sp", u[:, :, lo_c - lo:134 - (hi - hi_c)], k.XBCT[:, lo_c:hi_c].rearrange("(c p) t -> p c t", p=128),
                  writes=[b_u])
            if d == 0:
                if c == 0:
                    s.op("pool", lambda: nc.gpsimd.memset(u[:, :, 0:3], 0.0), writes=[b_u])
                elif seg_first:
                    s.op("pool", lambda: nc.gpsimd.tensor_scalar(u[:, :, 0:3], u[:, :, 0:3], f0[:, 0:1], None, ALU.mult),
                         reads=[b_c], writes=[b_u])
            else:
                if c == NT - 1:
                    s.op("pool", lambda: nc.gpsimd.memset(u[:, :, 131:134], 0.0), writes=[b_u])
                elif seg_first:
                    s.op("pool", lambda: nc.gpsimd.tensor_scalar(u[:, :, 131:134], u[:, :, 131:134], f0[:, 0:1], None, ALU.mult),
                         reads=[b_c], writes=[b_u])
            for ch in range(12):
                bank = ch // 4
                for j in range(4):
                    off = j if d == 0 else 6 - j
                    s.op("pe", lambda: nc.tensor.matmul(P[bank][:, (ch % 4) * 128:(ch % 4 + 1) * 128], lhsT=diag[:, j * 12 + ch, :],
                                                        rhs=u[:, ch, off:off + 128], start=(j == 0), stop=(j == 3)),
                         reads=[b_u, b_c], writes=[PB[bank]])
            for ch in range(12):
                bank = ch // 4
                s.op("act", lambda: nc.scalar.activation(out=ua[:, ch, :], in_=P[bank][:, (ch % 4) * 128:(ch % 4 + 1) * 128],
                                                         func=AF.Silu, bias=cb[:, ch:ch + 1]),
                     reads=[PB[bank], b_c], writes=[b_ua])
            s.op("pool", lambda: nc.gpsimd.tensor_copy(out=uab[:], in_=ua[:, 8:12, :]), reads=[b_ua], writes=[b_uab])
            for ch in range(8):
                bank = 5 + ch // 4
                s.op("pe", lambda: nc.tensor.transpose(P[bank][:, (ch % 4) * 128:(ch % 4 + 1) * 128], ua[:, ch, :], ident[:]),
                     reads=[b_ua, b_c], writes=[PB[bank]])
            for g in range(2):
                s.op("pe", lambda: nc.tensor.transpose(P[7][:, g * 128:(g + 1) * 128], ua[:, 8 + g, :], ident[:]),
                     reads=[b_ua, b_c], writes=[PB[7]])
            s.op("act", lambda: nc.scalar.copy(out=xs[:, 0:512], in_=P[5][:, :]), reads=[PB[5]], writes=[b_xs])
            s.op("act", lambda: nc.scalar.copy(out=xs[:, 512:1024], in_=P[6][:, :]), reads=[PB[6]], writes=[b_xs])
            s.op("dve", lambda: nc.vector.tensor_copy(out=Btok[:], in_=P[7][:, 0:256]), reads=[PB[7]], writes=[b_Btok])
            s.dma("sp", dtr[:], k.DT[t0:t0 + 128, d * 16:(d + 1) * 16], writes=[b_dtr])
            s.op("dve", lambda: nc.vector.tensor_tensor(out=dtt[:], in0=dtr[:], in1=dtb[:], op=ALU.add),
                 reads=[b_dtr, b_dtb], writes=[b_dtt])
            s.op("act", lambda: nc.scalar.activation(out=dtt[:], in_=dtt[:], func=AF.Exp), reads=[b_dtt], writes=[b_dtt])
            s.op("act", lambda: nc.scalar.activation(out=dtt[:], in_=dtt[:], func=AF.Ln, bias=1.0), reads=[b_dtt], writes=[b_dtt])
            s.op("dve", lambda: nc.vector.tensor_tensor(out=a[:], in0=dtt[:], in1=Abc[:], op=ALU.mult),
                 reads=[b_dtt, b_A], writes=[b_a])
            s.op("dve", lambda: nc.vector.tensor_tensor(out=D2[:], in0=a[:, :, None].to_broadcast([128, 16, 128]),
                                                        in1=U[:, None, :].to_broadcast([128, 16, 128]), op=ALU.mult),
                 reads=[b_a, b_c], writes=[b_D2])
            s.op("pool", lambda: nc.gpsimd.tensor_copy(out=D3[:], in_=a[:, :, None].to_broadcast([128, 16, 128])),
                 reads=[b_a], writes=[b_D3])
            for q4 in range(4):
                s.op("pe", lambda: nc.tensor.matmul(P[q4][:, :], lhsT=ones[:], rhs=D2[:, q4 * 4:(q4 + 1) * 4, :],
                                                    start=True, stop=False), reads=[b_D2, b_c], writes=[PB[q4]])
                s.op("pe", lambda: nc.tensor.matmul(P[q4][:, :], lhsT=negUT[:], rhs=D3[:, q4 * 4:(q4 + 1) * 4, :],
                                                    start=False, stop=True), reads=[b_D3, b_c], writes=[PB[q4]])
            s.op("pe", lambda: nc.tensor.matmul(P[4][:, 256:272], lhsT=U[:], rhs=a[:], start=True, stop=True),
                 reads=[b_a, b_c], writes=[PB[4]])
            s.op("pe", lambda: nc.tensor.matmul(P[4][:, 272:288], lhsT=ones[:], rhs=a[:], start=True, stop=True),
                 reads=[b_a, b_c], writes=[PB[4]])
            for g in range(2):
                s.op("pe", lambda: nc.tensor.matmul(P[4][:, g * 128:(g + 1) * 128], lhsT=uab[:, g, :], rhs=uab[:, 2 + g, :],
                                                    start=True, stop=True), reads=[b_uab], writes=[PB[4]])
            for q4 in range(4):
                s.op("dve", lambda: nc.vector.tensor_scalar(E[:, q4 * 4:(q4 + 1) * 4, :], P[q4][:, :], 0.0, None, ALU.min),
                     reads=[PB[q4]], writes=[b_E])
            s.op("act", lambda: nc.scalar.activation(out=E[:], in_=E[:], func=AF.Exp), reads=[b_E], writes=[b_E])
            s.op("dve", lambda: nc.vector.tensor_tensor(out=cbm[:], in0=P[4][:, 0:256], in1=U[:, None, :].to_broadcast([128, 2, 128]),
                                                        op=ALU.mult), reads=[PB[4], b_c], writes=[b_cbm])
            for g in range(2):
                s.op("pool", lambda: nc.gpsimd.tensor_tensor(out=MT[:, g * 8:(g + 1) * 8, :], in0=E[:, g * 8:(g + 1) * 8, :],
                                                             in1=cbm[:, g:g + 1, :].to_broadcast([128, 8, 128]), op=ALU.mult),
                     reads=[b_E, b_cbm], writes=[b_MT])
            s.op("dve", lambda: nc.vector.tensor_copy(out=sm[:, 0, :], in_=P[4][:, 256:272]), reads=[PB[4]], writes=[b_sm])
            s.op("act", lambda: nc.scalar.activation(out=sm[:, 1, :], in_=P[4][:, 256:272], func=AF.Exp), reads=[PB[4]], writes=[b_sm])
            s.op("dve", lambda: nc.vector.tensor_tensor(out=sm[:, 2, :], in0=P[4][:, 272:288], in1=sm[:, 0, :], op=ALU.subtract),
                 reads=[PB[4], b_sm], writes=[b_sm])
            s.op("act", lambda: nc.scalar.activation(out=sm[:, 2, :], in_=sm[:, 2, :], func=AF.Exp), reads=[b_sm], writes=[b_sm])
            s.op("dve", lambda: nc.vector.tensor_tensor(out=sm[:, 2, :], in0=sm[:, 2, :], in1=dtt[:], op=ALU.mult),
                 reads=[b_sm, b_dtt], writes=[b_sm])
            s.op("act", lambda: nc.scalar.activation(out=sm[:, 3, :], in_=P[4][:, 272:288], func=AF.Exp), reads=[PB[4]], writes=[b_sm])
            xs3 = xs[:].rearrange("p (h e) -> p h e", e=64)
            s.op("dve", lambda: nc.vector.tensor_tensor(out=xg[:].rearrange("p (h e) -> p h e", e=64), in0=xs3,
                                                        in1=dtt[:, :, None].to_broadcast([128, 16, 64]), op=ALU.mult),
                 reads=[b_xs, b_dtt], writes=[b_xg])
            s.op("pool", lambda: nc.gpsimd.tensor_tensor(out=xge[:].rearrange("p (h e) -> p h e", e=64), in0=xs3,
                                                         in1=sm[:, 2, :, None].to_broadcast([128, 16, 64]), op=ALU.mult),
                 reads=[b_xs, b_sm], writes=[b_xge])
            if seg_first and c != order[0]:
                s.op("dve", lambda: nc.vector.tensor_scalar(HT[:], HT[:], f0[:, 0:1], None, ALU.mult), reads=[b_c], writes=[b_HT])
            s.op("act", lambda: nc.scalar.copy(out=HTb[:], in_=HT[:]), reads=[b_HT], writes=[b_HTb])
            for h in range(16):
                bank = h // 8
                s.op("pe", lambda: nc.tensor.matmul(P[bank][:, (h % 8) * 64:(h % 8 + 1) * 64], lhsT=MT[:, h, :],
                                                    rhs=xg[:, h * 64:(h + 1) * 64], start=True, stop=True),
                     reads=[b_MT, b_xg], writes=[PB[bank]])
            for g in range(2):
                s.op("pe", lambda: nc.tensor.matmul(P[2 + g][:, :], lhsT=uab[:, 2 + g, :], rhs=HTb[:, g * 512:(g + 1) * 512],
                                                    start=True, stop=True), reads=[b_uab, b_HTb], writes=[PB[2 + g]])
            for g in range(2):
                s.op("pe", lambda: nc.tensor.matmul(P[5 + g][:, :], lhsT=Btok[:, g * 128:(g + 1) * 128], rhs=xge[:, g * 512:(g + 1) * 512],
                                                    start=True, stop=True), reads=[b_Btok, b_xge], writes=[PB[5 + g]])
            for g in range(2):
                sl = slice(g * 512, (g + 1) * 512)
                s.op("dve", lambda: nc.vector.tensor_tensor(out=t1[:, sl].rearrange("p (h e) -> p h e", e=64),
                                                            in0=P[2 + g][:, :].rearrange("p (h e) -> p h e", e=64),
                                                            in1=sm[:, 1, g * 8:(g + 1) * 8, None].to_broadcast([128, 8, 64]), op=ALU.mult),
                     reads=[PB[2 + g], b_sm], writes=[b_t1])
            s.op("pool", lambda: nc.gpsimd.tensor_tensor(out=t2[:].rearrange("p (h e) -> p h e", e=64), in0=xs3,
                                                         in1=Dbc[:, :, None].to_broadcast([128, 16, 64]), op=ALU.mult),
                 reads=[b_xs, b_Dbc], writes=[b_t2])
            s.op("pool", lambda: nc.gpsimd.tensor_tensor(out=t1[:], in0=t1[:], in1=t2[:], op=ALU.add), reads=[b_t1, b_t2], writes=[b_t1])
            for g in range(2):
                sl = slice(g * 512, (g + 1) * 512)
                s.op("dve", lambda: nc.vector.tensor_tensor(out=t1[:, sl], in0=P[g][:, :], in1=t1[:, sl], op=ALU.add),
                     reads=[PB[g], b_t1], writes=[b_t1])
            s.op("dve", lambda: nc.vector.tensor_tensor(out=HT[:].rearrange("p (h e) -> p h e", e=64),
                                                        in0=HT[:].rearrange("p (h e) -> p h e", e=64),
                                                        in1=sm[:, 3, :, None].to_broadcast([128, 16, 64]), op=ALU.mult),
                 reads=[b_sm, b_HTb], writes=[b_HT])
            for g in range(2):
                sl = slice(g * 512, (g + 1) * 512)
                s.op("dve", lambda: nc.vector.tensor_tensor(out=HT[:, sl], in0=P[5 + g][:, :], in1=HT[:, sl], op=ALU.add),
                     reads=[PB[5 + g]], writes=[b_HT])
            if d == 1:
                s.dma("pool", k.YB[t0:t0 + 128, :], t1[:], reads=[b_t1])
            else:
                s.dma("sp", yb[:], k.YB[t0:t0 + 128, :], writes=[b_yb])
                s.dma("sp", zt[:], k.Z[t0:t0 + 128, :], writes=[b_zt])
                s.op("pool", lambda: nc.gpsimd.tensor_tensor(out=t1[:], in0=t1[:], in1=yb[:], op=ALU.add), reads=[b_yb, b_t1], writes=[b_t1])
                s.op("act", lambda: nc.scalar.activation(out=zt[:], in_=zt[:], func=AF.Silu), reads=[b_zt], writes=[b_zt])
                s.op("dve", lambda: nc.vector.tensor_tensor(out=t1[:], in0=t1[:], in1=zt[:], op=ALU.mult), reads=[b_t1, b_zt], writes=[b_t1])
                s.op("act", lambda: nc.scalar.activation(out=t2[:], in_=t1[:], func=AF.Square, accum_out=ssq[:]),
                     reads=[b_t1], writes=[b_t2, b_ssq])
                s.op("dve", lambda: nc.vector.tensor_scalar(ssq[:], ssq[:], 1.0 / 1024, EPS, ALU.mult, ALU.add), reads=[b_ssq], writes=[b_ssq])
                s.op("act", lambda: nc.scalar.activation(out=ssq[:], in_=ssq[:], func=AF.Sqrt), reads=[b_ssq], writes=[b_ssq])
                s.op("dve", lambda: nc.vector.reciprocal(ssq[:], ssq[:]), reads=[b_ssq], writes=[b_ssq])
                s.op("dve", lambda: nc.vector.scalar_tensor_tensor(out=syo[:], in0=t1[:], scalar=ssq[:, 0:1], in1=gn[:],
                                                                   op0=ALU.mult, op1=ALU.mult),
                     reads=[b_t1, b_ssq, b_gn], writes=[b_syo])
                s.dma("pool", k.SY[t0:t0 + 128, :], syo[:], reads=[b_syo])
            if seg_last:
                for half in range(2):
                    for j in range(4):
                        s.op("pe", lambda: nc.tensor.transpose(P[half][:, j * 128:(j + 1) * 128],
                                                               HT[:, (half * 4 + j) * 128:(half * 4 + j + 1) * 128], ident[:]),
                             reads=[b_HT, b_c], writes=[PB[half]])
                    s.op("act", lambda: nc.scalar.copy(out=so[:, half * 4:(half + 1) * 4, :],
                                                       in_=P[half][:, :].rearrange("p (c n) -> p c n", n=128)),
                         reads=[PB[half]], writes=[b_so])
                s.dma("pool", k.SSDO[l, d, c // 2].rearrange("(c p) n -> p c n", p=128), so[:], reads=[b_so])
        s.barrier()


def phase_attn(k, l):
    nc, s, T, NT = k.nc, k.s, k.T, k.NT
    SCALE = 64 ** -0.5
    with ExitStack() as st:
        new_ps_bufs(k)
        P, PB = k.ps, k.psb
        b_c = s.buf("consts")
        masks = sbt(k, st, "masks", [128, 4, 128], F32)
        s.dma("sp", masks[:], k.masks.rearrange("m k q -> k m q"), writes=[b_c])
        ctxb = sbt(k, st, "ctxb", [128, 1], F32)
        s.dma("sp", ctxb[:], k.ctxbias[:, :], writes=[b_c])
        esink, b_es = bc_load(k, st, "esink", k.sink[l], 16)
        s.op("act", lambda: nc.scalar.activation(out=esink[:], in_=esink[:], func=AF.Exp), reads=[b_es], writes=[b_es])
        ckT = sbt(k, st, "ckT", [64, 4, 256], BF16)
        s.dma("pool", ckT[:], k.ckT[l], writes=[b_c])
        cv = sbt(k, st, "cv", [128, 2, 4, 65], BF16)
        s.op("dve", lambda: nc.vector.memset(cv[:], 1.0), writes=[b_c])
        for b2 in range(2):
            s.dma("pool", cv[:, b2, :, 0:64], k.cv[l][b2 * 128:(b2 + 1) * 128, :].rearrange("p (h e) -> p h e", e=64), writes=[b_c])
        NB = 2
        qt = [sbt(k, st, f"qt{i}", [64, 16, 128], BF16) for i in range(NB)]; b_qt = s.bufs_n(NB, "qt")
        kt = [sbt(k, st, f"kt{i}", [64, 4, 384], BF16) for i in range(NB)]; b_kt = s.bufs_n(NB, "kt")
        vt = [sbt(k, st, f"vt{i}", [128, 3, 4, 65], BF16) for i in range(NB)]; b_vt = s.bufs_n(NB, "vt")
        for i in range(NB):
            s.op("dve", lambda: nc.vector.memset(vt[i][:], 1.0), writes=[b_vt[i]])
        pt = [sbt(k, st, f"pt{i}", [128, 5, 512], BF16) for i in range(2)]; b_pt = s.bufs_n(2, "pt")
        ao = [sbt(k, st, f"ao{i}", [128, 1024], F32) for i in range(2)]; b_ao = s.bufs_n(2, "ao")
        rden = sbt(k, st, "rden", [128, 4], F32); b_rden = s.buf("rden")
        it = 0
        for q in range(NT):
            t0 = q * 128
            bi = q % NB
            blks = []
            if q > 0:
                blks.append(0)
            blks.append(1)
            if q < NT - 1:
                blks.append(2)
            s.dma("sp", qt[bi][:], k.QT[:, t0:t0 + 128].rearrange("(h d) t -> d h t", d=64), writes=[b_qt[bi]])
            lo, hi = max(t0 - 128, 0), min(t0 + 256, T)
            s.dma("sp", kt[bi][:, :, lo - (t0 - 128):hi - (t0 - 128)], k.KT[:, lo:hi].rearrange("(h d) t -> d h t", d=64),
                  writes=[b_kt[bi]])
            for b in blks:
                r0 = t0 + (b - 1) * 128
                s.dma("pool", vt[bi][:, b, :, 0:64], k.KVO[l][r0:r0 + 128, 256:512].rearrange("p (h e) -> p h e", e=64),
                      writes=[b_vt[bi]])
            par = q % 2
            for hk in range(4):
                pi = it % 2
                it += 1
                allb = blks + [3, 4]
                for n, b in enumerate(allb):
                    bank = (it * 5 + n) % 6
                    if b < 3:
                        lhsT = kt[bi][:, hk, b * 128:(b + 1) * 128]
                    else:
                        lhsT = ckT[:, hk, (b - 3) * 128:(b - 2) * 128]
                    s.op("pe", lambda: nc.tensor.matmul(P[bank][:, :], lhsT=lhsT, rhs=qt[bi][:, hk * 4:(hk + 1) * 4, :],
                                                        start=True, stop=True),
                         reads=[b_kt[bi], b_qt[bi], b_c], writes=[PB[bank]])
                    if b < 3:
                        s.op("act", lambda: nc.scalar.activation(out=pt[pi][:, b, :], in_=P[bank][:, :], func=AF.Exp, scale=SCALE),
                             reads=[PB[bank]], writes=[b_pt[pi]])
                        if b != 1:
                            m = (0 if b == 0 else 2) + par
                            s.op("pool", lambda: nc.gpsimd.tensor_tensor(
                                out=pt[pi][:, b, :].rearrange("p (g q) -> p g q", g=4),
                                in0=pt[pi][:, b, :].rearrange("p (g q) -> p g q", g=4),
                                in1=masks[:, m:m + 1, :].to_broadcast([128, 4, 128]), op=ALU.mult),
                                reads=[b_c], writes=[b_pt[pi]])
                    else:
                        s.op("act", lambda: nc.scalar.activation(out=pt[pi][:, b, :], in_=P[bank][:, :], func=AF.Exp, scale=SCALE,
                                                                 bias=ctxb[:, 0:1]),
                             reads=[PB[bank], b_c], writes=[b_pt[pi]])
                ob = 6 + (it % 2)
                for g in range(4):
                    for n, b in enumerate(allb):
                        rhs = vt[bi][:, b, hk, :] if b < 3 else cv[:, b - 3, hk, :]
                        s.op("pe", lambda: nc.tensor.matmul(P[ob][:, g * 65:(g + 1) * 65], lhsT=pt[pi][:, b, g * 128:(g + 1) * 128],
                                                            rhs=rhs, start=(n == 0), stop=(n == len(allb) - 1)),
                             reads=[b_pt[pi], b_vt[bi], b_c], writes=[PB[ob]])
                o3 = P[ob][:, 0:260].rearrange("p (g e) -> p g e", e=65)
                s.op("dve", lambda: nc.vector.tensor_tensor(out=rden[:], in0=o3[:, :, 64], in1=esink[:, hk * 4:(hk + 1) * 4], op=ALU.add),
                     reads=[PB[ob], b_es], writes=[b_rden])
                s.op("dve", lambda: nc.vector.reciprocal(rden[:], rden[:]), reads=[b_rden], writes=[b_rden])
                ai = q % 2
                s.op("dve", lambda: nc.vector.tensor_tensor(
                    out=ao[ai][:, hk * 256:(hk + 1) * 256].rearrange("p (g e) -> p g e", e=64), in0=o3[:, :, 0:64],
                    in1=rden[:, :, None].to_broadcast([128, 4, 64]), op=ALU.mult),
                    reads=[PB[ob], b_rden], writes=[b_ao[ai]])
            s.dma("pool", k.AO[t0:t0 + 128, :], ao[q % 2][:], reads=[b_ao[q % 2]])
        s.barrier()


def phase_conf(k, l):
    nc, s, T = k.nc, k.s, k.T
    NU = T // 256
    with ExitStack() as st:
        new_ps_bufs(k)
        P, PB = k.ps, k.psb
        b_c = s.buf("consts")
        ident = sbt(k, st, "ident", [128, 128], F32)
        s.dma("sp", ident[:], k.ident[:, :], writes=[b_c])
        f0 = sbt(k, st, "f0", [128, 1], F32)
        s.dma("sp", f0[:], k.f0[:, :], writes=[b_c])
        cw = sbt(k, st, "cw", [128, 248], F32)
        cb = sbt(k, st, "cb", [128, 8], F32)
        s.dma("sp", cw[:], k.conf_w[l], writes=[b_c])
        s.dma("sp", cb[:], k.conf_b[l], writes=[b_c])
        diag = sbt(k, st, "diag", [128, 248, 128], BF16)
        for h2 in range(2):
            s.op("dve", lambda: nc.vector.tensor_tensor(out=diag[:, h2 * 124:(h2 + 1) * 124, :],
                                                        in0=ident[:, None, :].to_broadcast([128, 124, 128]),
                                                        in1=cw[:, h2 * 124:(h2 + 1) * 124, None].to_broadcast([128, 124, 128]), op=ALU.mult),
                 reads=[b_c], writes=[b_c])
        lg, b_lg = bc_load(k, st, "lng", k.conf_ln_g[l], 1024)
        lb, b_lb = bc_load(k, st, "lnb", k.conf_ln_b[l], 1024)
        hin = [sbt(k, st, f"hin{i}", [128, 8, 286], BF16) for i in range(2)]; b_hin = s.bufs_n(2, "hin")
        cvo = sbt(k, st, "cvo", [128, 8, 256], F32); b_cvo = s.buf("cvo")
        st4 = sbt(k, st, "st4", [128, 4], F32); b_st4 = s.buf("st4")
        junk = sbt(k, st, "junk", [128, 1024], F32); b_junk = s.buf("junk")
        y = sbt(k, st, "y", [128, 1024], F32); b_y = s.buf("y")
        yo = [sbt(k, st, f"yo{i}", [128, 1024], F32) for i in range(2)]; b_yo = s.bufs_n(2, "yo")
        for un in range(NU):
            t0 = un * 256
            hi_ = un % 2
            lo, hi = t0 - 15, t0 + 271
            lo_c, hi_c = max(lo, 0), min(hi, T)
            s.dma("sp", hin[hi_][:, :, lo_c - lo:286 - (hi - hi_c)], k.GLUT[:, lo_c:hi_c].rearrange("(c p) t -> p c t", p=128),
                  writes=[b_hin[hi_]])
            if un == 0:
                s.op("pool", lambda: nc.gpsimd.memset(hin[hi_][:, :, 0:15], 0.0), writes=[b_hin[hi_]])
            else:
                s.op("pool", lambda: nc.gpsimd.tensor_scalar(hin[hi_][:, :, 0:15], hin[hi_][:, :, 0:15], f0[:, 0:1], None, ALU.mult),
                     reads=[b_c], writes=[b_hin[hi_]])
            if un == NU - 1:
                s.op("pool", lambda: nc.gpsimd.memset(hin[hi_][:, :, 271:286], 0.0), writes=[b_hin[hi_]])
            else:
                s.op("pool", lambda: nc.gpsimd.tensor_scalar(hin[hi_][:, :, 271:286], hin[hi_][:, :, 271:286], f0[:, 0:1], None, ALU.mult),
                     reads=[b_c], writes=[b_hin[hi_]])
            for ch in range(8):
                bank = ch % 4
                for j in range(31):
                    s.op("pe", lambda: nc.tensor.matmul(P[bank][:, 0:256], lhsT=diag[:, j * 8 + ch, :], rhs=hin[hi_][:, ch, j:j + 256],
                                                        start=(j == 0), stop=(j == 30)),
                         reads=[b_hin[hi_], b_c], writes=[PB[bank]])
                s.op("act", lambda: nc.scalar.activation(out=cvo[:, ch, :], in_=P[bank][:, 0:256], func=AF.Identity, bias=cb[:, ch:ch + 1]),
                     reads=[PB[bank], b_c], writes=[b_cvo])
            for tt in range(2):
                for ch in range(8):
                    bank = 4 + ch // 4
                    s.op("pe", lambda: nc.tensor.transpose(P[bank][:, (ch % 4) * 128:(ch % 4 + 1) * 128],
                                                           cvo[:, ch, tt * 128:(tt + 1) * 128], ident[:]),
                         reads=[b_cvo, b_c], writes=[PB[bank]])
                for hf in range(2):
                    s.op("act", lambda: nc.scalar.activation(out=y[:, hf * 512:(hf + 1) * 512], in_=P[4 + hf][:, :], func=AF.Copy,
                                                             accum_out=st4[:, hf:hf + 1]),
                         reads=[PB[4 + hf]], writes=[b_y, b_st4])
                s.op("dve", lambda: nc.vector.tensor_tensor(out=st4[:, 0:1], in0=st4[:, 0:1], in1=st4[:, 1:2], op=ALU.add),
                     reads=[b_st4], writes=[b_st4])
                s.op("dve", lambda: nc.vector.tensor_scalar(st4[:, 0:1], st4[:, 0:1], 1.0 / 1024, None, ALU.mult), reads=[b_st4], writes=[b_st4])
                s.op("dve", lambda: nc.vector.tensor_scalar(y[:], y[:], st4[:, 0:1], None, ALU.subtract), reads=[b_st4], writes=[b_y])
                s.op("act", lambda: nc.scalar.activation(out=junk[:], in_=y[:], func=AF.Square, accum_out=st4[:, 2:3]),
                     reads=[b_y], writes=[b_junk, b_st4])
                s.op("dve", lambda: nc.vector.tensor_scalar(st4[:, 2:3], st4[:, 2:3], 1.0 / 1024, EPS, ALU.mult, ALU.add),
                     reads=[b_st4], writes=[b_st4])
                s.op("act", lambda: nc.scalar.activation(out=st4[:, 2:3], in_=st4[:, 2:3], func=AF.Sqrt), reads=[b_st4], writes=[b_st4])
                s.op("dve", lambda: nc.vector.reciprocal(st4[:, 2:3], st4[:, 2:3]), reads=[b_st4], writes=[b_st4])
                s.op("dve", lambda: nc.vector.scalar_tensor_tensor(out=y[:], in0=y[:], scalar=st4[:, 2:3], in1=lg[:],
                                                                   op0=ALU.mult, op1=ALU.mult), reads=[b_st4, b_lg], writes=[b_y])
                s.op("pool", lambda: nc.gpsimd.tensor_tensor(out=y[:], in0=y[:], in1=lb[:], op=ALU.add), reads=[b_lb], writes=[b_y])
                oi = (un * 2 + tt) % 2
                s.op("act", lambda: nc.scalar.activation(out=yo[oi][:], in_=y[:], func=AF.Silu), reads=[b_y], writes=[b_yo[oi]])
                s.dma("pool", k.CO[t0 + tt * 128:t0 + (tt + 1) * 128, :], yo[oi][:], reads=[b_yo[oi]])
        s.barrier()


def phase_merge(k, l):
    nc, s, T = k.nc, k.s, k.T
    xsrc = k.x0 if l == 0 else k.XR
    with ExitStack() as st:
        new_ps_bufs(k)
        P, PB = k.ps, k.psb
        b_c = s.buf("consts")
        ident = sbt(k, st, "ident", [128, 128], F32)
        s.dma("sp", ident[:], k.ident[:, :], writes=[b_c])
        gb, b_gb = bc_load(k, st, "gb", k.gate_b[l], 6144)
        g1, b_g1 = bc_load(k, st, "g1", k.MOD[l][0, 2 * D:3 * D], D)
        XT = sbt(k, st, "XT", [128, 24, 512], BF16); b_XT = s.buf("XT")
        xin = [sbt(k, st, f"xin{i}", [128, 1024], F32) for i in range(2)]; b_xin = s.bufs_n(2, "xin")
        wcb = [sbt(k, st, f"wcb{i}", [128, 16, 512], BF16) for i in range(2)]; b_w = s.bufs_n(2, "wcb")
        mg = [sbt(k, st, f"mg{i}", [128, D], F32) for i in range(4)]; b_mg = s.bufs_n(4, "mg")
        xt = [sbt(k, st, f"xt{i}", [128, D], F32) for i in range(4)]; b_xt = s.bufs_n(4, "xt")
        gt = [sbt(k, st, f"gt{i}", [128, 512], F32) for i in range(2)]; b_gt = s.bufs_n(2, "gt")
        tm = [sbt(k, st, f"tm{i}", [128, 512], F32) for i in range(2)]; b_tm = s.bufs_n(2, "tm")
        srcs = [k.AO, k.CO, k.SY]
        wsrc = [k.w_attn_o, k.w_conv_o, k.w_ssd_o]
        wit = 0; xit = 0; git = 0; prot = 0
        for g in range(k.NG):
            t0 = g * 512
            for ti in range(4):
                s.dma("sp", xt[ti][:], xsrc[t0 + ti * 128:t0 + (ti + 1) * 128, :], writes=[b_xt[ti]])
            for b in range(3):
                for ti in range(4):
                    xi = xit % 2; xit += 1
                    s.dma("sp", xin[xi][:], srcs[b][t0 + ti * 128:t0 + (ti + 1) * 128, :], writes=[b_xin[xi]])
                    for half in range(2):
                        bank = prot % 8; prot += 1
                        for j in range(4):
                            c = half * 4 + j
                            s.op("pe", lambda: nc.tensor.transpose(P[bank][:, j * 128:(j + 1) * 128], xin[xi][:, c * 128:(c + 1) * 128], ident[:]),
                                 reads=[b_xin[xi], b_c], writes=[PB[bank]])
                        s.op("act", lambda: nc.scalar.copy(out=XT[:, b * 8 + half * 4:b * 8 + half * 4 + 4, ti * 128:(ti + 1) * 128],
                                                           in_=P[bank][:, :].rearrange("p (c t) -> p c t", t=128)),
                             reads=[PB[bank]], writes=[b_XT])
            for n in range(4):
                for b in range(3):
                    wi = wit % 2; wit += 1
                    s.dma("pool", wcb[wi][:, 0:8, :], wsrc[b][l][:, n * 512:(n + 1) * 512].rearrange("(c p) n -> p c n", p=128),
                          writes=[b_w[wi]])
                    for ti in range(4):
                        bank = prot % 8; prot += 1
                        for c in range(8):
                            s.op("pe", lambda: nc.tensor.matmul(P[bank][:, :], lhsT=XT[:, b * 8 + c, ti * 128:(ti + 1) * 128],
                                                                rhs=wcb[wi][:, c, :], start=(c == 0), stop=(c == 7)),
                                 reads=[b_XT, b_w[wi]], writes=[PB[bank]])
                        gi = git % 2; git += 1
                        col = b * 2048 + n * 512
                        s.dma("sp", gt[gi][:], k.GATES[t0 + ti * 128:t0 + (ti + 1) * 128, col:col + 512], writes=[b_gt[gi]])
                        s.op("pool", lambda: nc.gpsimd.tensor_tensor(out=gt[gi][:], in0=gt[gi][:], in1=gb[:, col:col + 512], op=ALU.add),
                             reads=[b_gb], writes=[b_gt[gi]])
                        s.op("act", lambda: nc.scalar.activation(out=gt[gi][:], in_=gt[gi][:], func=AF.Sigmoid), reads=[], writes=[b_gt[gi]])
                        msl = mg[ti][:, n * 512:(n + 1) * 512]
                        if b == 0:
                            s.op("dve", lambda: nc.vector.tensor_tensor(out=msl, in0=P[bank][:, :], in1=gt[gi][:], op=ALU.mult),
                                 reads=[PB[bank], b_gt[gi]], writes=[b_mg[ti]])
                        else:
                            s.op("dve", lambda: nc.vector.tensor_tensor(out=tm[gi][:], in0=P[bank][:, :], in1=gt[gi][:], op=ALU.mult),
                                 reads=[PB[bank], b_gt[gi]], writes=[b_tm[gi]])
                            s.op("pool", lambda: nc.gpsimd.tensor_tensor(out=msl, in0=msl, in1=tm[gi][:], op=ALU.add),
                                 reads=[b_tm[gi]], writes=[b_mg[ti]])
            for ti in range(4):
                for c4 in range(4):
                    bank = prot % 8; prot += 1
                    for j in range(4):
                        c = c4 * 4 + j
                        s.op("pe", lambda: nc.tensor.transpose(P[bank][:, j * 128:(j + 1) * 128], mg[ti][:, c * 128:(c + 1) * 128], ident[:]),
                             reads=[b_mg[ti], b_c], writes=[PB[bank]])
                    s.op("act", lambda: nc.scalar.copy(out=XT[:, c4 * 4:c4 * 4 + 4, ti * 128:(ti + 1) * 128],
                                                       in_=P[bank][:, :].rearrange("p (c t) -> p c t", t=128)),
                         reads=[PB[bank]], writes=[b_XT])
            for n in range(4):
                wi = wit % 2; wit += 1
                s.dma("pool", wcb[wi][:], k.w_out[l][:, n * 512:(n + 1) * 512].rearrange("(c p) n -> p c n", p=128), writes=[b_w[wi]])
                for ti in range(4):
                    bank = prot % 8; prot += 1
                    for c in range(16):
                        s.op("pe", lambda: nc.tensor.matmul(P[bank][:, :], lhsT=XT[:, c, ti * 128:(ti + 1) * 128], rhs=wcb[wi][:, c, :],
                                                            start=(c == 0), stop=(c == 15)),
                             reads=[b_XT, b_w[wi]], writes=[PB[bank]])
                    gi = git % 2; git += 1
                    sl = slice(n * 512, (n + 1) * 512)
                    s.op("dve", lambda: nc.vector.tensor_tensor(out=tm[gi][:], in0=P[bank][:, :], in1=g1[:, sl], op=ALU.mult),
                         reads=[PB[bank], b_g1], writes=[b_tm[gi]])
                    s.op("pool", lambda: nc.gpsimd.tensor_tensor(out=xt[ti][:, sl], in0=xt[ti][:, sl], in1=tm[gi][:], op=ALU.add),
                         reads=[b_tm[gi]], writes=[b_xt[ti]])
            for ti in range(4):
                s.dma("sp", k.XR[t0 + ti * 128:t0 + (ti + 1) * 128, :], xt[ti][:], reads=[b_xt[ti]])
        s.barrier()


def phase_peer(k, l):
    nc, s, T = k.nc, k.s, k.T
    last = (l == k.L - 1)
    NEG = -1.0e30
    with ExitStack() as st:
        new_ps_bufs(k)
        k.psrot = 0
        P, PB = k.ps, k.psb
        A2, b_A2, sh2, b_sh2 = load_mod_cols(k, st, l, 1)
        tmp = alloc_norm_tmp(k, st)
        xs, b_xs = tmp[6], tmp[7]
        b_c = s.buf("consts")
        A2r, b_A2r = bc_load(k, st, "A2r", k.MOD[l][0, 4 * D:5 * D], D)
        g2r, b_g2r = bc_load(k, st, "g2r", k.norm2_g[l], D)
        s.op("dve", lambda: nc.vector.scalar_tensor_tensor(out=A2r[:], in0=A2r[:], scalar=1.0, in1=g2r[:], op0=ALU.add, op1=ALU.mult),
             reads=[b_g2r], writes=[b_A2r])
        sh2r, b_sh2r = bc_load(k, st, "sh2r", k.MOD[l][0, 3 * D:4 * D], D)
        gate2, b_gate2 = bc_load(k, st, "gate2", k.MOD[l][0, 5 * D:6 * D], D)
        if last:
            s.dma("sp", g2r[:], k.final_g.partition_broadcast(128), reads=[b_A2r], writes=[b_g2r])
        skT = sbt(k, st, "skT", [128, 16, 128], BF16)
        s.dma("pool", skT[:], k.skT[l].rearrange("j d n -> d j n"), writes=[b_c])
        iota = sbt(k, st, "iota", [128, 16], F32)
        s.dma("sp", iota[:], k.iota16[:, :], writes=[b_c])
        xt = [sbt(k, st, f"xt{i}", [128, D], F32) for i in range(2)]; b_xt = s.bufs_n(2, "xt")
        h2t = [sbt(k, st, f"h2t{i}", [128, D], BF16) for i in range(4)]; b_h2t = s.bufs_n(4, "h2t")
        hT = sbt(k, st, "hT", [128, NKC, 512], BF16); b_hT = s.buf("hT")
        qT = sbt(k, st, "qT", [128, 16, 512], BF16); b_qT = s.buf("qT")
        wcb = [sbt(k, st, f"wcb{i}", [128, 16, 512], BF16) for i in range(2)]; b_w = s.bufs_n(2, "wcb")
        NGB = 4
        gbuf = [sbt(k, st, f"gbuf{i}", [128, D], F32) for i in range(NGB)]; b_gbuf = s.bufs_n(NGB, "gbuf")
        acc = sbt(k, st, "acc", [128, D], F32); b_acc = s.buf("acc")
        sv = sbt(k, st, "sv", [128, 16, 16], F32); b_sv = s.buf("sv")
        si = sbt(k, st, "si", [128, 16, 16], U32); b_si = s.buf("si")
        sif = sbt(k, st, "sif", [128, 16, 16], F32); b_sif = s.buf("sif")
        scr = sbt(k, st, "scr", [128, 256], F32); b_scr = s.buf("scr")
        cand = sbt(k, st, "cand", [128, 256], F32); b_cand = s.buf("cand")
        topv = sbt(k, st, "topv", [128, 8, 16], F32); b_topv = s.buf("topv")
        pos = sbt(k, st, "pos", [128, 8, 16], U32); b_pos = s.buf("pos")
        pa = sbt(k, st, "pa", [128, 128], U32); b_pa = s.buf("pa")
        paf = sbt(k, st, "paf", [128, 2, 128], F32); b_paf = s.buf("paf")
        oh = sbt(k, st, "oh", [128, 128, 16], F32); b_oh = s.buf("oh")
        iab = sbt(k, st, "iab", [128, 2, 128], F32); b_iab = s.buf("iab")
        idx = sbt(k, st, "idx", [128, 128], I32); b_idx = s.buf("idx")
        wgt = sbt(k, st, "wgt", [128, 8, 16], F32); b_wgt = s.buf("wgt")
        rs = sbt(k, st, "rs", [128, 8], F32); b_rs = s.buf("rs")
        av = sbt(k, st, "av", [128, 128], F32); b_av = s.buf("av")
        coef = sbt(k, st, "coef", [128, 128], F32); b_coef = s.buf("coef")
        wit = 0; git = 0
        for g in range(k.NG):
            t0 = g * 512
            for ti in range(4):
                xi = ti % 2
                s.dma("sp", xt[xi][:], k.XR[t0 + ti * 128:t0 + (ti + 1) * 128, :], writes=[b_xt[xi]])
                norm_mod_transpose(k, xt[xi], b_xt[xi], hT, b_hT, ti * 128, A2, b_A2, sh2, b_sh2, tmp)
                s.op("dve", lambda: nc.vector.tensor_tensor(out=xs[:], in0=xs[:], in1=A2r[:], op=ALU.mult), reads=[b_A2r], writes=[b_xs])
                s.op("pool", lambda: nc.gpsimd.tensor_tensor(out=h2t[ti][:], in0=xs[:], in1=sh2r[:], op=ALU.add),
                     reads=[b_xs, b_sh2r], writes=[b_h2t[ti]])
            for n in range(4):
                wi = wit % 2; wit += 1
                s.dma("pool", wcb[wi][:], k.w_q[l][:, n * 512:(n + 1) * 512].rearrange("(c p) n -> p c n", p=128), writes=[b_w[wi]])
                for j in range(4):
                    bank = k.psrot % 8; k.psrot += 1
                    for c in range(16):
                        s.op("pe", lambda: nc.tensor.matmul(P[bank][:, :], lhsT=wcb[wi][:, c, j * 128:(j + 1) * 128], rhs=hT[:, c, :],
                                                            start=(c == 0), stop=(c == 15)), reads=[b_hT, b_w[wi]], writes=[PB[bank]])
                    s.op("act", lambda: nc.scalar.copy(out=qT[:, n * 4 + j, :], in_=P[bank][:, :]), reads=[PB[bank]], writes=[b_qT])
            for ti in range(4):
                r0 = t0 + ti * 128
                for jj in range(16):
                    s.op("pe", lambda: nc.tensor.matmul(P[jj // 4][:, (jj % 4) * 128:(jj % 4 + 1) * 128], lhsT=qT[:, jj, ti * 128:(ti + 1) * 128],
                                                        rhs=skT[:, jj, :], start=True, stop=True), reads=[b_qT, b_c], writes=[PB[jj // 4]])
                for jj in range(16):
                    S_ = P[jj // 4][:, (jj % 4) * 128:(jj % 4 + 1) * 128]
                    pb_ = PB[jj // 4]
                    s.op("dve", lambda: nc.vector.max(out=sv[:, jj, 0:8], in_=S_), reads=[pb_], writes=[b_sv])
                    s.op("dve", lambda: nc.vector.max_index(out=si[:, jj, 0:8], in_max=sv[:, jj, 0:8], in_values=S_), reads=[pb_, b_sv], writes=[b_si])
                    s.op("dve", lambda: nc.vector.match_replace(out=scr[:, 0:128], in_to_replace=sv[:, jj, 0:8], in_values=S_, imm_value=NEG),
                         reads=[pb_, b_sv], writes=[b_scr])
                    s.op("dve", lambda: nc.vector.max(out=sv[:, jj, 8:16], in_=scr[:, 0:128]), reads=[b_scr], writes=[b_sv])
                    s.op("dve", lambda: nc.vector.max_index(out=si[:, jj, 8:16], in_max=sv[:, jj, 8:16], in_values=scr[:, 0:128]),
                         reads=[b_scr, b_sv], writes=[b_si])
                s.op("dve", lambda: nc.vector.tensor_copy(out=sif[:], in_=si[:]), reads=[b_si], writes=[b_sif])
                for h in range(8):
                    s.op("dve", lambda: nc.vector.tensor_tensor(out=cand[:].rearrange("p (a b) -> p a b", b=16),
                                                                in0=sv[:, 2 * h, :, None].to_broadcast([128, 16, 16]),
                                                                in1=sv[:, 2 * h + 1, None, :].to_broadcast([128, 16, 16]), op=ALU.add),
                         reads=[b_sv], writes=[b_cand])
                    s.op("dve", lambda: nc.vector.max(out=topv[:, h, 0:8], in_=cand[:]), reads=[b_cand], writes=[b_topv])
                    s.op("dve", lambda: nc.vector.max_index(out=pos[:, h, 0:8], in_max=topv[:, h, 0:8], in_values=cand[:]),
                         reads=[b_cand, b_topv], writes=[b_pos])
                    s.op("dve", lambda: nc.vector.match_replace(out=scr[:], in_to_replace=topv[:, h, 0:8], in_values=cand[:], imm_value=NEG),
                         reads=[b_cand, b_topv], writes=[b_scr])
                    s.op("dve", lambda: nc.vector.max(out=topv[:, h, 8:16], in_=scr[:]), reads=[b_scr], writes=[b_topv])
                    s.op("dve", lambda: nc.vector.max_index(out=pos[:, h, 8:16], in_max=topv[:, h, 8:16], in_values=scr[:]),
                         reads=[b_scr, b_topv], writes=[b_pos])
                posf = pos[:].rearrange("p h k -> p (h k)")
                s.op("dve", lambda: nc.vector.tensor_scalar(pa[:], posf, 4, None, ALU.logical_shift_right), reads=[b_pos], writes=[b_pa])
                s.op("dve", lambda: nc.vector.tensor_copy(out=paf[:, 0, :], in_=pa[:]), reads=[b_pa], writes=[b_paf])
                s.op("dve", lambda: nc.vector.tensor_scalar(pa[:], posf, 15, None, ALU.bitwise_and), reads=[b_pos], writes=[b_pa])
                s.op("dve", lambda: nc.vector.tensor_copy(out=paf[:, 1, :], in_=pa[:]), reads=[b_pa], writes=[b_paf])
                sif4 = sif[:].rearrange("p (h c) a -> p h c a", c=2)
                for ab in range(2):
                    s.op("dve", lambda: nc.vector.tensor_tensor(out=oh[:], in0=paf[:, ab, :, None].to_broadcast([128, 128, 16]),
                                                                in1=iota[:, None, :].to_broadcast([128, 128, 16]), op=ALU.is_equal),
                         reads=[b_paf, b_c], writes=[b_oh])
                    s.op("dve", lambda: nc.vector.tensor_tensor(out=oh[:].rearrange("p (h k) a -> p h k a", k=16),
                                                                in0=oh[:].rearrange("p (h k) a -> p h k a", k=16),
                                                                in1=sif4[:, :, ab, None, :].to_broadcast([128, 8, 16, 16]), op=ALU.mult),
                         reads=[b_sif], writes=[b_oh])
                    s.op("dve", lambda: nc.vector.tensor_reduce(out=iab[:, ab, :], in_=oh[:], axis=AX.X, op=ALU.add), reads=[b_oh], writes=[b_iab])
                s.op("dve", lambda: nc.vector.scalar_tensor_tensor(out=iab[:, 0, :], in0=iab[:, 0, :], scalar=128.0, in1=iab[:, 1, :],
                                                                   op0=ALU.mult, op1=ALU.add), reads=[], writes=[b_iab])
                s.op("dve", lambda: nc.vector.tensor_copy(out=idx[:], in_=iab[:, 0, :]), reads=[b_iab], writes=[b_idx])
                s.op("dve", lambda: nc.vector.tensor_tensor(out=wgt[:], in0=topv[:], in1=topv[:, :, 0:1].to_broadcast([128, 8, 16]), op=ALU.subtract),
                     reads=[b_topv], writes=[b_wgt])
                s.op("act", lambda: nc.scalar.activation(out=wgt[:], in_=wgt[:], func=AF.Exp), reads=[], writes=[b_wgt])
                s.op("dve", lambda: nc.vector.tensor_reduce(out=rs[:], in_=wgt[:], axis=AX.X, op=ALU.add), reads=[b_wgt], writes=[b_rs])
                s.op("dve", lambda: nc.vector.reciprocal(rs[:], rs[:]), reads=[], writes=[b_rs])
                s.op("dve", lambda: nc.vector.tensor_tensor(out=wgt[:], in0=wgt[:], in1=rs[:, :, None].to_broadcast([128, 8, 16]), op=ALU.mult),
                     reads=[b_rs], writes=[b_wgt])
                for e in range(128):
                    gi = git % NGB; git += 1
                    s.gather(gbuf[gi][:], k.peer_u[l], idx[:, e:e + 1], reads=[b_idx], writes=[b_gbuf[gi]])
                    s.op("dve", lambda: nc.vector.scalar_tensor_tensor(out=tmp[0][:], in0=gbuf[gi][:], scalar=1.0, in1=h2t[ti][:],
                                                                       op0=ALU.mult, op1=ALU.mult, accum_out=av[:, e:e + 1]),
                         reads=[b_gbuf[gi], b_h2t[ti]], writes=([tmp[1], b_av] if e in (0, 127) else []))
                s.op("act", lambda: nc.scalar.activation(out=coef[:], in_=av[:], func=AF.Gelu), reads=[b_av], writes=[b_coef])
                s.op("dve", lambda: nc.vector.tensor_tensor(out=coef[:], in0=coef[:], in1=wgt[:].rearrange("p h k -> p (h k)"), op=ALU.mult),
                     reads=[b_wgt], writes=[b_coef])
                for e in range(128):
                    gi = git % NGB; git += 1
                    s.gather(gbuf[gi][:], k.peer_v[l], idx[:, e:e + 1], reads=[b_idx], writes=[b_gbuf[gi]])
                    if e == 0:
                        s.op("act", lambda: nc.scalar.activation(out=acc[:], in_=gbuf[gi][:], func=AF.Copy, scale=coef[:, 0:1]),
                             reads=[b_gbuf[gi], b_coef], writes=[b_acc])
                    else:
                        s.op("dve", lambda: nc.vector.scalar_tensor_tensor(out=acc[:], in0=gbuf[gi][:], scalar=coef[:, e:e + 1], in1=acc[:],
                                                                           op0=ALU.mult, op1=ALU.add),
                             reads=[b_gbuf[gi], b_coef], writes=[b_acc])
                xi = ti % 2
                s.dma("sp", xt[xi][:], k.XR[r0:r0 + 128, :], writes=[b_xt[xi]])
                s.op("dve", lambda: nc.vector.tensor_tensor(out=acc[:], in0=acc[:], in1=gate2[:], op=ALU.mult), reads=[b_gate2], writes=[b_acc])
                s.op("pool", lambda: nc.gpsimd.tensor_tensor(out=xt[xi][:], in0=xt[xi][:], in1=acc[:], op=ALU.add), reads=[b_acc], writes=[b_xt[xi]])
                if not last:
                    s.dma("sp", k.XR[r0:r0 + 128, :], xt[xi][:], reads=[b_xt[xi]])
                else:
                    junk, b_junk, ssq, b_ssq, rstd, b_rstd = tmp[0], tmp[1], tmp[2], tmp[3], tmp[4], tmp[5]
                    s.op("act", lambda: nc.scalar.activation(out=junk[:], in_=xt[xi][:], func=AF.Square, accum_out=ssq[:]),
                         reads=[b_xt[xi]], writes=[b_junk, b_ssq])
                    s.op("dve", lambda: nc.vector.tensor_scalar(rstd[:], ssq[:], 1.0 / D, EPS, ALU.mult, ALU.add), reads=[b_ssq], writes=[b_rstd])
                    s.op("act", lambda: nc.scalar.activation(out=rstd[:], in_=rstd[:], func=AF.Sqrt), reads=[], writes=[b_rstd])
                    s.op("dve", lambda: nc.vector.reciprocal(rstd[:], rstd[:]), reads=[], writes=[b_rstd])
                    s.op("dve", lambda: nc.vector.scalar_tensor_tensor(out=xt[xi][:], in0=xt[xi][:], scalar=rstd[:, 0:1], in1=g2r[:],
                                                                       op0=ALU.mult, op1=ALU.mult), reads=[b_rstd, b_g2r], writes=[b_xt[xi]])
                    s.dma("sp", k.Y[r0:r0 + 128, :], xt[xi][:], reads=[b_xt[xi]])
            s.barrier()
        s.barrier()


def host_shared(inp, L):
    hw = host_weights(inp, L)
    f = np.float32
    c = lambda a: np.ascontiguousarray(a, dtype=f)
    scw = inp["ssd_conv_w"][:L]
    hw["ssd_cw"] = c(scw.reshape(L, 2, 4, 12, 128).transpose(0, 1, 4, 2, 3).reshape(L, 2, 128, 48))
    hw["ssd_cb"] = c(inp["ssd_conv_b"][:L].reshape(L, 2, 12, 128).transpose(0, 1, 3, 2))
    hw["ssd_A_log"] = c(inp["ssd_A_log"][:L]); hw["ssd_dt_bias"] = c(inp["ssd_dt_bias"][:L]); hw["ssd_D"] = c(inp["ssd_D"][:L])
    hw["ssd_norm_g"] = c(inp["ssd_norm_g"][:L])
    hw["sink"] = c(inp["attn_sink"][:L])
    hw["conf_w"] = c(inp["conv_dw_w"][:L].reshape(L, 31, 8, 128).transpose(0, 3, 1, 2).reshape(L, 128, 248))
    hw["conf_b"] = c(inp["conv_dw_b"][:L].reshape(L, 8, 128).transpose(0, 2, 1))
    hw["conf_ln_g"] = c(inp["conv_ln_g"][:L]); hw["conf_ln_b"] = c(inp["conv_ln_b"][:L])
    hw["gate_b"] = c(inp["gate_b"][:L].reshape(L, 6144))
    for nm in ("w_attn_o", "w_conv_o", "w_ssd_o", "w_out"):
        hw[nm] = c(inp[nm][:L])
    hw["w_q"] = c(inp["peer_w_q"][:L])
    hw["norm2_g"] = c(inp["norm2_g"][:L]); hw["final_g"] = c(inp["final_g"])
    hw["skT"] = c(inp["peer_sub_keys"][:L].reshape(L, 16, 128, 128).transpose(0, 1, 3, 2))
    for i in range(L):
        hw[f"peer_u{i}"] = c(inp["peer_u"][i]); hw[f"peer_v{i}"] = c(inp["peer_v"][i])
    ii = np.arange(128)
    tri = np.zeros((3, 128, 128), f)
    tri[0] = (ii[:, None] <= ii[None, :]); tri[1] = (ii[:, None] >= ii[None, :]); tri[2] = 1.0
    hw["tri"] = tri
    hw["iota16"] = c(np.tile(np.arange(16, dtype=f)[None, :], (128, 1)))
    return hw


def host_core(sample, T, L, x, cond, ck=None, cv=None, h0=None):
    f = np.float32
    cos, sin = rope_tables(T, sample)
    ii = np.arange(128)
    masks = np.zeros((4, 128, 128), f)
    if sample:
        masks[0] = masks[1] = (ii[None, :] <= ii[:, None])
        masks[2] = masks[3] = (ii[:, None] <= ii[None, :])
    else:
        masks[1] = 1.0
        masks[2] = 1.0
    m = dict(x0=np.ascontiguousarray(x, dtype=f), cond=np.ascontiguousarray(cond.reshape(16, 128).T, dtype=f), cos=cos, sin=sin,
             masks=masks, f0=np.full((128, 1), 1.0 if sample else 0.0, f),
             ctxbias=np.full((128, 1), 0.0 if sample else -30000.0, f))
    if sample:
        m["ckT"] = np.ascontiguousarray(ck[:L].transpose(0, 3, 2, 1), dtype=f)
        m["cv"] = np.ascontiguousarray(cv[:L].reshape(L, 256, 256), dtype=f)
        m["h0"] = np.ascontiguousarray(h0[:L].reshape(L, 2, 1024, 128), dtype=f)
    else:
        m["ckT"] = np.zeros((L, 64, 4, 256), f); m["cv"] = np.zeros((L, 256, 256), f); m["h0"] = np.zeros((L, 2, 1024, 128), f)
    return m


T_CORE, N_LAYERS = 4096, 4


def kernel(**inp):
    inp = {k_: np.asarray(v) for k_, v in inp.items()}
    T, L = T_CORE, N_LAYERS
    nc = build(T, L)
    hw = host_shared(inp, L)
    in_maps = []
    for b in range(4):
        m = dict(hw)
        m.update(host_core(True, T, L, inp["x_sample"][b], inp["c"][b], inp["cache_k"][b], inp["cache_v"][b], inp["state_ssd"][b]))
        in_maps.append(m)
    for i in range(4):
        xp = inp["x_prompt"][8 * i:8 * i + 8].reshape(2048, 2048)
        m = dict(hw)
        m.update(host_core(False, T, L, np.concatenate([xp, xp], 0), inp["c_ctx"]))
        in_maps.append(m)
    res = run_bass_kernel_spmd(nc, in_maps, core_ids=list(range(8)))
    r = res.results
    y_sample = np.stack([r[b]["Y"] for b in range(4)], 0).astype(np.float32)
    y_prompt = np.concatenate([r[4 + i]["Y"][:2048].reshape(8, 256, 2048) for i in range(4)], 0).astype(np.float32)
    nk = np.zeros((32, L, 256, 4, 64), np.float32)
    nv = np.zeros((32, L, 256, 4, 64), np.float32)
    ns = np.zeros((32, L, 2, 16, 64, 128), np.float32)
    for i in range(4):
        kvo = r[4 + i]["KVO"]
        sso = r[4 + i]["SSDO"]
        for sq in range(8):
            nk[8 * i + sq] = kvo[:, sq * 256:(sq + 1) * 256, 0:256].reshape(L, 256, 4, 64)
            nv[8 * i + sq] = kvo[:, sq * 256:(sq + 1) * 256, 256:512].reshape(L, 256, 4, 64)
            ns[8 * i + sq] = sso[:, :, sq].reshape(L, 2, 16, 64, 128)
    return (y_prompt, y_sample, nk, nv, ns)
```
